# Optimizing a Trainium2 kernel written in Bass

```python
import jax, jax.numpy as jnp
from jax import lax
import numpy as np


D_MODEL = 2048
BATCH = 1
SEQ = 16384
DEPTH = 1

N_MEM = 256
EPS = 1e-6
RET_HEADS = 4
RET_DK = 256
RET_DV = 256
RET_CHUNK = 128
RET_THETA = 10000.0
SWA_HEADS = 16
SWA_KV_HEADS = 4
SWA_HD = 64
WINDOW = 128
ROPE_THETA = 500000.0
ROPE_DIM = SWA_HD // 4
X_HEADS = 4
X_HD = 128
D_FF = -(-8 * D_MODEL // (3 * 256)) * 256
IN_SPLITS = (RET_HEADS * RET_DK, RET_HEADS * RET_DK, RET_HEADS * RET_DV, RET_HEADS * RET_DV,
             SWA_HEADS * SWA_HD, SWA_KV_HEADS * SWA_HD, SWA_KV_HEADS * SWA_HD,
             D_MODEL, D_MODEL)
D_IN = sum(IN_SPLITS)
NEG = -1e30

kernel_name = 'hybrid_retention_swa_block'


def rmsnorm(x, g=None):
    xf = x.astype(jnp.float32)
    y = xf * lax.rsqrt(jnp.mean(xf * xf, axis=-1, keepdims=True) + EPS)
    if g is not None:
        y = y * g.astype(jnp.float32)
    return y.astype(x.dtype)


def to_heads(t, n_heads):
    b, s, _ = t.shape
    return t.reshape(b, s, n_heads, -1).transpose(0, 2, 1, 3)


def rope(x, pos, theta, rot_dim):
    half = rot_dim // 2
    inv = 1.0 / (theta ** (jnp.arange(half, dtype=jnp.float32) / half))
    ang = pos.astype(jnp.float32)[:, None, :, None] * inv
    cos, sin = jnp.cos(ang), jnp.sin(ang)
    xf = x[..., :rot_dim].astype(jnp.float32)
    x1, x2 = xf[..., :half], xf[..., half:]
    rot = jnp.concatenate([x1 * cos - x2 * sin, x2 * cos + x1 * sin], axis=-1)
    return jnp.concatenate([rot.astype(x.dtype), x[..., rot_dim:]], axis=-1)


def retention(q, k, v):
    b, h, s, dk = q.shape
    dv = v.shape[-1]
    c = RET_CHUNK
    n = s // c
    log_g = jnp.log(1.0 - jnp.power(2.0, -5.0 - jnp.arange(h, dtype=jnp.float32)))
    qf = q.astype(jnp.float32).reshape(b, h, n, c, dk)
    kf = (k.astype(jnp.float32) * (dk ** -0.5)).reshape(b, h, n, c, dk)
    vf = v.astype(jnp.float32).reshape(b, h, n, c, dv)
    idx = jnp.arange(c, dtype=jnp.float32)
    rel = idx[:, None] - idx[None, :]
    dmask = jnp.where(rel >= 0, jnp.exp(log_g[:, None, None] * jnp.maximum(rel, 0.0)), 0.0)
    scores = jnp.einsum('bhnid,bhnjd->bhnij', qf, kf) * dmask[:, None]
    inner = jnp.einsum('bhnij,bhnje->bhnie', scores, vf)
    q_dec = qf * jnp.exp(log_g[:, None] * (idx + 1.0))[:, None, :, None]
    k_dec = kf * jnp.exp(log_g[:, None] * (c - 1.0 - idx))[:, None, :, None]
    chunk_decay = jnp.exp(log_g * c)[None, :, None, None]

    def step(state, inp):
        qn, kn, vn = inp
        cross = jnp.einsum('bhid,bhde->bhie', qn, state)
        state = state * chunk_decay + jnp.einsum('bhjd,bhje->bhde', kn, vn)
        return state, cross

    xs = (jnp.moveaxis(q_dec, 2, 0), jnp.moveaxis(k_dec, 2, 0), jnp.moveaxis(vf, 2, 0))
    _, cross = lax.scan(step, jnp.zeros((b, h, dk, dv), jnp.float32), xs)
    out = inner + jnp.moveaxis(cross, 0, 2)
    return out.reshape(b, h, s, dv)


def sliding_window_attention(q, k, v, sinks):
    b, hq, s, hd = q.shape
    hkv = k.shape[1]
    g = hq // hkv
    w = WINDOW
    n = s // w
    qb = q.reshape(b, hkv, g, n, w, hd)

    def band(t):
        tb = t.reshape(b, hkv, n, w, hd)
        prev = jnp.pad(tb[:, :, :-1], ((0, 0), (0, 0), (1, 0), (0, 0), (0, 0)))
        return jnp.concatenate([prev, tb], axis=3)

    kb, vb = band(k), band(v)
    scores = jnp.einsum('bkgnqd,bknjd->bkgnqj', qb, kb).astype(jnp.float32) * (hd ** -0.5)
    qi = jnp.arange(w)[:, None]
    kj = jnp.arange(2 * w)[None, :]
    dist = qi + w - kj
    valid = (dist >= 0) & (dist < WINDOW)
    has_prev = (jnp.arange(n)[:, None, None] > 0) | (kj[None] >= w)
    mask = valid[None] & has_prev
    scores = jnp.where(mask, scores, NEG)
    sink = jnp.broadcast_to(sinks.astype(jnp.float32).reshape(1, hkv, g, 1, 1, 1), scores.shape[:-1] + (1,))
    probs = jax.nn.softmax(jnp.concatenate([scores, sink], axis=-1), axis=-1)[..., :-1]
    out = jnp.einsum('bkgnqj,bknjd->bkgnqd', probs.astype(v.dtype), vb)
    return out.reshape(b, hq, s, hd)


def memory_cross_attention(hn, memn, w_xq, w_xkv, w_xo):
    b, s, _ = hn.shape
    q = (hn @ w_xq).reshape(b, s, X_HEADS, X_HD)
    kv = (memn @ w_xkv).reshape(b, memn.shape[1], 2, X_HEADS, X_HD)
    k, v = kv[:, :, 0], kv[:, :, 1]
    scores = jnp.einsum('bshd,bmhd->bhsm', q, k).astype(jnp.float32) * (X_HD ** -0.5)
    probs = jax.nn.softmax(scores, axis=-1).astype(v.dtype)
    out = jnp.einsum('bhsm,bmhd->bshd', probs, v).reshape(b, s, X_HEADS * X_HD)
    return out @ w_xo


def setup_inputs(seed: int = 0) -> dict:
    key = jax.random.key(seed)
    ks = jax.random.split(key, 20)
    f32 = jnp.float32
    L = DEPTH

    def wt(k, shape, fan_in):
        return jax.random.normal(k, shape, f32) * (fan_in ** -0.5)

    def gain(k, shape):
        return 1.0 + 0.02 * jax.random.normal(k, shape, f32)

    return {
        'x': jax.random.normal(ks[0], (BATCH, SEQ, D_MODEL), f32),
        'mem': jax.random.normal(ks[1], (BATCH, N_MEM, D_MODEL), f32),
        'positions': jnp.broadcast_to(jnp.arange(SEQ, dtype=jnp.int32)[None, :], (BATCH, SEQ)),
        'g_mix': gain(ks[2], (L, D_MODEL)),
        'w_in': wt(ks[3], (L, D_MODEL, D_IN), D_MODEL),
        'w_up_ret': wt(ks[4], (L, RET_HEADS * RET_DV, D_MODEL), RET_HEADS * RET_DV),
        'w_up_swa': wt(ks[5], (L, SWA_HEADS * SWA_HD, D_MODEL), SWA_HEADS * SWA_HD),
        'sinks': 0.5 * jax.random.normal(ks[6], (L, SWA_HEADS), f32),
        'w_o': wt(ks[7], (L, D_MODEL, D_MODEL), D_MODEL),
        'g_x': gain(ks[8], (L, D_MODEL)),
        'g_mem': gain(ks[9], (L, D_MODEL)),
        'w_xq': wt(ks[10], (L, D_MODEL, X_HEADS * X_HD), D_MODEL),
        'w_xkv': wt(ks[11], (L, D_MODEL, 2 * X_HEADS * X_HD), D_MODEL),
        'w_xo': wt(ks[12], (L, X_HEADS * X_HD, D_MODEL), X_HEADS * X_HD),
        'g_ffn': gain(ks[13], (L, D_MODEL)),
        'w_ffn_gate': wt(ks[14], (L, D_MODEL, D_FF), D_MODEL),
        'w_ffn_up': wt(ks[15], (L, D_MODEL, D_FF), D_MODEL),
        'w_ffn_down': wt(ks[16], (L, D_FF, D_MODEL), D_FF),
        'g_final': gain(ks[17], (D_MODEL,)),
    }


def reference(x, mem, positions, g_mix, w_in, w_up_ret, w_up_swa, sinks, w_o, g_x, g_mem,
              w_xq, w_xkv, w_xo, g_ffn, w_ffn_gate, w_ffn_up, w_ffn_down, g_final):
    h = x
    offsets = [int(o) for o in np.cumsum(IN_SPLITS)[:-1]]
    for l in range(DEPTH):
        n1 = rmsnorm(h, g_mix[l])
        proj = n1 @ w_in[l]
        q_r, k_r, v_r, g_r, q_s, k_s, v_s, gate_r, gate_s = jnp.split(proj, offsets, axis=-1)
        qr = rope(to_heads(q_r, RET_HEADS), positions, RET_THETA, RET_DK)
        kr = rope(to_heads(k_r, RET_HEADS), positions, RET_THETA, RET_DK)
        yr = retention(qr, kr, to_heads(v_r, RET_HEADS))
        yr = rmsnorm(yr.transpose(0, 2, 1, 3))
        b, s = yr.shape[0], yr.shape[1]
        yr = (yr.reshape(b, s, RET_HEADS * RET_DV) * jax.nn.silu(g_r.astype(jnp.float32))).astype(h.dtype)
        qs = rope(to_heads(q_s, SWA_HEADS), positions, ROPE_THETA, ROPE_DIM)
        ksw = rope(to_heads(k_s, SWA_KV_HEADS), positions, ROPE_THETA, ROPE_DIM)
        ys = sliding_window_attention(qs, ksw, to_heads(v_s, SWA_KV_HEADS), sinks[l])
        ys = ys.transpose(0, 2, 1, 3).reshape(b, s, SWA_HEADS * SWA_HD)
        merged = jax.nn.sigmoid(gate_r) * (yr @ w_up_ret[l]) + jax.nn.sigmoid(gate_s) * (ys @ w_up_swa[l])
        h = h + merged @ w_o[l]
        h = h + memory_cross_attention(rmsnorm(h, g_x[l]), rmsnorm(mem, g_mem[l]), w_xq[l], w_xkv[l], w_xo[l])
        n2 = rmsnorm(h, g_ffn[l])
        h = h + (jax.nn.silu(n2 @ w_ffn_gate[l]) * (n2 @ w_ffn_up[l])) @ w_ffn_down[l]
    return rmsnorm(h, g_final)
```

```python
import math
from contextlib import ExitStack

import numpy as np
import concourse.bass as bass
import concourse.mybir as mybir
from concourse.bass_utils import run_bass_kernel_spmd

F32 = mybir.dt.float32
BF16 = mybir.dt.bfloat16
I32 = mybir.dt.int32
AF = mybir.ActivationFunctionType
ALU = mybir.AluOpType
AX = mybir.AxisListType

D = 2048
KC = 16
SEQ = 16384
NCORE = 8
TOK = SEQ // NCORE
TT = 512
CH = 4
NTILE = TOK // TT
NPRE = 8
DFF = 5632
EPS = 1e-6
NSLOT = 3
USE_SCRATCH = True
TWO_PI = 2.0 * math.pi

O_QR, O_KR, O_VR, O_GR, O_QS, O_KS, O_VS, O_GTR, O_GTS = 0, 1024, 2048, 3072, 4096, 5120, 5376, 5632, 7680


class Res:
    __slots__ = ("name", "w", "r")

    def __init__(self, name, r=None):
        self.name = name
        self.w = None
        self.r = list(r) if r else []


class DmaSem:
    def __init__(self, idx):
        self.idx = idx
        self.val = 0


class _Rec:
    def __init__(self):
        self.calls = []

    def __getattr__(self, name):
        def m(*a, **k):
            self.calls.append((name, a, k))
            return self
        return m


class Prog:
    ENGS = ("pe", "act", "dve", "pool", "sp")

    def __init__(self, nc, es):
        self.nc = nc
        self.es = es
        self.sems = []
        self.q = {e: [] for e in self.ENGS}
        self.cnt = {e: 0 for e in self.ENGS}
        self.waited = {e: {} for e in self.ENGS}
        self.esem = {}
        for e in self.ENGS:
            self.esem[e] = self.new_sem("s_" + e)
        self.n_ops = 0

    def new_sem(self, name):
        h = self.es.enter_context(self.nc.semaphore(name))
        self.sems.append(h)
        return len(self.sems) - 1

    def dma_sem(self, name):
        return DmaSem(self.new_sem(name))

    def op(self, eng, fn, reads=(), writes=(), dsem=None):
        waits = {}
        wd = self.waited[eng]
        own = self.esem[eng]

        def need(ev):
            if ev is None:
                return
            s, v = ev
            if s == own and eng == "pe":
                return
            if wd.get(s, 0) >= v:
                return
            if waits.get(s, 0) < v:
                waits[s] = v

        for r in reads:
            need(r.w)
            if r.name.startswith("ps") and eng != "pe":
                for ev in r.r:
                    if ev[0] != own:
                        need(ev)
        for w in writes:
            need(w.w)
            for ev in w.r:
                need(ev)
        if dsem is not None and dsem.val > 0:
            need((dsem.idx, dsem.val))
        for s, v in waits.items():
            wd[s] = v
        if dsem is None:
            self.cnt[eng] += 1
            ev = (self.esem[eng], self.cnt[eng])
            inc = (self.esem[eng], 1)
        else:
            dsem.val += 16
            ev = (dsem.idx, dsem.val)
            inc = (dsem.idx, 16)
        rec = _Rec()
        fn(rec)
        self.q[eng].append((list(waits.items()), rec.calls, inc))
        for r in reads:
            r.r.append(ev)
            if len(r.r) > 24:
                m = {}
                for s, v in r.r:
                    if m.get(s, 0) < v:
                        m[s] = v
                r.r = list(m.items())
        for w in writes:
            w.w = ev
            w.r = []
        self.n_ops += 1
        return ev

    def replay(self, eng, e):
        for waits, calls, inc in self.q[eng]:
            for s, v in waits:
                e.wait_ge(self.sems[s], v)
            inst = None
            for name, a, k in calls:
                inst = getattr(e, name)(*a, **k)
            inst.then_inc(self.sems[inc[0]], inc[1])

    def run(self, final_events):
        nc = self.nc
        with nc.Block() as block:
            @block.tensor
            def _(e):
                self.replay("pe", e)

            @block.scalar
            def _(e):
                self.replay("act", e)

            @block.vector
            def _(e):
                self.replay("dve", e)

            @block.gpsimd
            def _(e):
                self.replay("pool", e)

            @block.sync
            def _(e):
                self.replay("sp", e)
                for s, v in final_events:
                    e.wait_ge(self.sems[s], v)


def build(n_main_tiles=NTILE, n_pre=NPRE, dbg=None, stop_after=None, ret_level=9):
    nc = bass.Bass("TRN2", target_bir_lowering=False)

    def din(n, s, d=F32):
        return nc.dram_tensor(n, s, d, kind="ExternalInput").ap()

    x_own = din("x_own", [TOK, D])
    x_halo = din("x_halo", [128, D])
    x_prev = din("x_prev", [NPRE * TT, D])
    pos_in = din("pos", [128, 17 + NPRE * CH], I32)
    mem_in = din("mem", [256, D])
    w_in = din("w_in", [D, 9728])
    w_up_ret = din("w_up_ret", [1024, D])
    w_up_swa = din("w_up_swa", [1024, D])
    w_o = din("w_o", [D, D])
    w_xq = din("w_xq", [D, 512])
    w_xkv = din("w_xkv", [D, 1024])
    w_xo = din("w_xo", [512, D])
    w_fg = din("w_fg", [D, DFF])
    w_fu = din("w_fu", [D, DFF])
    w_fd = din("w_fd", [DFF, D])
    gcols_in = din("gcols", [128, 4 * 16])
    gfin_in = din("gfin", [128, D])
    sinkb_in = din("sinkb", [128, 16])
    invr_in = din("invr", [128, 128])
    invs_in = din("invs", [128, 8])
    maskb_in = din("maskb", [128, 2 * 256])
    dmT_in = din("dmT", [128, 4 * 128])
    qdecT_in = din("qdecT", [128, 4 * 128])
    kdec_in = din("kdec", [128, 4])
    y_out = nc.dram_tensor("y", [TOK, D], F32, kind="ExternalOutput").ap()
    dbg_outs = {}

    g128 = [float(np.exp(np.log(1.0 - 2.0 ** (-5.0 - h)) * 128.0)) for h in range(4)]

    es = ExitStack()
    with es:
        P = Prog(nc, es)

        def sb(n, s, d):
            return es.enter_context(nc.sbuf_tensor("sb_" + n, s, d))

        h_t = sb("h", [128, CH, D], F32)
        xnb = sb("xnb", [128, D], BF16)
        nT = sb("nT", [128, KC, TT], BF16)
        wr = [sb(f"wr{i}", [128, 4096], F32) for i in range(NSLOT)]
        ysT = sb("ysT", [128, 8, TT], BF16)
        yrT = sb("yrT", [128, 8, TT], BF16)
        S_t = sb("S", [128, 8, 256], F32)
        Sb_t = sb("Sb", [128, 8, 256], BF16)
        ksT = sb("ksT", [128, 4, 640], BF16)
        vs_t = sb("vs", [128, 5, 256], BF16)
        maskb = sb("maskb", [128, 2, 256], F32)
        dmT = sb("dmT", [128, 4, 128], F32)
        qdecT = sb("qdecT", [128, 4, 128], F32)
        kdec = sb("kdec", [128, 4], F32)
        invr = sb("invr", [128, 128], F32)
        invs = sb("invs", [128, 8], F32)
        sinkb = sb("sinkb", [128, 16], F32)
        gcols = sb("gcols", [128, 4, 16], F32)
        ident = sb("ident", [128, 128], BF16)
        identf = sb("identf", [128, 128], F32)
        kmT = sb("kmT", [128, 4, 256], BF16)
        vm = sb("vm", [128, 2, 512], BF16)
        posi = sb("posi", [128, 17 + NPRE * CH], I32)
        posf = sb("posf", [128, 17 + NPRE * CH], F32)
        coss = sb("coss", [128, 17, 8], F32)
        sins = sb("sins", [128, 17, 8], F32)
        st = sb("st", [128, 160], F32)
        cst = sb("cst", [128, 4], F32)
        ARENA_F32 = 13440
        arena = sb("arena", [128, ARENA_F32], F32)
        ps = [es.enter_context(nc.psum_tensor(f"ps{i}", [128, 512], F32)) for i in range(8)]

        r_h = [Res(f"h{c}") for c in range(CH)]
        r_xnb = Res("xnb")
        r_nT = [Res(f"nT{c}") for c in range(CH)]
        r_wr = [Res(f"wr{i}") for i in range(NSLOT)]
        s_wr = [P.dma_sem(f"s_wr{i}") for i in range(NSLOT)]
        r_ps = [Res(f"ps{i}") for i in range(8)]
        r_ysT = [Res(f"ysT{c}") for c in range(CH)]
        r_yrT = [Res(f"yrT{c}") for c in range(CH)]
        r_S = [Res(f"S{h}") for h in range(4)]
        r_Sb = [Res(f"Sb{h}") for h in range(4)]
        r_ks = [Res(f"ks{s}") for s in range(5)]
        r_vs = [Res(f"vs{s}") for s in range(5)]
        r_const = Res("const")
        r_id = Res("ident")
        r_km = Res("kmT")
        r_vm = Res("vm")
        r_pos = Res("pos")
        r_st = {}
        s_ld = P.dma_sem("s_ld")
        s_x = [P.dma_sem(f"s_x{c}") for c in range(CH)]
        s_y = [P.dma_sem(f"s_y{c}") for c in range(CH)]
        s_misc = P.dma_sem("s_misc")

        bank_ctr = [0]

        def bank():
            b = bank_ctr[0] % 8
            bank_ctr[0] += 1
            return b

        arena_res = []
        fence = [[]]

        def AR(name):
            r = Res(name, fence[0])
            arena_res.append(r)
            return r

        def new_phase():
            m = {}
            for s, v in fence[0]:
                m[s] = max(m.get(s, 0), v)
            for r in arena_res:
                evs = list(r.r)
                if r.w is not None:
                    evs.append(r.w)
                for s, v in evs:
                    if m.get(s, 0) < v:
                        m[s] = v
            fence[0] = list(m.items())
            arena_res.clear()

        def carve(off_bytes, shape, dt):
            n = 1
            for s in shape[1:]:
                n *= s
            if dt == F32:
                a = arena[:, off_bytes // 4: off_bytes // 4 + n]
            else:
                nf = (n + 1) // 2
                a = arena[:, off_bytes // 4: off_bytes // 4 + nf].bitcast(BF16)[:, 0:n]
            if len(shape) == 2:
                return a
            if len(shape) == 3:
                return a.rearrange("p (a b) -> p a b", a=shape[1])
            if len(shape) == 4:
                return a.rearrange("p (a b c) -> p a b c", a=shape[1], b=shape[2])
            raise ValueError

        def stat(name, n):
            if name not in r_st:
                off = sum(v[1] for v in r_st.values())
                assert off + n <= 160
                r_st[name] = (off, n, Res("st_" + name))
            off, n0, r = r_st[name]
            return st[:, off:off + n0], r

        def wview(slot, kcs, cols):
            return wr[slot][:].bitcast(BF16)[:, 0:kcs * cols].rearrange("p (k c) -> p k c", k=kcs)

        def ld(dst_ap, src_ap, res, q="sp"):
            P.op(q, lambda e: e.dma_start(out=dst_ap, in_=src_ap), writes=[res], dsem=s_ld)

        ld(maskb[:].rearrange("p a b -> p (a b)"), maskb_in, r_const)
        ld(dmT[:].rearrange("p a b -> p (a b)"), dmT_in, r_const)
        ld(qdecT[:].rearrange("p a b -> p (a b)"), qdecT_in, r_const)
        ld(kdec[:], kdec_in, r_const)
        ld(invr[:], invr_in, r_const)
        ld(invs[:], invs_in, r_const)
        ld(sinkb[:], sinkb_in, r_const)
        ld(gcols[:].rearrange("p a b -> p (a b)"), gcols_in, r_const)
        ld(posi[:], pos_in, r_pos)
        P.op("pool", lambda e: e.memset(identf[:], 1.0), writes=[r_id])
        P.op("pool", lambda e: e.affine_select(out=identf[:], in_=identf[:], pattern=[[-1, 128]],
                                               compare_op=ALU.is_equal, fill=0.0, base=0, channel_multiplier=1),
             reads=[r_id], writes=[r_id])
        P.op("dve", lambda e: e.tensor_copy(out=ident[:], in_=identf[:]), reads=[r_id], writes=[r_id])
        P.op("dve", lambda e: e.memset(cst[:, 0:1], math.pi), writes=[r_const])
        P.op("dve", lambda e: e.memset(cst[:, 1:2], EPS), writes=[r_const])
        P.op("dve", lambda e: e.tensor_copy(out=posf[:], in_=posi[:]), reads=[r_pos], writes=[r_pos])

        tg_ang = sb("tg_ang", [128, 128], F32)
        tg_r = sb("tg_r", [128, 128], F32)
        tg_kf = sb("tg_kf", [128, 128], F32)
        tg_ki = sb("tg_ki", [128, 128], I32)
        r_tg = Res("tg")
        CW1 = 6.28125
        CW2 = TWO_PI - 6.28125

        def sincos(dst_sin, dst_cos, inv_ap, pcol, n, r_dst, tmp=None, r_tmp=None):
            ang, rr, kf, ki = tg_ang[:, 0:n], tg_r[:, 0:n], tg_kf[:, 0:n], tg_ki[:, 0:n]
            P.op("dve", lambda e: e.tensor_scalar(out=ang, in0=inv_ap, scalar1=posf[:, pcol:pcol + 1], scalar2=None,
                                                  op0=ALU.mult), reads=[r_pos, r_const], writes=[r_tg])
            for which, dst in ((0, dst_sin), (1, dst_cos)):
                if which == 1:
                    P.op("dve", lambda e: e.tensor_scalar(out=ang, in0=ang, scalar1=0.5 * math.pi, scalar2=None, op0=ALU.add),
                         reads=[r_tg], writes=[r_tg])
                P.op("dve", lambda e: e.tensor_scalar(out=ki, in0=ang, scalar1=1.0 / TWO_PI, scalar2=None, op0=ALU.mult),
                     reads=[r_tg], writes=[r_tg])
                P.op("dve", lambda e: e.tensor_copy(out=kf, in_=ki), reads=[r_tg], writes=[r_tg])
                P.op("dve", lambda e: e.scalar_tensor_tensor(out=rr, in0=kf, scalar=-CW1, in1=ang, op0=ALU.mult, op1=ALU.add),
                     reads=[r_tg], writes=[r_tg])
                P.op("dve", lambda e: e.scalar_tensor_tensor(out=rr, in0=kf, scalar=-CW2, in1=rr, op0=ALU.mult, op1=ALU.add),
                     reads=[r_tg], writes=[r_tg])
                P.op("dve", lambda e: e.tensor_scalar(out=kf, in0=rr, scalar1=math.pi, scalar2=-TWO_PI, op0=ALU.is_gt, op1=ALU.mult),
                     reads=[r_tg], writes=[r_tg])
                P.op("dve", lambda e: e.tensor_tensor(out=rr, in0=rr, in1=kf, op=ALU.add), reads=[r_tg], writes=[r_tg])
                P.op("dve", lambda e: e.tensor_scalar(out=kf, in0=rr, scalar1=-math.pi, scalar2=TWO_PI, op0=ALU.is_lt, op1=ALU.mult),
                     reads=[r_tg], writes=[r_tg])
                P.op("dve", lambda e: e.tensor_tensor(out=rr, in0=rr, in1=kf, op=ALU.add), reads=[r_tg], writes=[r_tg])
                P.op("act", lambda e, dst=dst: e.activation(out=dst, in_=rr, func=AF.Sin), reads=[r_tg], writes=[r_dst, r_tg])

        r_cs = Res("coss")
        tmp8, r_tmp8 = stat("tmp8", 8)
        for ci in range(17):
            sincos(sins[:, ci, :], coss[:, ci, :], invs[:], ci, 8, r_cs)

        ss_ap, r_ss = stat("ss", 1)
        rstd_ap, r_rstd = stat("rstd", 1)

        def rstd_from(ss, r_s, out, r_o, n, inv_n):
            P.op("act", lambda e: e.activation(out=out, in_=ss, func=AF.Sqrt, bias=cst[:, 1:2], scale=inv_n),
                 reads=[r_s, r_const], writes=[r_o])
            P.op("dve", lambda e: e.reciprocal(out=out, in_=out), reads=[r_o], writes=[r_o])

        def norm_T(src, r_src, gi, dst, r_dst):
            P.op("act", lambda e: e.activation(out=xnb[:], in_=src, func=AF.Square, accum_out=ss_ap),
                 reads=[r_src], writes=[r_xnb, r_ss])
            rstd_from(ss_ap, r_ss, rstd_ap, r_rstd, 1, 1.0 / D)
            P.op("dve", lambda e: e.tensor_scalar(out=xnb[:], in0=src, scalar1=rstd_ap, scalar2=None, op0=ALU.mult),
                 reads=[r_src, r_rstd], writes=[r_xnb])
            for half in range(2):
                b = bank()
                pv = ps[b][:].bitcast(BF16)

                def tr(e, half=half, pv=pv):
                    inst = None
                    for j in range(8):
                        kc = half * 8 + j
                        inst = e.transpose(out=pv[:, j * 128:(j + 1) * 128], in_=xnb[:, kc * 128:(kc + 1) * 128], identity=ident[:])
                    return inst
                P.op("pe", tr, reads=[r_xnb, r_id], writes=[r_ps[b]])
                P.op("dve", lambda e, half=half, pv=pv: e.tensor_tensor(
                    out=dst[:, half * 8:(half + 1) * 8, :], in0=pv.rearrange("p (a b) -> p a b", a=8),
                    in1=gcols[:, gi, half * 8:(half + 1) * 8].unsqueeze(2).broadcast_to([128, 8, 128]), op=ALU.mult),
                    reads=[r_ps[b], r_const], writes=[r_dst])

        def mm_group(b, cols, pairs, reads):
            n = len(pairs)

            def f(e):
                inst = None
                for i, (l, r) in enumerate(pairs):
                    inst = e.matmul(ps[b][:, cols[0]:cols[1]], lhsT=l, rhs=r, start=(i == 0), stop=(i == n - 1))
                return inst
            P.op("pe", f, reads=reads, writes=[r_ps[b]])

        def transposes(b, srcs, reads):
            pv = ps[b][:].bitcast(BF16)

            def f(e):
                inst = None
                for j, s in enumerate(srcs):
                    inst = e.transpose(out=pv[:, j * 128:(j + 1) * 128], in_=s, identity=ident[:])
                return inst
            P.op("pe", f, reads=list(reads) + [r_id], writes=[r_ps[b]])
            return pv

        evac_ctr = [0]

        def copy_any(out, in_, reads, writes, eng=None):
            if eng is None:
                eng = "act" if evac_ctr[0] % 2 == 0 else "dve"
                evac_ctr[0] += 1
            if eng == "act":
                P.op("act", lambda e: e.activation(out=out, in_=in_, func=AF.Copy), reads=reads, writes=writes)
            else:
                P.op("dve", lambda e: e.tensor_copy(out=out, in_=in_), reads=reads, writes=writes)

        def load_x(src_rows, c):
            P.op("sp", lambda e: e.dma_start(out=h_t[:, c, :], in_=src_rows), writes=[r_h[c]], dsem=s_x[c])

        steps = []

        wnames = {}

        def wdesc(w, r0, kcs, c0, cols):
            v = w[r0:r0 + kcs * 128, c0:c0 + cols].rearrange("(k p) c -> p k c", p=128)
            wn = wnames.setdefault(id(w.tensor), f"w{len(wnames)}") if False else None
            return (v, kcs, cols, (w.tensor.name, r0, kcs, c0, cols))

        def step(descs, fn):
            steps.append((descs, fn))

        def mem_setup():
            def pre(slots):
                for mc in range(2):
                    P.op("sp", lambda e, mc=mc: e.dma_start(out=h_t[:, mc, :], in_=mem_in[mc * 128:(mc + 1) * 128, :]),
                         writes=[r_h[mc]], dsem=s_x[mc])
                    norm_T(h_t[:, mc, :], r_h[mc], 3, nT[:, :, mc * 128:(mc + 1) * 128], r_nT[mc])
            step([], pre)

            def kstep(slots):
                (wv, r_w), = slots
                for hd in range(4):
                    b = bank()
                    mm_group(b, (0, 256), [(wv[:, kc, hd * 128:(hd + 1) * 128], nT[:, kc, 0:256]) for kc in range(KC)],
                             [r_w, r_nT[0], r_nT[1]])
                    copy_any(kmT[:, hd, :], ps[b][:, 0:256], [r_ps[b]], [r_km])
            step([wdesc(w_xkv, 0, KC, 0, 512)], kstep)

            def vstep(slots):
                (wv, r_w), = slots
                for mc in range(2):
                    b = bank()
                    mm_group(b, (0, 512), [(nT[:, kc, mc * 128:(mc + 1) * 128], wv[:, kc, :]) for kc in range(KC)],
                             [r_w, r_nT[mc]])
                    copy_any(vm[:, mc, :], ps[b][:], [r_ps[b]], [r_vm])
            step([wdesc(w_xkv, 0, KC, 512, 512)], vstep)

        A_QTM, A_KTM, A_VTM, A_GS = 0, 8192, 16384, 24576
        A_QT, A_QDT, A_KT, A_ST, A_VD = 32768, 34816, 36864, 38912, 39936
        A_YTMP, A_YRTM, A_COSR, A_SINR = 41984, 46080, 48128, 50176
        A_TMPR = 41984

        class RetBufs:
            pass

        def ret_bufs():
            R = RetBufs()
            R.q_tm = carve(A_QTM, [128, CH, 1024], BF16)
            R.k_tm = carve(A_KTM, [128, CH, 1024], BF16)
            R.v_tm = carve(A_VTM, [128, CH, 1024], BF16)
            R.gs = carve(A_GS, [128, CH, 1024], BF16)
            R.qT = carve(A_QT, [128, 8, 128], BF16)
            R.qdT = carve(A_QDT, [128, 8, 128], BF16)
            R.kT = carve(A_KT, [128, 8, 128], BF16)
            R.sT = carve(A_ST, [128, 4, 128], BF16)
            R.vd = carve(A_VD, [128, 4, 256], BF16)
            R.ytmp = carve(A_YTMP, [128, 4, 256], F32)
            R.yr_tm = carve(A_YRTM, [128, 1024], BF16)
            R.cosr = carve(A_COSR, [128, CH, 128], F32)
            R.sinr = carve(A_SINR, [128, CH, 128], F32)
            R.tmps = [carve(A_TMPR + i * 1024, [128, 2, 128], F32) for i in range(4)]
            R.r_q = [AR(f"q_tm{c}") for c in range(CH)]
            R.r_k = [AR(f"k_tm{c}") for c in range(CH)]
            R.r_v = [AR(f"v_tm{c}") for c in range(CH)]
            R.r_g = [AR(f"gs{c}") for c in range(CH)]
            R.r_qT, R.r_qdT, R.r_kT, R.r_sT, R.r_vd = AR("qT"), AR("qdT"), AR("kT"), AR("sT"), AR("vd")
            R.r_ytmp, R.r_yr, R.r_cs = AR("ytmp"), AR("yr_tm"), AR("cosr")
            R.r_tmp = R.r_ytmp
            return R

        def ret_tables(R, pcol0):
            for c in range(CH):
                sincos(R.sinr[:, c, :], R.cosr[:, c, :], invr[:], pcol0 + c, 128, R.r_cs)

        def rope_evac(R, b, c, dst, r_dst, hsl):
            pv = ps[b][:].rearrange("p (h d) -> p h d", h=2)
            x1, x2 = pv[:, :, 0:128], pv[:, :, 128:256]
            cb = R.cosr[:, c, :].unsqueeze(1).broadcast_to([128, 2, 128])
            sbb = R.sinr[:, c, :].unsqueeze(1).broadcast_to([128, 2, 128])
            dv = dst[:, c, hsl * 512:(hsl + 1) * 512].rearrange("p (h d) -> p h d", h=2)
            t = R.tmps
            rt = R.r_tmp
            P.op("dve", lambda e: e.tensor_tensor(out=t[0], in0=x1, in1=cb, op=ALU.mult), reads=[r_ps[b], R.r_cs], writes=[rt])
            P.op("dve", lambda e: e.tensor_tensor(out=t[1], in0=x2, in1=sbb, op=ALU.mult), reads=[r_ps[b], R.r_cs], writes=[rt])
            P.op("dve", lambda e: e.tensor_tensor(out=t[2], in0=x2, in1=cb, op=ALU.mult), reads=[r_ps[b], R.r_cs], writes=[rt])
            P.op("dve", lambda e: e.tensor_tensor(out=t[3], in0=x1, in1=sbb, op=ALU.mult), reads=[r_ps[b], R.r_cs], writes=[rt])
            P.op("dve", lambda e: e.tensor_tensor(out=dv[:, :, 0:128], in0=t[0], in1=t[1], op=ALU.subtract), reads=[rt], writes=[r_dst])
            P.op("dve", lambda e: e.tensor_tensor(out=dv[:, :, 128:256], in0=t[2], in1=t[3], op=ALU.add), reads=[rt], writes=[r_dst])

        def proj_tm(wv, r_w, c, kcs=KC, cols=512, src=None, r_src=None):
            b = bank()
            if src is None:
                src, r_src = nT, r_nT[c]
                l = lambda kc: nT[:, kc, c * 128:(c + 1) * 128]
            else:
                l = src
            mm_group(b, (0, cols), [(l(kc), wv[:, kc, 0:cols]) for kc in range(kcs)], [r_w, r_src])
            return b

        proj_tm0 = proj_tm

        def ret_kv_steps(R, heads=(0, 1, 2, 3)):
            def proj_tm(wv, r_w, c):
                try:
                    nb = R.nTbuf
                except AttributeError:
                    nb = None
                if nb is None:
                    return proj_tm0(wv, r_w, c)
                return proj_tm0(wv, r_w, c, src=lambda kc: nb[:, kc, c * 128:(c + 1) * 128], r_src=R.r_nTbuf[c])
            if len(heads) == 4:
                for hsl in range(2):
                    def kst(slots, hsl=hsl):
                        (wv, r_w), = slots
                        for c in range(CH):
                            b = proj_tm(wv, r_w, c)
                            rope_evac(R, b, c, R.k_tm, R.r_k[c], hsl)
                    step([wdesc(w_in, 0, KC, O_KR + hsl * 512, 512)], kst)
                for hsl in range(2):
                    def vst(slots, hsl=hsl):
                        (wv, r_w), = slots
                        for c in range(CH):
                            b = proj_tm(wv, r_w, c)
                            copy_any(R.v_tm[:, c, hsl * 512:(hsl + 1) * 512], ps[b][:], [r_ps[b]], [R.r_v[c]])
                    step([wdesc(w_in, 0, KC, O_VR + hsl * 512, 512)], vst)
            else:
                def kst(slots):
                    (wv, r_w), = slots
                    for c in range(CH):
                        b = proj_tm(wv, r_w, c)
                        rope_evac(R, b, c, R.k_tm, R.r_k[c], 1)
                step([wdesc(w_in, 0, KC, O_KR + 512, 512)], kst)

                def vst(slots):
                    (wv, r_w), = slots
                    for c in range(CH):
                        b = proj_tm(wv, r_w, c)
                        copy_any(R.v_tm[:, c, 512:1024], ps[b][:], [r_ps[b]], [R.r_v[c]])
                step([wdesc(w_in, 0, KC, O_VR + 512, 512)], vst)

        def state_update(R, c, heads=(0, 1, 2, 3)):
            h0 = heads[0]
            nh = len(heads)
            P.op("dve", lambda e: e.tensor_tensor(
                out=R.vd[:, h0:h0 + nh, :], in0=R.v_tm[:, c, h0 * 256:(h0 + nh) * 256].rearrange("p (h d) -> p h d", h=nh),
                in1=kdec[:, h0:h0 + nh].unsqueeze(2).broadcast_to([128, nh, 256]), op=ALU.mult),
                reads=[R.r_v[c], r_const], writes=[R.r_vd])
            for h in heads:
                b = bank()
                for dc in range(2):
                    mm_group(b, (dc * 256, dc * 256 + 256),
                             [(R.k_tm[:, c, h * 256 + dc * 128: h * 256 + dc * 128 + 128], R.vd[:, h, :])],
                             [R.r_k[c], R.r_vd])
                P.op("dve", lambda e, h=h, b=b: e.scalar_tensor_tensor(
                    out=S_t[:, 2 * h:2 * h + 2, :], in0=S_t[:, 2 * h:2 * h + 2, :], scalar=g128[h],
                    in1=ps[b][:].rearrange("p (a d) -> p a d", a=2), op0=ALU.mult, op1=ALU.add),
                    reads=[r_ps[b], r_S[h]], writes=[r_S[h]])
                P.op("act", lambda e, h=h: e.activation(out=Sb_t[:, 2 * h:2 * h + 2, :], in_=S_t[:, 2 * h:2 * h + 2, :], func=AF.Copy),
                     reads=[r_S[h]], writes=[r_Sb[h]])

        PP = {}

        def pp_setup(slots):
            new_phase()
            PP["k_tm"] = carve(0, [128, CH, 1024], BF16)
            PP["v_tm"] = carve(8192, [128, CH, 1024], BF16)
            PP["vd"] = carve(16384, [128, 4, 256], BF16)
            PP["tmps"] = [carve(18432 + i * 1024, [128, 2, 128], F32) for i in range(4)]
            PP["cos"] = [carve(22528 + par * 4096, [128, CH, 128], F32) for par in range(2)]
            PP["sin"] = [carve(22528 + par * 4096 + 2048, [128, CH, 128], F32) for par in range(2)]
            PP["nT2"] = carve(30720, [128, KC, TT], BF16)
            PP["r_k"] = [AR(f"ppk{c}") for c in range(CH)]
            PP["r_v"] = [AR(f"ppv{c}") for c in range(CH)]
            PP["r_vd"], PP["r_tmp"] = AR("ppvd"), AR("pptmp")
            PP["r_cs"] = [AR("ppcs0"), AR("ppcs1")]
            PP["r_nT2"] = [AR(f"ppnT2{c}") for c in range(CH)]

        def pp_bufs(pt):
            par = pt % 2
            R = RetBufs()
            R.k_tm, R.v_tm, R.vd, R.tmps = PP["k_tm"], PP["v_tm"], PP["vd"], PP["tmps"]
            R.cosr, R.sinr = PP["cos"][par], PP["sin"][par]
            R.r_k, R.r_v, R.r_vd, R.r_tmp, R.r_cs = PP["r_k"], PP["r_v"], PP["r_vd"], PP["r_tmp"], PP["r_cs"][par]
            if par == 0:
                R.nTbuf, R.r_nTbuf = nT, r_nT
            else:
                R.nTbuf, R.r_nTbuf = PP["nT2"], PP["r_nT2"]
            return R

        def prepass_all(tiles):
            step([], pp_setup)
            boxes = {pt: {} for pt in tiles}

            def mk_norm(pt):
                def f(slots):
                    R = pp_bufs(pt)
                    boxes[pt]["R"] = R
                    for c in range(CH):
                        load_x(x_prev[(pt * CH + c) * 128:(pt * CH + c + 1) * 128, :], c)
                    for c in range(CH):
                        norm_T(h_t[:, c, :], r_h[c], 0, R.nTbuf[:, :, c * 128:(c + 1) * 128], R.r_nTbuf[c])
                    ret_tables(R, 17 + pt * CH)
                return f

            def mk_post(pt, heads):
                def f(slots):
                    R = boxes[pt]["R"]
                    for c in range(CH):
                        state_update(R, c, heads)
                return f

            for i, pt in enumerate(tiles):
                if i == 0:
                    step([], mk_norm(pt))
                if i + 1 < len(tiles):
                    step([], mk_norm(tiles[i + 1]))
                far = pt < NPRE - NTILE
                heads = (2, 3) if far else (0, 1, 2, 3)
                ret_kv_steps(RetProxy(boxes[pt]), heads)
                step([], mk_post(pt, heads))

        class RetProxy:
            def __init__(self, box):
                object.__setattr__(self, "_box", box)

            def __getattr__(self, k):
                return getattr(self._box["R"], k)

        B_SC, B_PROBS, B_PT, B_QST = 0, 16384, 24576, 32768
        B_QSTM, B_KSTM, B_TMPS, B_NTH, B_XH = 40960, 43008, 44032, 45056, 0

        def main_tile(t):
            box = {}
            first = (t == 0)

            def swa_pre(slots):
                new_phase()
                W = RetBufs()
                box["W"] = W
                W.sc = carve(B_SC, [128, 16, 256], F32)
                W.probs = carve(B_PROBS, [128, 16, 256], BF16)
                W.pT = carve(B_PT, [128, 16, 2, 128], BF16)
                W.qsT = carve(B_QST, [128, 8, TT], BF16)
                W.qs_tm = [carve(B_QSTM + i * 1024, [128, 512], BF16) for i in range(2)]
                W.ks_tm = carve(B_KSTM, [128, 4, 128], BF16)
                W.tmps = [carve(B_TMPS + i * 256, [128, 8, 8], F32) for i in range(4)]
                W.nTh = carve(B_NTH, [128, KC, 128], BF16)
                W.xh = carve(B_XH, [128, D], F32)
                W.r_sc, W.r_probs, W.r_pT = AR("sc"), AR("probs"), AR("pT")
                W.r_qsT = [AR(f"qsT{c}") for c in range(CH)]
                W.r_qstm = [AR("qstm0"), AR("qstm1")]
                W.r_kstm, W.r_tmps, W.r_nTh = AR("kstm"), AR("tmps"), AR("nTh")
                W.r_xh = W.r_sc
                for c in range(CH):
                    load_x(x_own[(t * CH + c) * 128:(t * CH + c + 1) * 128, :], c)
                if first:
                    P.op("sp", lambda e: e.dma_start(out=W.xh, in_=x_halo), writes=[W.r_xh], dsem=s_misc)
                    norm_T(W.xh, W.r_xh, 0, W.nTh, W.r_nTh)
                else:
                    P.op("dve", lambda e: e.tensor_copy(out=ksT[:, :, 0:128], in_=ksT[:, :, 512:640]),
                         reads=[r_ks[4]], writes=[r_ks[0]])
                    P.op("dve", lambda e: e.tensor_copy(out=vs_t[:, 0, :], in_=vs_t[:, 4, :]),
                         reads=[r_vs[4]], writes=[r_vs[0]])
                for c in range(CH):
                    norm_T(h_t[:, c, :], r_h[c], 0, nT[:, :, c * 128:(c + 1) * 128], r_nT[c])
            step([], swa_pre)

            def rope_s(W, pv, r_pv, nh, ci, dst, r_dst):
                cb = coss[:, ci, :].unsqueeze(1).broadcast_to([128, nh, 8])
                sbb = sins[:, ci, :].unsqueeze(1).broadcast_to([128, nh, 8])
                x1, x2 = pv[:, :, 0:8], pv[:, :, 8:16]
                t = [w[:, 0:nh, :] for w in W.tmps]
                rt = W.r_tmps
                rd = [r_cs, r_pv]
                P.op("dve", lambda e: e.tensor_tensor(out=t[0], in0=x1, in1=cb, op=ALU.mult), reads=rd, writes=[rt])
                P.op("dve", lambda e: e.tensor_tensor(out=t[1], in0=x2, in1=sbb, op=ALU.mult), reads=rd, writes=[rt])
                P.op("dve", lambda e: e.tensor_tensor(out=t[2], in0=x2, in1=cb, op=ALU.mult), reads=rd, writes=[rt])
                P.op("dve", lambda e: e.tensor_tensor(out=t[3], in0=x1, in1=sbb, op=ALU.mult), reads=rd, writes=[rt])
                P.op("dve", lambda e: e.tensor_tensor(out=dst[:, :, 0:8], in0=t[0], in1=t[1], op=ALU.subtract), reads=[rt], writes=[r_dst])
                P.op("dve", lambda e: e.tensor_tensor(out=dst[:, :, 8:16], in0=t[2], in1=t[3], op=ALU.add), reads=[rt], writes=[r_dst])
                P.op("act", lambda e: e.activation(out=dst[:, :, 16:64], in_=pv[:, :, 16:64], func=AF.Copy), reads=[r_pv], writes=[r_dst])

            for s2 in range(2):
                def qs_step(slots, s2=s2):
                    W = box["W"]
                    (wv, r_w), = slots
                    for c in range(CH):
                        b = proj_tm(wv, r_w, c)
                        i = (s2 * CH + c) % 2
                        dst = W.qs_tm[i].rearrange("p (h d) -> p h d", h=8)
                        pv = ps[b][:].rearrange("p (h d) -> p h d", h=8)
                        rope_s(W, pv, r_ps[b], 8, 1 + t * CH + c, dst, W.r_qstm[i])
                        b2 = bank()
                        pvt = transposes(b2, [W.qs_tm[i][:, j * 128:(j + 1) * 128] for j in range(4)], [W.r_qstm[i]])
                        copy_any(W.qsT[:, 4 * s2:4 * s2 + 4, c * 128:(c + 1) * 128],
                                 pvt[:, 0:512].rearrange("p (a b) -> p a b", a=4), [r_ps[b2]], [W.r_qsT[c]])
                step([wdesc(w_in, 0, KC, O_QS + s2 * 512, 512)], qs_step)

            def kv_step(slots):
                W = box["W"]
                (wv, r_w), = slots
                chunks = ([(-1, 0)] if first else []) + [(c, c + 1) for c in range(CH)]
                for c, slot in chunks:
                    if c < 0:
                        b = proj_tm(wv, r_w, 0, src=lambda kc: W.nTh[:, kc, :], r_src=W.r_nTh)
                        ci = 0
                    else:
                        b = proj_tm(wv, r_w, c)
                        ci = 1 + t * CH + c
                    pv = ps[b][:, 0:256].rearrange("p (h d) -> p h d", h=4)
                    dst = W.ks_tm[:, :, 0:64]
                    rope_s(W, pv, r_ps[b], 4, ci, dst, W.r_kstm)
                    P.op("dve", lambda e: e.tensor_copy(out=W.ks_tm[:, :, 64:128], in_=W.ks_tm[:, :, 0:64]),
                         reads=[W.r_kstm], writes=[W.r_kstm])
                    P.op("act", lambda e, b=b, slot=slot: e.activation(out=vs_t[:, slot, :], in_=ps[b][:, 256:512], func=AF.Copy),
                         reads=[r_ps[b]], writes=[r_vs[slot]])
                    b2 = bank()
                    pvt = transposes(b2, [W.ks_tm[:, g, :] for g in range(4)], [W.r_kstm])
                    copy_any(ksT[:, :, slot * 128:(slot + 1) * 128], pvt[:, 0:512].rearrange("p (a b) -> p a b", a=4),
                             [r_ps[b2]], [r_ks[slot]])
            step([wdesc(w_in, 0, KC, O_KS, 512)], kv_step)

            def mi_mask(first_, c_):
                return 1 if (first_ and c_ == 0) else 0

            def swa_attn(slots):
                W = box["W"]
                mx, r_mx = stat("mx", 16)
                sm, r_sm = stat("sm", 16)
                esk, r_es = stat("es", 16)
                rden, r_rden = stat("rden", 16)
                for c in range(CH):
                    mi = 1 if (first and c == 0) else 0
                    for mp in range(4):
                        bAB = (bank(), bank())
                        for mi in range(2):
                            m = 2 * mp + mi
                            for hh in range(2):
                                hq = 2 * m + hh
                                g = hq // 4
                                po = hh * 64
                                mm_group(bAB[hh], (mi * 256, mi * 256 + 256),
                                         [(W.qsT[po:po + 64, m, c * 128:(c + 1) * 128], ksT[po:po + 64, g, c * 128:c * 128 + 256])],
                                         [W.r_qsT[c], r_ks[c], r_ks[c + 1]])
                        for hh in range(2):
                            b = bAB[hh]
                            for mi in range(2):
                                hq = 4 * mp + 2 * mi + hh
                                P.op("dve", lambda e: e.scalar_tensor_tensor(
                                    out=W.sc[:, hq, :], in0=ps[b][:, mi * 256:(mi + 1) * 256], scalar=0.125,
                                    in1=maskb[:, mi_mask(first, c), :], op0=ALU.mult, op1=ALU.add),
                                    reads=[r_ps[b], r_const], writes=[W.r_sc])
                    P.op("dve", lambda e: e.tensor_reduce(out=mx, in_=W.sc, axis=AX.X, op=ALU.max), reads=[W.r_sc], writes=[r_mx])
                    P.op("dve", lambda e: e.tensor_scalar(out=mx, in0=mx, scalar1=-1.0, scalar2=None, op0=ALU.mult), reads=[r_mx], writes=[r_mx])
                    for hq in range(16):
                        P.op("act", lambda e: e.activation(out=W.sc[:, hq, :], in_=W.sc[:, hq, :], func=AF.Exp, bias=mx[:, hq:hq + 1], scale=1.0,
                                                           accum_out=sm[:, hq:hq + 1]), reads=[W.r_sc, r_mx], writes=[W.r_sc, r_sm])
                    P.op("dve", lambda e: e.tensor_tensor(out=esk, in0=sinkb[:], in1=mx, op=ALU.add), reads=[r_mx, r_const], writes=[r_es])
                    P.op("act", lambda e: e.activation(out=esk, in_=esk, func=AF.Exp), reads=[r_es], writes=[r_es])
                    P.op("dve", lambda e: e.tensor_tensor(out=esk, in0=esk, in1=sm, op=ALU.add), reads=[r_es, r_sm], writes=[r_es])
                    P.op("dve", lambda e: e.reciprocal(out=rden, in_=esk), reads=[r_es], writes=[r_rden])
                    P.op("dve", lambda e: e.tensor_tensor(out=W.probs, in0=W.sc, in1=rden.unsqueeze(2).broadcast_to([128, 16, 256]),
                                                           op=ALU.mult), reads=[W.r_sc, r_rden], writes=[W.r_probs])
                    for q4 in range(4):
                        b = bank()
                        srcs = []
                        for hh in range(4):
                            for half in range(2):
                                srcs.append(W.probs[:, 4 * q4 + hh, half * 128:(half + 1) * 128])
                        pvt = transposes(b, srcs, [W.r_probs])
                        copy_any(W.pT[:, 4 * q4:4 * q4 + 4, :, :].rearrange("p a b c -> p (a b c)"), pvt, [r_ps[b]], [W.r_pT])
                    for bb in range(2):
                        b = bank()
                        for mm_ in range(4):
                            m = bb * 4 + mm_
                            for hh in range(2):
                                hq = 2 * m + hh
                                g = hq // 4

                                def f(e, b=b, mm_=mm_, hh=hh, hq=hq, g=g, c=c):
                                    inst = None
                                    for half in range(2):
                                        inst = e.matmul(ps[b][hh * 64:(hh + 1) * 64, mm_ * 128:(mm_ + 1) * 128],
                                                        lhsT=vs_t[:, c + half, g * 64:(g + 1) * 64], rhs=W.pT[:, hq, half, :],
                                                        start=(half == 0), stop=(half == 1))
                                    return inst
                                P.op("pe", f, reads=[W.r_pT, r_vs[c], r_vs[c + 1]], writes=[r_ps[b]])
                        copy_any(ysT[:, 4 * bb:4 * bb + 4, c * 128:(c + 1) * 128], ps[b][:].rearrange("p (a q) -> p a q", a=4),
                                 [r_ps[b]], [r_ysT[c]])
            step([], swa_attn)
            steps.append(("MARK", "swa"))

            def ret_pre(slots):
                new_phase()
                R = ret_bufs()
                box["R"] = R
                ret_tables(R, 1 + t * CH)
            step([], ret_pre)
            Rp = RetProxy(box)
            for hsl in range(2):
                def qst(slots, hsl=hsl):
                    R = box["R"]
                    (wv, r_w), = slots
                    for c in range(CH):
                        b = proj_tm(wv, r_w, c)
                        rope_evac(R, b, c, R.q_tm, R.r_q[c], hsl)
                step([wdesc(w_in, 0, KC, O_QR + hsl * 512, 512)], qst)
            ret_kv_steps(Rp)
            for hsl in range(2):
                def gst(slots, hsl=hsl):
                    R = box["R"]
                    (wv, r_w), = slots
                    for c in range(CH):
                        b = proj_tm(wv, r_w, c)
                        P.op("act", lambda e, b=b, c=c: e.activation(out=R.gs[:, c, hsl * 512:(hsl + 1) * 512], in_=ps[b][:], func=AF.Silu),
                             reads=[r_ps[b]], writes=[R.r_g[c]])
                step([wdesc(w_in, 0, KC, O_GR + hsl * 512, 512)], gst)

            steps.append(("MARK", "retproj"))

            def ret_core(slots):
                R = box["R"]
                ssr, r_ssr = stat("ssr", 4)
                rs4, r_rs4 = stat("rs4", 4)
                for c in range(CH):
                    bq = bank()
                    pvq = transposes(bq, [R.q_tm[:, c, j * 128:(j + 1) * 128] for j in range(8)], [R.r_q[c]])
                    P.op("act", lambda e: e.activation(out=R.qT[:].rearrange("p a b -> p (a b)"), in_=pvq, func=AF.Copy),
                         reads=[r_ps[bq]], writes=[R.r_qT])
                    for dc in range(2):
                        P.op("dve", lambda e: e.tensor_tensor(
                            out=R.qdT[:].rearrange("p (h a) b -> p h a b", h=4)[:, :, dc, :],
                            in0=pvq.rearrange("p (h a b) -> p h a b", h=4, a=2)[:, :, dc, :],
                            in1=qdecT[:], op=ALU.mult),
                            reads=[r_ps[bq], r_const], writes=[R.r_qdT])
                    bk = bank()
                    pvk = transposes(bk, [R.k_tm[:, c, j * 128:(j + 1) * 128] for j in range(8)], [R.r_k[c]])
                    copy_any(R.kT[:].rearrange("p a b -> p (a b)"), pvk, [r_ps[bk]], [R.r_kT])
                    if ret_level < 2:
                        continue
                    bs = bank()
                    for h in range(4):
                        mm_group(bs, (h * 128, h * 128 + 128),
                                 [(R.kT[:, 2 * h + dc, :], R.qT[:, 2 * h + dc, :]) for dc in range(2)], [R.r_kT, R.r_qT])
                    P.op("dve", lambda e: e.tensor_tensor(out=R.sT, in0=ps[bs][:].rearrange("p (h i) -> p h i", h=4), in1=dmT[:], op=ALU.mult),
                         reads=[r_ps[bs], r_const], writes=[R.r_sT])
                    if ret_level < 3:
                        continue
                    bo = [bank(), bank()]
                    for h in range(4):
                        b = bo[h // 2]
                        pairs = [(R.sT[:, h, :], R.v_tm[:, c, h * 256:(h + 1) * 256])]
                        pairs += [(R.qdT[:, 2 * h + dc, :], Sb_t[:, 2 * h + dc, :]) for dc in range(2)]
                        mm_group(b, ((h % 2) * 256, (h % 2) * 256 + 256), pairs, [R.r_sT, R.r_v[c], R.r_qdT, r_Sb[h]])
                    state_update(R, c)
                    if ret_level < 4:
                        continue
                    for h in range(4):
                        b = bo[h // 2]
                        P.op("act", lambda e, h=h, b=b: e.activation(out=R.ytmp[:, h, :], in_=ps[b][:, (h % 2) * 256:(h % 2) * 256 + 256],
                                                                    func=AF.Square, accum_out=ssr[:, h:h + 1]),
                             reads=[r_ps[b]], writes=[R.r_ytmp, r_ssr])
                    rstd_from(ssr, r_ssr, rs4, r_rs4, 4, 1.0 / 256.0)
                    for hb in range(2):
                        b = bo[hb]
                        P.op("dve", lambda e, hb=hb, b=b: e.tensor_tensor(
                            out=R.ytmp[:, 2 * hb:2 * hb + 2, :], in0=ps[b][:].rearrange("p (a d) -> p a d", a=2),
                            in1=rs4[:, 2 * hb:2 * hb + 2].unsqueeze(2).broadcast_to([128, 2, 256]), op=ALU.mult),
                            reads=[r_ps[b], r_rs4], writes=[R.r_ytmp])
                    P.op("dve", lambda e, c=c: e.tensor_tensor(out=R.yr_tm, in0=R.ytmp[:].rearrange("p a b -> p (a b)"), in1=R.gs[:, c, :], op=ALU.mult),
                         reads=[R.r_ytmp, R.r_g[c]], writes=[R.r_yr])
                    by = bank()
                    pvy = transposes(by, [R.yr_tm[:, j * 128:(j + 1) * 128] for j in range(8)], [R.r_yr])
                    copy_any(yrT[:, :, c * 128:(c + 1) * 128], pvy.rearrange("p (a b) -> p a b", a=8), [r_ps[by]], [r_yrT[c]])
            step([], ret_core)
            steps.append(("MARK", "ret"))

            C_T1, C_SG, C_MT = 0, 8192, 12288

            def merge_pre(slots):
                new_phase()
                M = RetBufs()
                box["M"] = M
                M.t1 = carve(C_T1, [128, 4, TT], F32)
                M.sg = [carve(C_SG + i * 2048, [128, TT], F32) for i in range(2)]
                M.mT = carve(C_MT, [128, KC, TT], BF16)
                M.r_t1 = [AR(f"t1_{j}") for j in range(4)]
                M.r_sg = [AR("sg0"), AR("sg1")]
                M.r_mT = [AR(f"mT{k}") for k in range(KC)]
            step([], merge_pre)

            sg_ctr = [0]
            for s in range(4):
                def mr(slots, s=s):
                    M = box["M"]
                    (wg, r_wg), (wu, r_wu) = slots
                    for j in range(4):
                        bg, bu = bank(), bank()
                        mm_group(bg, (0, 512), [(wg[:, kc, j * 128:(j + 1) * 128], nT[:, kc, :]) for kc in range(KC)], [r_wg] + r_nT)
                        mm_group(bu, (0, 512), [(wu[:, kc, j * 128:(j + 1) * 128], yrT[:, kc, :]) for kc in range(8)], [r_wu] + r_yrT)
                        i = sg_ctr[0] % 2
                        sg_ctr[0] += 1
                        P.op("act", lambda e, bg=bg, i=i: e.activation(out=M.sg[i], in_=ps[bg][:], func=AF.Sigmoid),
                             reads=[r_ps[bg]], writes=[M.r_sg[i]])
                        P.op("dve", lambda e, bu=bu, i=i, j=j: e.tensor_tensor(out=M.t1[:, j, :], in0=M.sg[i], in1=ps[bu][:], op=ALU.mult),
                             reads=[r_ps[bu], M.r_sg[i]], writes=[M.r_t1[j]])
                step([wdesc(w_in, 0, KC, O_GTR + s * 512, 512), wdesc(w_up_ret, 0, 8, s * 512, 512)], mr)

                def ms(slots, s=s):
                    M = box["M"]
                    (wg, r_wg), (wu, r_wu) = slots
                    for j in range(4):
                        bg, bu = bank(), bank()
                        mm_group(bg, (0, 512), [(wg[:, kc, j * 128:(j + 1) * 128], nT[:, kc, :]) for kc in range(KC)], [r_wg] + r_nT)
                        mm_group(bu, (0, 512), [(wu[:, kc, j * 128:(j + 1) * 128], ysT[:, kc, :]) for kc in range(8)], [r_wu] + r_ysT)
                        i = sg_ctr[0] % 2
                        sg_ctr[0] += 1
                        P.op("act", lambda e, bg=bg, i=i: e.activation(out=M.sg[i], in_=ps[bg][:], func=AF.Sigmoid),
                             reads=[r_ps[bg]], writes=[M.r_sg[i]])
                        P.op("dve", lambda e, bu=bu, i=i: e.tensor_tensor(out=M.sg[i], in0=M.sg[i], in1=ps[bu][:], op=ALU.mult),
                             reads=[r_ps[bu], M.r_sg[i]], writes=[M.r_sg[i]])
                        P.op("dve", lambda e, i=i, j=j: e.tensor_tensor(out=M.mT[:, 4 * s + j, :], in0=M.sg[i], in1=M.t1[:, j, :], op=ALU.add),
                             reads=[M.r_sg[i], M.r_t1[j]], writes=[M.r_mT[4 * s + j]])
                step([wdesc(w_in, 0, KC, O_GTS + s * 512, 512), wdesc(w_up_swa, 0, 8, s * 512, 512)], ms)

            def add_into_h(b, c, c0, cols=512):
                P.op("dve", lambda e: e.tensor_tensor(out=h_t[:, c, c0:c0 + cols], in0=h_t[:, c, c0:c0 + cols], in1=ps[b][:, 0:cols], op=ALU.add),
                     reads=[r_ps[b], r_h[c]], writes=[r_h[c]])

            for s in range(4):
                def wo_step(slots, s=s):
                    M = box["M"]
                    (wv, r_w), = slots
                    for c in range(CH):
                        b = bank()
                        mm_group(b, (0, 512), [(M.mT[:, kc, c * 128:(c + 1) * 128], wv[:, kc, :]) for kc in range(KC)], [r_w] + M.r_mT)
                        add_into_h(b, c, s * 512)
                step([wdesc(w_o, 0, KC, s * 512, 512)], wo_step)

            steps.append(("MARK", "wo"))
            X_QX, X_SC, X_PX, X_PT, X_OX = 0, 4096, 8192, 10240, 12288

            def x_pre(slots):
                new_phase()
                X = RetBufs()
                box["X"] = X
                X.qxT = carve(X_QX, [128, 4, TT], BF16)
                X.sc = carve(X_SC, [128, 4, 256], F32)
                X.px = carve(X_PX, [128, 4, 256], BF16)
                X.pT = carve(X_PT, [128, 4, 2, 128], BF16)
                X.oxT = carve(X_OX, [128, 4, TT], BF16)
                X.r_qx, X.r_sc, X.r_px, X.r_pT = AR("qxT"), AR("xsc"), AR("px"), AR("xpT")
                X.r_ox = [AR(f"oxT{c}") for c in range(CH)]
                for c in range(CH):
                    norm_T(h_t[:, c, :], r_h[c], 1, nT[:, :, c * 128:(c + 1) * 128], r_nT[c])
            step([], x_pre)

            def xq_step(slots):
                X = box["X"]
                (wv, r_w), = slots
                for hd in range(4):
                    b = bank()
                    mm_group(b, (0, 512), [(wv[:, kc, hd * 128:(hd + 1) * 128], nT[:, kc, :]) for kc in range(KC)], [r_w] + r_nT)
                    P.op("act", lambda e, b=b, hd=hd: e.activation(out=X.qxT[:, hd, :], in_=ps[b][:], func=AF.Copy, scale=128.0 ** -0.5),
                         reads=[r_ps[b]], writes=[X.r_qx])
                mx, r_mx = stat("xmx", 4)
                sm, r_sm = stat("xsm", 4)
                for c in range(CH):
                    bb = [bank(), bank()]
                    for hd in range(4):
                        mm_group(bb[hd // 2], ((hd % 2) * 256, (hd % 2) * 256 + 256),
                                 [(X.qxT[:, hd, c * 128:(c + 1) * 128], kmT[:, hd, :])], [X.r_qx, r_km])
                    for i2 in range(2):
                        copy_any(X.sc[:, 2 * i2:2 * i2 + 2, :], ps[bb[i2]][:].rearrange("p (a k) -> p a k", a=2), [r_ps[bb[i2]]], [X.r_sc])
                    P.op("dve", lambda e: e.tensor_reduce(out=mx, in_=X.sc, axis=AX.X, op=ALU.max), reads=[X.r_sc], writes=[r_mx])
                    P.op("dve", lambda e: e.tensor_tensor(out=X.sc, in0=X.sc, in1=mx.unsqueeze(2).broadcast_to([128, 4, 256]), op=ALU.subtract),
                         reads=[X.r_sc, r_mx], writes=[X.r_sc])
                    P.op("act", lambda e: e.activation(out=X.sc, in_=X.sc, func=AF.Exp), reads=[X.r_sc], writes=[X.r_sc])
                    P.op("dve", lambda e: e.tensor_reduce(out=sm, in_=X.sc, axis=AX.X, op=ALU.add), reads=[X.r_sc], writes=[r_sm])
                    P.op("dve", lambda e: e.reciprocal(out=sm, in_=sm), reads=[r_sm], writes=[r_sm])
                    P.op("dve", lambda e: e.tensor_tensor(out=X.px, in0=X.sc, in1=sm.unsqueeze(2).broadcast_to([128, 4, 256]), op=ALU.mult),
                         reads=[X.r_sc, r_sm], writes=[X.r_px])
                    b = bank()
                    srcs = [X.px[:, hd, half * 128:(half + 1) * 128] for hd in range(4) for half in range(2)]
                    pvt = transposes(b, srcs, [X.r_px])
                    copy_any(X.pT[:].rearrange("p a b c -> p (a b c)"), pvt, [r_ps[b]], [X.r_pT])
                    b = bank()
                    for hd in range(4):
                        mm_group(b, (hd * 128, hd * 128 + 128),
                                 [(vm[:, half, hd * 128:(hd + 1) * 128], X.pT[:, hd, half, :]) for half in range(2)], [r_vm, X.r_pT])
                    copy_any(X.oxT[:, :, c * 128:(c + 1) * 128], ps[b][:].rearrange("p (a q) -> p a q", a=4), [r_ps[b]], [X.r_ox[c]])
            step([wdesc(w_xq, 0, KC, 0, 512)], xq_step)

            def xo_step(slots):
                X = box["X"]
                (wv, r_w), = slots
                for s in range(4):
                    for c in range(CH):
                        b = bank()
                        mm_group(b, (0, 512), [(X.oxT[:, kc, c * 128:(c + 1) * 128], wv[:, kc, s * 512:(s + 1) * 512]) for kc in range(4)],
                                 [r_w, X.r_ox[c]])
                        add_into_h(b, c, s * 512)
            step([wdesc(w_xo, 0, 4, 0, 2048)], xo_step)

            steps.append(("MARK", "xattn"))
            F_ACT, F_SG, F_GF = 0, 45056, 0

            def f_pre(slots):
                new_phase()
                Fb = RetBufs()
                box["F"] = Fb
                Fb.aT = carve(F_ACT, [128, 44, TT], BF16)
                Fb.sg = [carve(F_SG + i * 2048, [128, TT], F32) for i in range(2)]
                Fb.r_aT = [AR(f"aT{k}") for k in range(44)]
                Fb.r_sg = [AR("fsg0"), AR("fsg1")]
                for c in range(CH):
                    norm_T(h_t[:, c, :], r_h[c], 2, nT[:, :, c * 128:(c + 1) * 128], r_nT[c])
            step([], f_pre)

            for s in range(11):
                def fgu(slots, s=s):
                    Fb = box["F"]
                    (wg, r_wg), (wu, r_wu) = slots
                    for j in range(4):
                        bg, bu = bank(), bank()
                        mm_group(bg, (0, 512), [(wg[:, kc, j * 128:(j + 1) * 128], nT[:, kc, :]) for kc in range(KC)], [r_wg] + r_nT)
                        mm_group(bu, (0, 512), [(wu[:, kc, j * 128:(j + 1) * 128], nT[:, kc, :]) for kc in range(KC)], [r_wu] + r_nT)
                        i = sg_ctr[0] % 2
                        sg_ctr[0] += 1
                        P.op("act", lambda e, bg=bg, i=i: e.activation(out=Fb.sg[i], in_=ps[bg][:], func=AF.Silu),
                             reads=[r_ps[bg]], writes=[Fb.r_sg[i]])
                        P.op("dve", lambda e, bu=bu, i=i, j=j: e.tensor_tensor(out=Fb.aT[:, 4 * s + j, :], in0=Fb.sg[i], in1=ps[bu][:], op=ALU.mult),
                             reads=[r_ps[bu], Fb.r_sg[i]], writes=[Fb.r_aT[4 * s + j]])
                step([wdesc(w_fg, 0, KC, s * 512, 512), wdesc(w_fu, 0, KC, s * 512, 512)], fgu)

            kgroups = [(0, 16), (16, 16), (32, 12)]
            for cs in range(4):
                dbanks = {}
                for gi, (k0, kn) in enumerate(kgroups):
                    def fd(slots, cs=cs, gi=gi, k0=k0, kn=kn, dbanks=dbanks):
                        Fb = box["F"]
                        (wv, r_w), = slots
                        if gi == 0:
                            for c in range(CH):
                                dbanks[c] = bank()
                        for c in range(CH):
                            b = dbanks[c]

                            def f(e, b=b, c=c):
                                inst = None
                                for kk in range(kn):
                                    kc = k0 + kk
                                    inst = e.matmul(ps[b][:], lhsT=Fb.aT[:, kc, c * 128:(c + 1) * 128], rhs=wv[:, kk, :],
                                                    start=(kc == 0), stop=(kc == 43))
                                return inst
                            P.op("pe", f, reads=[r_w] + Fb.r_aT[k0:k0 + kn], writes=[r_ps[b]])
                            if gi == 2:
                                add_into_h(b, c, cs * 512)
                    step([wdesc(w_fd, k0 * 128, kn, cs * 512, 512)], fd)

            steps.append(("MARK", "ffn"))
            def fin(slots):
                new_phase()
                gf = carve(F_GF, [128, D], F32)
                r_gf = AR("gfin")
                P.op("sp", lambda e: e.dma_start(out=gf, in_=gfin_in), writes=[r_gf], dsem=s_misc)
                for c in range(CH):
                    P.op("act", lambda e, c=c: e.activation(out=xnb[:], in_=h_t[:, c, :], func=AF.Square, accum_out=ss_ap),
                         reads=[r_h[c]], writes=[r_xnb, r_ss])
                    rstd_from(ss_ap, r_ss, rstd_ap, r_rstd, 1, 1.0 / D)
                    P.op("dve", lambda e, c=c: e.scalar_tensor_tensor(out=h_t[:, c, :], in0=h_t[:, c, :], scalar=rstd_ap, in1=gf,
                                                                      op0=ALU.mult, op1=ALU.mult),
                         reads=[r_h[c], r_rstd, r_gf], writes=[r_h[c]])
                    P.op("sp", lambda e, c=c: e.dma_start(out=y_out[(t * CH + c) * 128:(t * CH + c + 1) * 128, :], in_=h_t[:, c, :]),
                         reads=[r_h[c]], dsem=s_y[c])
            step([], fin)

        def zero_state(slots):
            for h in range(4):
                P.op("dve", lambda e, h=h: e.memset(S_t[:, 2 * h:2 * h + 2, :], 0.0), writes=[r_S[h]])
                P.op("dve", lambda e, h=h: e.memset(Sb_t[:, 2 * h:2 * h + 2, :], 0.0), writes=[r_Sb[h]])
        step([], zero_state)
        mem_setup()
        if n_pre > 0:
            prepass_all(list(range(NPRE - n_pre, NPRE)))
        for t in range(n_main_tiles):
            main_tile(t)

        if stop_after is not None:
            cut = [i for i, st_ in enumerate(steps) if st_ == ("MARK", stop_after)][0]
            del steps[cut:]
        steps[:] = [st_ for st_ in steps if st_[0] != "MARK"]
        all_descs = []
        for descs, fn in steps:
            for d in descs:
                all_descs.append(d)
        issued = [0]
        consumed = [0]

        scratch = {}
        s_ws = [P.dma_sem(f"s_ws{i}") for i in range(NSLOT)]

        def issue(i):
            v, kcs, cols, key = all_descs[i]
            slot = i % NSLOT
            dst = wview(slot, kcs, cols)
            n = kcs * cols
            if key not in scratch:
                P.op("pool", lambda e: e.dma_start(out=dst, in_=v), writes=[r_wr[slot]], dsem=s_wr[slot])
                if USE_SCRATCH:
                    scr = nc.dram_tensor(f"scr{len(scratch)}", [128, n], BF16, kind="Internal").ap()
                    r_scr = Res(f"scr{len(scratch)}")
                    scratch[key] = (scr, r_scr)
                    flat = wr[slot][:].bitcast(BF16)[:, 0:n]
                    P.op("sp", lambda e: e.dma_start(out=scr, in_=flat), reads=[r_wr[slot]], writes=[r_scr], dsem=s_ws[slot])
            else:
                scr, r_scr = scratch[key]
                flat = wr[slot][:].bitcast(BF16)[:, 0:n]
                P.op("pool", lambda e: e.dma_start(out=flat, in_=scr), reads=[r_scr], writes=[r_wr[slot]], dsem=s_wr[slot])

        total = len(all_descs)
        for descs, fn in steps:
            need = len(descs)
            assert need <= NSLOT
            while issued[0] < consumed[0] + need:
                issue(issued[0])
                issued[0] += 1
            while issued[0] < total and issued[0] - consumed[0] < NSLOT:
                issue(issued[0])
                issued[0] += 1
            slots = []
            for k in range(need):
                i = consumed[0] + k
                v, kcs, cols, key = all_descs[i]
                slots.append((wview(i % NSLOT, kcs, cols), r_wr[i % NSLOT]))
            fn(slots)
            consumed[0] += need

        finals = [(s.idx, s.val) for s in s_y if s.val]
        P.run(finals)
        build.n_ops = P.n_ops
    return nc


def _consts(core):
    f32 = np.float32
    half = 128
    invr = (f32(1.0) / (f32(10000.0) ** (np.arange(half, dtype=f32) / f32(half)))).astype(f32)
    invs = (f32(1.0) / (f32(500000.0) ** (np.arange(8, dtype=f32) / f32(8)))).astype(f32)
    qi = np.arange(128)[:, None]
    kj = np.arange(256)[None, :]
    valid = (kj >= qi + 1) & (kj <= qi + 128)
    m0 = np.where(valid, 0.0, -1e30).astype(f32)
    m1 = m0.copy()
    if core == 0:
        m1[:, :128] = -1e30
    maskb = np.concatenate([m0, m1], axis=1)
    log_g = np.log(1.0 - np.power(2.0, -5.0 - np.arange(4, dtype=np.float64)))
    j = np.arange(128)[:, None]
    i = np.arange(128)[None, :]
    dmT = np.zeros((128, 4, 128), f32)
    qdecT = np.zeros((128, 4, 128), f32)
    kdec = np.zeros((128, 4), f32)
    for h in range(4):
        dmT[:, h, :] = np.where(i >= j, np.exp(log_g[h] * np.maximum(i - j, 0)), 0.0) / 16.0
        qdecT[:, h, :] = np.exp(log_g[h] * (np.arange(128) + 1.0))[None, :]
        kdec[:, h] = np.exp(log_g[h] * (127.0 - np.arange(128))) / 16.0
    return {
        "invr": np.ascontiguousarray(np.broadcast_to(invr[None, :], (128, 128))),
        "invs": np.ascontiguousarray(np.broadcast_to(invs[None, :], (128, 8))),
        "maskb": np.ascontiguousarray(maskb),
        "dmT": np.ascontiguousarray(dmT.reshape(128, 512)),
        "qdecT": np.ascontiguousarray(qdecT.reshape(128, 512)),
        "kdec": kdec,
    }


def _make_in_maps(x, mem, positions, g_mix, w_in, w_up_ret, w_up_swa, sinks, w_o, g_x, g_mem,
                  w_xq, w_xkv, w_xo, g_ffn, w_ffn_gate, w_ffn_up, w_ffn_down, g_final):
    c = np.ascontiguousarray
    x2 = np.asarray(x, np.float32).reshape(SEQ, D)
    pos = np.asarray(positions).reshape(SEQ).astype(np.int32)
    shared = {
        "mem": c(np.asarray(mem, np.float32).reshape(256, D)),
        "w_in": c(np.asarray(w_in, np.float32)[0]),
        "w_up_ret": c(np.asarray(w_up_ret, np.float32)[0]),
        "w_up_swa": c(np.asarray(w_up_swa, np.float32)[0]),
        "w_o": c(np.asarray(w_o, np.float32)[0]),
        "w_xq": c(np.asarray(w_xq, np.float32)[0]),
        "w_xkv": c(np.asarray(w_xkv, np.float32)[0]),
        "w_xo": c(np.asarray(w_xo, np.float32)[0]),
        "w_fg": c(np.asarray(w_ffn_gate, np.float32)[0]),
        "w_fu": c(np.asarray(w_ffn_up, np.float32)[0]),
        "w_fd": c(np.asarray(w_ffn_down, np.float32)[0]),
        "gfin": c(np.broadcast_to(np.asarray(g_final, np.float32).reshape(1, D), (128, D))),
        "sinkb": c(np.broadcast_to(np.asarray(sinks, np.float32).reshape(1, 16), (128, 16))),
    }
    gs = [np.asarray(g, np.float32).reshape(D) for g in (g_mix, g_x, g_ffn, g_mem)]
    shared["gcols"] = c(np.stack([g.reshape(16, 128).T for g in gs], axis=1).reshape(128, 64))
    in_maps = []
    npre_tok = NPRE * TT
    for core in range(NCORE):
        t0 = core * TOK
        xp = np.zeros((npre_tok + 128, D), np.float32)
        pp = np.zeros((npre_tok,), np.int32)
        lo = t0 - npre_tok
        if lo < 0:
            n = t0
            if n > 0:
                xp[npre_tok - n:npre_tok] = x2[0:t0]
                pp[npre_tok - n:] = pos[0:t0]
        else:
            xp[:npre_tok] = x2[lo:t0]
            pp[:] = pos[lo:t0]
        x_prev = c(xp[:npre_tok])
        x_halo = c(x_prev[npre_tok - 128:npre_tok])
        pos_halo = pp[npre_tok - 128:]
        pos_all = np.concatenate([pos_halo, pos[t0:t0 + TOK], pp])
        pos_dev = c(pos_all.reshape(17 + NPRE * CH, 128).T)
        m = dict(shared)
        m.update(_consts(core))
        m["x_own"] = c(x2[t0:t0 + TOK])
        m["x_halo"] = x_halo
        m["x_prev"] = x_prev
        m["pos"] = pos_dev
        in_maps.append(m)
    return in_maps


_NC_CACHE = {}


def kernel(**inputs):
    in_maps = _make_in_maps(**inputs)
    if "nc" not in _NC_CACHE:
        _NC_CACHE["nc"] = build()
    nc = _NC_CACHE["nc"]
    res = run_bass_kernel_spmd(nc, in_maps, core_ids=list(range(NCORE)))
    out = np.concatenate([np.asarray(r["y"], np.float32) for r in res.results], axis=0)
    return out.reshape(1, SEQ, D)
```

```python
import math
from contextlib import ExitStack

import numpy as np
import concourse.bass as bass
import concourse.mybir as mybir
from concourse.bass_utils import run_bass_kernel_spmd

F32 = mybir.dt.float32
BF16 = mybir.dt.bfloat16
I32 = mybir.dt.int32
AF = mybir.ActivationFunctionType
ALU = mybir.AluOpType
AX = mybir.AxisListType

D = 2048
KC = 16
SEQ = 16384
NCORE = 8
TOK = SEQ // NCORE
TT = 512
CH = 4
NTILE = TOK // TT
NPRE = 8
DFF = 5632
EPS = 1e-6
NSLOT = 3
USE_SCRATCH = True
TWO_PI = 2.0 * math.pi

O_QR, O_KR, O_VR, O_GR, O_QS, O_KS, O_VS, O_GTR, O_GTS = 0, 1024, 2048, 3072, 4096, 5120, 5376, 5632, 7680


class Res:
    __slots__ = ("name", "w", "r")

    def __init__(self, name, r=None):
        self.name = name
        self.w = None
        self.r = list(r) if r else []


class DmaSem:
    def __init__(self, idx):
        self.idx = idx
        self.val = 0


class _Rec:
    def __init__(self):
        self.calls = []

    def __getattr__(self, name):
        def m(*a, **k):
            self.calls.append((name, a, k))
            return self
        return m


class Prog:
    ENGS = ("pe", "act", "dve", "pool", "sp")

    def __init__(self, nc, es):
        self.nc = nc
        self.es = es
        self.sems = []
        self.q = {e: [] for e in self.ENGS}
        self.cnt = {e: 0 for e in self.ENGS}
        self.waited = {e: {} for e in self.ENGS}
        self.esem = {}
        for e in self.ENGS:
            self.esem[e] = self.new_sem("s_" + e)
        self.n_ops = 0

    def new_sem(self, name):
        h = self.es.enter_context(self.nc.semaphore(name))
        self.sems.append(h)
        return len(self.sems) - 1

    def dma_sem(self, name):
        return DmaSem(self.new_sem(name))

    def op(self, eng, fn, reads=(), writes=(), dsem=None):
        waits = {}
        wd = self.waited[eng]
        own = self.esem[eng]

        def need(ev):
            if ev is None:
                return
            s, v = ev
            if s == own and eng == "pe":
                return
            if wd.get(s, 0) >= v:
                return
            if waits.get(s, 0) < v:
                waits[s] = v

        for r in reads:
            need(r.w)
            if r.name.startswith("ps") and eng != "pe":
                for ev in r.r:
                    if ev[0] != own:
                        need(ev)
        for w in writes:
            need(w.w)
            for ev in w.r:
                need(ev)
        if dsem is not None and dsem.val > 0:
            need((dsem.idx, dsem.val))
        for s, v in waits.items():
            wd[s] = v
        if dsem is None:
            self.cnt[eng] += 1
            ev = (self.esem[eng], self.cnt[eng])
            inc = (self.esem[eng], 1)
        else:
            dsem.val += 16
            ev = (dsem.idx, dsem.val)
            inc = (dsem.idx, 16)
        rec = _Rec()
        fn(rec)
        self.q[eng].append((list(waits.items()), rec.calls, inc))
        for r in reads:
            r.r.append(ev)
            if len(r.r) > 24:
                m = {}
                for s, v in r.r:
                    if m.get(s, 0) < v:
                        m[s] = v
                r.r = list(m.items())
        for w in writes:
            w.w = ev
            w.r = []
        self.n_ops += 1
        return ev

    def replay(self, eng, e):
        for waits, calls, inc in self.q[eng]:
            for s, v in waits:
                e.wait_ge(self.sems[s], v)
            inst = None
            for name, a, k in calls:
                inst = getattr(e, name)(*a, **k)
            inst.then_inc(self.sems[inc[0]], inc[1])

    def run(self, final_events):
        nc = self.nc
        with nc.Block() as block:
            @block.tensor
            def _(e):
                self.replay("pe", e)

            @block.scalar
            def _(e):
                self.replay("act", e)

            @block.vector
            def _(e):
                self.replay("dve", e)

            @block.gpsimd
            def _(e):
                self.replay("pool", e)

            @block.sync
            def _(e):
                self.replay("sp", e)
                for s, v in final_events:
                    e.wait_ge(self.sems[s], v)


def build(n_main_tiles=NTILE, n_pre=NPRE, dbg=None, stop_after=None, ret_level=9):
    nc = bass.Bass("TRN2", target_bir_lowering=False)

    def din(n, s, d=F32):
        return nc.dram_tensor(n, s, d, kind="ExternalInput").ap()

    x_own = din("x_own", [TOK, D])
    x_halo = din("x_halo", [128, D])
    x_prev = din("x_prev", [NPRE * TT, D])
    pos_in = din("pos", [128, 17 + NPRE * CH], I32)
    mem_in = din("mem", [256, D])
    w_in = din("w_in", [D, 9728])
    w_up_ret = din("w_up_ret", [1024, D])
    w_up_swa = din("w_up_swa", [1024, D])
    w_o = din("w_o", [D, D])
    w_xq = din("w_xq", [D, 512])
    w_xkv = din("w_xkv", [D, 1024])
    w_xo = din("w_xo", [512, D])
    w_fg = din("w_fg", [D, DFF])
    w_fu = din("w_fu", [D, DFF])
    w_fd = din("w_fd", [DFF, D])
    gcols_in = din("gcols", [128, 4 * 16])
    gfin_in = din("gfin", [128, D])
    sinkb_in = din("sinkb", [128, 16])
    invr_in = din("invr", [128, 128])
    invs_in = din("invs", [128, 8])
    maskb_in = din("maskb", [128, 2 * 256])
    dmT_in = din("dmT", [128, 4 * 128])
    qdecT_in = din("qdecT", [128, 4 * 128])
    kdec_in = din("kdec", [128, 4])
    y_out = nc.dram_tensor("y", [TOK, D], F32, kind="ExternalOutput").ap()
    dbg_outs = {}

    g128 = [float(np.exp(np.log(1.0 - 2.0 ** (-5.0 - h)) * 128.0)) for h in range(4)]

    es = ExitStack()
    with es:
        P = Prog(nc, es)

        def sb(n, s, d):
            return es.enter_context(nc.sbuf_tensor("sb_" + n, s, d))

        h_t = sb("h", [128, CH, D], F32)
        xnb = sb("xnb", [128, D], BF16)
        nT = sb("nT", [128, KC, TT], BF16)
        wr = [sb(f"wr{i}", [128, 4096], F32) for i in range(NSLOT)]
        ysT = sb("ysT", [128, 8, TT], BF16)
        yrT = sb("yrT", [128, 8, TT], BF16)
        S_t = sb("S", [128, 8, 256], F32)
        Sb_t = sb("Sb", [128, 8, 256], BF16)
        ksT = sb("ksT", [128, 4, 640], BF16)
        vs_t = sb("vs", [128, 5, 256], BF16)
        maskb = sb("maskb", [128, 2, 256], F32)
        dmT = sb("dmT", [128, 4, 128], F32)
        qdecT = sb("qdecT", [128, 4, 128], F32)
        kdec = sb("kdec", [128, 4], F32)
        invr = sb("invr", [128, 128], F32)
        invs = sb("invs", [128, 8], F32)
        sinkb = sb("sinkb", [128, 16], F32)
        gcols = sb("gcols", [128, 4, 16], F32)
        ident = sb("ident", [128, 128], BF16)
        identf = sb("identf", [128, 128], F32)
        kmT = sb("kmT", [128, 4, 256], BF16)
        vm = sb("vm", [128, 2, 512], BF16)
        posi = sb("posi", [128, 17 + NPRE * CH], I32)
        posf = sb("posf", [128, 17 + NPRE * CH], F32)
        coss = sb("coss", [128, 17, 8], F32)
        sins = sb("sins", [128, 17, 8], F32)
        st = sb("st", [128, 160], F32)
        cst = sb("cst", [128, 4], F32)
        ARENA_F32 = 13440
        arena = sb("arena", [128, ARENA_F32], F32)
        ps = [es.enter_context(nc.psum_tensor(f"ps{i}", [128, 512], F32)) for i in range(8)]

        r_h = [Res(f"h{c}") for c in range(CH)]
        r_xnb = Res("xnb")
        r_nT = [Res(f"nT{c}") for c in range(CH)]
        r_wr = [Res(f"wr{i}") for i in range(NSLOT)]
        s_wr = [P.dma_sem(f"s_wr{i}") for i in range(NSLOT)]
        r_ps = [Res(f"ps{i}") for i in range(8)]
        r_ysT = [Res(f"ysT{c}") for c in range(CH)]
        r_yrT = [Res(f"yrT{c}") for c in range(CH)]
        r_S = [Res(f"S{h}") for h in range(4)]
        r_Sb = [Res(f"Sb{h}") for h in range(4)]
        r_ks = [Res(f"ks{s}") for s in range(5)]
        r_vs = [Res(f"vs{s}") for s in range(5)]
        r_const = Res("const")
        r_id = Res("ident")
        r_km = Res("kmT")
        r_vm = Res("vm")
        r_pos = Res("pos")
        r_st = {}
        s_ld = P.dma_sem("s_ld")
        s_x = [P.dma_sem(f"s_x{c}") for c in range(CH)]
        s_y = [P.dma_sem(f"s_y{c}") for c in range(CH)]
        s_misc = P.dma_sem("s_misc")

        bank_ctr = [0]

        def bank():
            b = bank_ctr[0] % 8
            bank_ctr[0] += 1
            return b

        arena_res = []
        fence = [[]]

        def AR(name):
            r = Res(name, fence[0])
            arena_res.append(r)
            return r

        def new_phase():
            m = {}
            for s, v in fence[0]:
                m[s] = max(m.get(s, 0), v)
            for r in arena_res:
                evs = list(r.r)
                if r.w is not None:
                    evs.append(r.w)
                for s, v in evs:
                    if m.get(s, 0) < v:
                        m[s] = v
            fence[0] = list(m.items())
            arena_res.clear()

        def carve(off_bytes, shape, dt):
            n = 1
            for s in shape[1:]:
                n *= s
            if dt == F32:
                a = arena[:, off_bytes // 4: off_bytes // 4 + n]
            else:
                nf = (n + 1) // 2
                a = arena[:, off_bytes // 4: off_bytes // 4 + nf].bitcast(BF16)[:, 0:n]
            if len(shape) == 2:
                return a
            if len(shape) == 3:
                return a.rearrange("p (a b) -> p a b", a=shape[1])
            if len(shape) == 4:
                return a.rearrange("p (a b c) -> p a b c", a=shape[1], b=shape[2])
            raise ValueError

        def stat(name, n):
            if name not in r_st:
                off = sum(v[1] for v in r_st.values())
                assert off + n <= 160
                r_st[name] = (off, n, Res("st_" + name))
            off, n0, r = r_st[name]
            return st[:, off:off + n0], r

        def wview(slot, kcs, cols):
            return wr[slot][:].bitcast(BF16)[:, 0:kcs * cols].rearrange("p (k c) -> p k c", k=kcs)

        def ld(dst_ap, src_ap, res, q="sp"):
            P.op(q, lambda e: e.dma_start(out=dst_ap, in_=src_ap), writes=[res], dsem=s_ld)

        ld(maskb[:].rearrange("p a b -> p (a b)"), maskb_in, r_const)
        ld(dmT[:].rearrange("p a b -> p (a b)"), dmT_in, r_const)
        ld(qdecT[:].rearrange("p a b -> p (a b)"), qdecT_in, r_const)
        ld(kdec[:], kdec_in, r_const)
        ld(invr[:], invr_in, r_const)
        ld(invs[:], invs_in, r_const)
        ld(sinkb[:], sinkb_in, r_const)
        ld(gcols[:].rearrange("p a b -> p (a b)"), gcols_in, r_const)
        ld(posi[:], pos_in, r_pos)
        P.op("pool", lambda e: e.memset(identf[:], 1.0), writes=[r_id])
        P.op("pool", lambda e: e.affine_select(out=identf[:], in_=identf[:], pattern=[[-1, 128]],
                                               compare_op=ALU.is_equal, fill=0.0, base=0, channel_multiplier=1),
             reads=[r_id], writes=[r_id])
        P.op("dve", lambda e: e.tensor_copy(out=ident[:], in_=identf[:]), reads=[r_id], writes=[r_id])
        P.op("dve", lambda e: e.memset(cst[:, 0:1], math.pi), writes=[r_const])
        P.op("dve", lambda e: e.memset(cst[:, 1:2], EPS), writes=[r_const])
        P.op("dve", lambda e: e.tensor_copy(out=posf[:], in_=posi[:]), reads=[r_pos], writes=[r_pos])

        tg_ang = sb("tg_ang", [128, 128], F32)
        tg_r = sb("tg_r", [128, 128], F32)
        tg_kf = sb("tg_kf", [128, 128], F32)
        tg_ki = sb("tg_ki", [128, 128], I32)
        r_tg = Res("tg")
        CW1 = 6.28125
        CW2 = TWO_PI - 6.28125

        def sincos(dst_sin, dst_cos, inv_ap, pcol, n, r_dst, tmp=None, r_tmp=None):
            ang, rr, kf, ki = tg_ang[:, 0:n], tg_r[:, 0:n], tg_kf[:, 0:n], tg_ki[:, 0:n]
            P.op("dve", lambda e: e.tensor_scalar(out=ang, in0=inv_ap, scalar1=posf[:, pcol:pcol + 1], scalar2=None,
                                                  op0=ALU.mult), reads=[r_pos, r_const], writes=[r_tg])
            for which, dst in ((0, dst_sin), (1, dst_cos)):
                if which == 1:
                    P.op("dve", lambda e: e.tensor_scalar(out=ang, in0=ang, scalar1=0.5 * math.pi, scalar2=None, op0=ALU.add),
                         reads=[r_tg], writes=[r_tg])
                P.op("dve", lambda e: e.tensor_scalar(out=ki, in0=ang, scalar1=1.0 / TWO_PI, scalar2=None, op0=ALU.mult),
                     reads=[r_tg], writes=[r_tg])
                P.op("dve", lambda e: e.tensor_copy(out=kf, in_=ki), reads=[r_tg], writes=[r_tg])
                P.op("dve", lambda e: e.scalar_tensor_tensor(out=rr, in0=kf, scalar=-CW1, in1=ang, op0=ALU.mult, op1=ALU.add),
                     reads=[r_tg], writes=[r_tg])
                P.op("dve", lambda e: e.scalar_tensor_tensor(out=rr, in0=kf, scalar=-CW2, in1=rr, op0=ALU.mult, op1=ALU.add),
                     reads=[r_tg], writes=[r_tg])
                P.op("dve", lambda e: e.tensor_scalar(out=kf, in0=rr, scalar1=math.pi, scalar2=-TWO_PI, op0=ALU.is_gt, op1=ALU.mult),
                     reads=[r_tg], writes=[r_tg])
                P.op("dve", lambda e: e.tensor_tensor(out=rr, in0=rr, in1=kf, op=ALU.add), reads=[r_tg], writes=[r_tg])
                P.op("dve", lambda e: e.tensor_scalar(out=kf, in0=rr, scalar1=-math.pi, scalar2=TWO_PI, op0=ALU.is_lt, op1=ALU.mult),
                     reads=[r_tg], writes=[r_tg])
                P.op("dve", lambda e: e.tensor_tensor(out=rr, in0=rr, in1=kf, op=ALU.add), reads=[r_tg], writes=[r_tg])
                P.op("act", lambda e, dst=dst: e.activation(out=dst, in_=rr, func=AF.Sin), reads=[r_tg], writes=[r_dst, r_tg])

        r_cs = Res("coss")
        tmp8, r_tmp8 = stat("tmp8", 8)
        for ci in range(17):
            sincos(sins[:, ci, :], coss[:, ci, :], invs[:], ci, 8, r_cs)

        ss_ap, r_ss = stat("ss", 1)
        rstd_ap, r_rstd = stat("rstd", 1)

        def rstd_from(ss, r_s, out, r_o, n, inv_n):
            P.op("act", lambda e: e.activation(out=out, in_=ss, func=AF.Sqrt, bias=cst[:, 1:2], scale=inv_n),
                 reads=[r_s, r_const], writes=[r_o])
            P.op("dve", lambda e: e.reciprocal(out=out, in_=out), reads=[r_o], writes=[r_o])

        def norm_T(src, r_src, gi, dst, r_dst):
            P.op("act", lambda e: e.activation(out=xnb[:], in_=src, func=AF.Square, accum_out=ss_ap),
                 reads=[r_src], writes=[r_xnb, r_ss])
            rstd_from(ss_ap, r_ss, rstd_ap, r_rstd, 1, 1.0 / D)
            P.op("dve", lambda e: e.tensor_scalar(out=xnb[:], in0=src, scalar1=rstd_ap, scalar2=None, op0=ALU.mult),
                 reads=[r_src, r_rstd], writes=[r_xnb])
            for half in range(2):
                b = bank()
                pv = ps[b][:].bitcast(BF16)

                def tr(e, half=half, pv=pv):
                    inst = None
                    for j in range(8):
                        kc = half * 8 + j
                        inst = e.transpose(out=pv[:, j * 128:(j + 1) * 128], in_=xnb[:, kc * 128:(kc + 1) * 128], identity=ident[:])
                    return inst
                P.op("pe", tr, reads=[r_xnb, r_id], writes=[r_ps[b]])
                P.op("dve", lambda e, half=half, pv=pv: e.tensor_tensor(
                    out=dst[:, half * 8:(half + 1) * 8, :], in0=pv.rearrange("p (a b) -> p a b", a=8),
                    in1=gcols[:, gi, half * 8:(half + 1) * 8].unsqueeze(2).broadcast_to([128, 8, 128]), op=ALU.mult),
                    reads=[r_ps[b], r_const], writes=[r_dst])

        def mm_group(b, cols, pairs, reads):
            n = len(pairs)

            def f(e):
                inst = None
                for i, (l, r) in enumerate(pairs):
                    inst = e.matmul(ps[b][:, cols[0]:cols[1]], lhsT=l, rhs=r, start=(i == 0), stop=(i == n - 1))
                return inst
            P.op("pe", f, reads=reads, writes=[r_ps[b]])

        def transposes(b, srcs, reads):
            pv = ps[b][:].bitcast(BF16)

            def f(e):
                inst = None
                for j, s in enumerate(srcs):
                    inst = e.transpose(out=pv[:, j * 128:(j + 1) * 128], in_=s, identity=ident[:])
                return inst
            P.op("pe", f, reads=list(reads) + [r_id], writes=[r_ps[b]])
            return pv

        evac_ctr = [0]

        def copy_any(out, in_, reads, writes, eng=None):
            if eng is None:
                eng = "act" if evac_ctr[0] % 2 == 0 else "dve"
                evac_ctr[0] += 1
            if eng == "act":
                P.op("act", lambda e: e.activation(out=out, in_=in_, func=AF.Copy), reads=reads, writes=writes)
            else:
                P.op("dve", lambda e: e.tensor_copy(out=out, in_=in_), reads=reads, writes=writes)

        def load_x(src_rows, c):
            P.op("sp", lambda e: e.dma_start(out=h_t[:, c, :], in_=src_rows), writes=[r_h[c]], dsem=s_x[c])

        steps = []

        wnames = {}

        def wdesc(w, r0, kcs, c0, cols):
            v = w[r0:r0 + kcs * 128, c0:c0 + cols].rearrange("(k p) c -> p k c", p=128)
            wn = wnames.setdefault(id(w.tensor), f"w{len(wnames)}") if False else None
            return (v, kcs, cols, (w.tensor.name, r0, kcs, c0, cols))

        def step(descs, fn):
            steps.append((descs, fn))

        def mem_setup():
            def pre(slots):
                for mc in range(2):
                    P.op("sp", lambda e, mc=mc: e.dma_start(out=h_t[:, mc, :], in_=mem_in[mc * 128:(mc + 1) * 128, :]),
                         writes=[r_h[mc]], dsem=s_x[mc])
                    norm_T(h_t[:, mc, :], r_h[mc], 3, nT[:, :, mc * 128:(mc + 1) * 128], r_nT[mc])
            step([], pre)

            def kstep(slots):
                (wv, r_w), = slots
                for hd in range(4):
                    b = bank()
                    mm_group(b, (0, 256), [(wv[:, kc, hd * 128:(hd + 1) * 128], nT[:, kc, 0:256]) for kc in range(KC)],
                             [r_w, r_nT[0], r_nT[1]])
                    copy_any(kmT[:, hd, :], ps[b][:, 0:256], [r_ps[b]], [r_km])
            step([wdesc(w_xkv, 0, KC, 0, 512)], kstep)

            def vstep(slots):
                (wv, r_w), = slots
                for mc in range(2):
                    b = bank()
                    mm_group(b, (0, 512), [(nT[:, kc, mc * 128:(mc + 1) * 128], wv[:, kc, :]) for kc in range(KC)],
                             [r_w, r_nT[mc]])
                    copy_any(vm[:, mc, :], ps[b][:], [r_ps[b]], [r_vm])
            step([wdesc(w_xkv, 0, KC, 512, 512)], vstep)

        A_QTM, A_KTM, A_VTM, A_GS = 0, 8192, 16384, 24576
        A_QT, A_QDT, A_KT, A_ST, A_VD = 32768, 34816, 36864, 38912, 39936
        A_YTMP, A_YRTM, A_COSR, A_SINR = 41984, 46080, 48128, 50176
        A_TMPR = 41984

        class RetBufs:
            pass

        def ret_bufs():
            R = RetBufs()
            R.q_tm = carve(A_QTM, [128, CH, 1024], BF16)
            R.k_tm = carve(A_KTM, [128, CH, 1024], BF16)
            R.v_tm = carve(A_VTM, [128, CH, 1024], BF16)
            R.gs = carve(A_GS, [128, CH, 1024], BF16)
            R.qT = carve(A_QT, [128, 8, 128], BF16)
            R.qdT = carve(A_QDT, [128, 8, 128], BF16)
            R.kT = carve(A_KT, [128, 8, 128], BF16)
            R.sT = carve(A_ST, [128, 4, 128], BF16)
            R.vd = carve(A_VD, [128, 4, 256], BF16)
            R.ytmp = carve(A_YTMP, [128, 4, 256], F32)
            R.yr_tm = carve(A_YRTM, [128, 1024], BF16)
            R.cosr = carve(A_COSR, [128, CH, 128], F32)
            R.sinr = carve(A_SINR, [128, CH, 128], F32)
            R.tmps = [carve(A_TMPR + i * 1024, [128, 2, 128], F32) for i in range(4)]
            R.r_q = [AR(f"q_tm{c}") for c in range(CH)]
            R.r_k = [AR(f"k_tm{c}") for c in range(CH)]
            R.r_v = [AR(f"v_tm{c}") for c in range(CH)]
            R.r_g = [AR(f"gs{c}") for c in range(CH)]
            R.r_qT, R.r_qdT, R.r_kT, R.r_sT, R.r_vd = AR("qT"), AR("qdT"), AR("kT"), AR("sT"), AR("vd")
            R.r_ytmp, R.r_yr, R.r_cs = AR("ytmp"), AR("yr_tm"), AR("cosr")
            R.r_tmp = R.r_ytmp
            return R

        def ret_tables(R, pcol0):
            for c in range(CH):
                sincos(R.sinr[:, c, :], R.cosr[:, c, :], invr[:], pcol0 + c, 128, R.r_cs)

        def rope_evac(R, b, c, dst, r_dst, hsl, nh=2, col0=None):
            if col0 is None:
                col0 = hsl * 512
            pv = ps[b][:, 0:nh * 256].rearrange("p (h d) -> p h d", h=nh)
            x1, x2 = pv[:, :, 0:128], pv[:, :, 128:256]
            cb = R.cosr[:, c, :].unsqueeze(1).broadcast_to([128, nh, 128])
            sbb = R.sinr[:, c, :].unsqueeze(1).broadcast_to([128, nh, 128])
            dv = dst[:, c, col0:col0 + nh * 256].rearrange("p (h d) -> p h d", h=nh)
            t = [w[:, 0:nh, :] for w in R.tmps]
            rt = R.r_tmp
            P.op("dve", lambda e: e.tensor_tensor(out=t[0], in0=x1, in1=cb, op=ALU.mult), reads=[r_ps[b], R.r_cs], writes=[rt])
            P.op("dve", lambda e: e.tensor_tensor(out=t[1], in0=x2, in1=sbb, op=ALU.mult), reads=[r_ps[b], R.r_cs], writes=[rt])
            P.op("dve", lambda e: e.tensor_tensor(out=t[2], in0=x2, in1=cb, op=ALU.mult), reads=[r_ps[b], R.r_cs], writes=[rt])
            P.op("dve", lambda e: e.tensor_tensor(out=t[3], in0=x1, in1=sbb, op=ALU.mult), reads=[r_ps[b], R.r_cs], writes=[rt])
            P.op("dve", lambda e: e.tensor_tensor(out=dv[:, :, 0:128], in0=t[0], in1=t[1], op=ALU.subtract), reads=[rt], writes=[r_dst])
            P.op("dve", lambda e: e.tensor_tensor(out=dv[:, :, 128:256], in0=t[2], in1=t[3], op=ALU.add), reads=[rt], writes=[r_dst])

        def proj_tm(wv, r_w, c, kcs=KC, cols=512, src=None, r_src=None):
            b = bank()
            if src is None:
                src, r_src = nT, r_nT[c]
                l = lambda kc: nT[:, kc, c * 128:(c + 1) * 128]
            else:
                l = src
            mm_group(b, (0, cols), [(l(kc), wv[:, kc, 0:cols]) for kc in range(kcs)], [r_w, r_src])
            return b

        proj_tm0 = proj_tm

        def ret_kv_steps(R, heads=(0, 1, 2, 3)):
            def proj_tm(wv, r_w, c, cols=512):
                try:
                    nb = R.nTbuf
                except AttributeError:
                    nb = None
                if nb is None:
                    return proj_tm0(wv, r_w, c, cols=cols)
                return proj_tm0(wv, r_w, c, cols=cols, src=lambda kc: nb[:, kc, c * 128:(c + 1) * 128], r_src=R.r_nTbuf[c])
            if len(heads) == 4:
                for hsl in range(2):
                    def kst(slots, hsl=hsl):
                        (wv, r_w), = slots
                        for c in range(CH):
                            b = proj_tm(wv, r_w, c)
                            rope_evac(R, b, c, R.k_tm, R.r_k[c], hsl)
                    step([wdesc(w_in, 0, KC, O_KR + hsl * 512, 512)], kst)
                for hsl in range(2):
                    def vst(slots, hsl=hsl):
                        (wv, r_w), = slots
                        for c in range(CH):
                            b = proj_tm(wv, r_w, c)
                            copy_any(R.v_tm[:, c, hsl * 512:(hsl + 1) * 512], ps[b][:], [r_ps[b]], [R.r_v[c]])
                    step([wdesc(w_in, 0, KC, O_VR + hsl * 512, 512)], vst)
            else:
                def kst(slots):
                    (wv, r_w), = slots
                    for c in range(CH):
                        b = proj_tm(wv, r_w, c, cols=256)
                        rope_evac(R, b, c, R.k_tm, R.r_k[c], 1, nh=1, col0=768)
                step([wdesc(w_in, 0, KC, O_KR + 768, 256)], kst)

                def vst(slots):
                    (wv, r_w), = slots
                    for c in range(CH):
                        b = proj_tm(wv, r_w, c, cols=256)
                        copy_any(R.v_tm[:, c, 768:1024], ps[b][:, 0:256], [r_ps[b]], [R.r_v[c]])
                step([wdesc(w_in, 0, KC, O_VR + 768, 256)], vst)

        def state_update(R, c, heads=(0, 1, 2, 3)):
            h0 = heads[0]
            nh = len(heads)
            P.op("dve", lambda e: e.tensor_tensor(
                out=R.vd[:, h0:h0 + nh, :], in0=R.v_tm[:, c, h0 * 256:(h0 + nh) * 256].rearrange("p (h d) -> p h d", h=nh),
                in1=kdec[:, h0:h0 + nh].unsqueeze(2).broadcast_to([128, nh, 256]), op=ALU.mult),
                reads=[R.r_v[c], r_const], writes=[R.r_vd])
            for h in heads:
                b = bank()
                for dc in range(2):
                    mm_group(b, (dc * 256, dc * 256 + 256),
                             [(R.k_tm[:, c, h * 256 + dc * 128: h * 256 + dc * 128 + 128], R.vd[:, h, :])],
                             [R.r_k[c], R.r_vd])
                P.op("dve", lambda e, h=h, b=b: e.scalar_tensor_tensor(
                    out=S_t[:, 2 * h:2 * h + 2, :], in0=S_t[:, 2 * h:2 * h + 2, :], scalar=g128[h],
                    in1=ps[b][:].rearrange("p (a d) -> p a d", a=2), op0=ALU.mult, op1=ALU.add),
                    reads=[r_ps[b], r_S[h]], writes=[r_S[h]])
                P.op("act", lambda e, h=h: e.activation(out=Sb_t[:, 2 * h:2 * h + 2, :], in_=S_t[:, 2 * h:2 * h + 2, :], func=AF.Copy),
                     reads=[r_S[h]], writes=[r_Sb[h]])

        PP = {}

        def pp_setup(slots):
            new_phase()
            PP["k_tm"] = carve(0, [128, CH, 1024], BF16)
            PP["v_tm"] = carve(8192, [128, CH, 1024], BF16)
            PP["vd"] = carve(16384, [128, 4, 256], BF16)
            PP["tmps"] = [carve(18432 + i * 1024, [128, 2, 128], F32) for i in range(4)]
            PP["cos"] = [carve(22528 + par * 4096, [128, CH, 128], F32) for par in range(2)]
            PP["sin"] = [carve(22528 + par * 4096 + 2048, [128, CH, 128], F32) for par in range(2)]
            PP["nT2"] = carve(30720, [128, KC, TT], BF16)
            PP["r_k"] = [AR(f"ppk{c}") for c in range(CH)]
            PP["r_v"] = [AR(f"ppv{c}") for c in range(CH)]
            PP["r_vd"], PP["r_tmp"] = AR("ppvd"), AR("pptmp")
            PP["r_cs"] = [AR("ppcs0"), AR("ppcs1")]
            PP["r_nT2"] = [AR(f"ppnT2{c}") for c in range(CH)]

        def pp_bufs(pt):
            par = pt % 2
            R = RetBufs()
            R.k_tm, R.v_tm, R.vd, R.tmps = PP["k_tm"], PP["v_tm"], PP["vd"], PP["tmps"]
            R.cosr, R.sinr = PP["cos"][par], PP["sin"][par]
            R.r_k, R.r_v, R.r_vd, R.r_tmp, R.r_cs = PP["r_k"], PP["r_v"], PP["r_vd"], PP["r_tmp"], PP["r_cs"][par]
            if par == 0:
                R.nTbuf, R.r_nTbuf = nT, r_nT
            else:
                R.nTbuf, R.r_nTbuf = PP["nT2"], PP["r_nT2"]
            return R

        def prepass_all(tiles):
            step([], pp_setup)
            boxes = {pt: {} for pt in tiles}

            def mk_norm(pt):
                def f(slots):
                    R = pp_bufs(pt)
                    boxes[pt]["R"] = R
                    for c in range(CH):
                        load_x(x_prev[(pt * CH + c) * 128:(pt * CH + c + 1) * 128, :], c)
                    for c in range(CH):
                        norm_T(h_t[:, c, :], r_h[c], 0, R.nTbuf[:, :, c * 128:(c + 1) * 128], R.r_nTbuf[c])
                    ret_tables(R, 17 + pt * CH)
                return f

            def mk_post(pt, heads):
                def f(slots):
                    R = boxes[pt]["R"]
                    for c in range(CH):
                        state_update(R, c, heads)
                return f

            for i, pt in enumerate(tiles):
                if i == 0:
                    step([], mk_norm(pt))
                if i + 1 < len(tiles):
                    step([], mk_norm(tiles[i + 1]))
                far = pt < NPRE - NTILE
                heads = (3,) if far else (0, 1, 2, 3)
                ret_kv_steps(RetProxy(boxes[pt]), heads)
                step([], mk_post(pt, heads))

        class RetProxy:
            def __init__(self, box):
                object.__setattr__(self, "_box", box)

            def __getattr__(self, k):
                return getattr(self._box["R"], k)

        B_SC, B_PROBS, B_PT, B_QST = 0, 16384, 24576, 32768
        B_QSTM, B_KSTM, B_TMPS, B_NTH, B_XH = 40960, 43008, 44032, 45056, 0

        def main_tile(t):
            box = {}
            first = (t == 0)

            def swa_pre(slots):
                new_phase()
                W = RetBufs()
                box["W"] = W
                W.sc = carve(B_SC, [128, 16, 256], F32)
                W.probs = carve(B_PROBS, [128, 16, 256], BF16)
                W.pT = carve(B_PT, [128, 16, 2, 128], BF16)
                W.qsT = carve(B_QST, [128, 8, TT], BF16)
                W.qs_tm = [carve(B_QSTM + i * 1024, [128, 512], BF16) for i in range(2)]
                W.ks_tm = carve(B_KSTM, [128, 4, 128], BF16)
                W.tmps = [carve(B_TMPS + i * 256, [128, 8, 8], F32) for i in range(4)]
                W.nTh = carve(B_NTH, [128, KC, 128], BF16)
                W.xh = carve(B_XH, [128, D], F32)
                W.r_sc, W.r_probs, W.r_pT = AR("sc"), AR("probs"), AR("pT")
                W.r_qsT = [AR(f"qsT{c}") for c in range(CH)]
                W.r_qstm = [AR("qstm0"), AR("qstm1")]
                W.r_kstm, W.r_tmps, W.r_nTh = AR("kstm"), AR("tmps"), AR("nTh")
                W.r_xh = W.r_sc
                for c in range(CH):
                    load_x(x_own[(t * CH + c) * 128:(t * CH + c + 1) * 128, :], c)
                if first:
                    P.op("sp", lambda e: e.dma_start(out=W.xh, in_=x_halo), writes=[W.r_xh], dsem=s_misc)
                    norm_T(W.xh, W.r_xh, 0, W.nTh, W.r_nTh)
                else:
                    P.op("dve", lambda e: e.tensor_copy(out=ksT[:, :, 0:128], in_=ksT[:, :, 512:640]),
                         reads=[r_ks[4]], writes=[r_ks[0]])
                    P.op("dve", lambda e: e.tensor_copy(out=vs_t[:, 0, :], in_=vs_t[:, 4, :]),
                         reads=[r_vs[4]], writes=[r_vs[0]])
                for c in range(CH):
                    norm_T(h_t[:, c, :], r_h[c], 0, nT[:, :, c * 128:(c + 1) * 128], r_nT[c])
            step([], swa_pre)

            def rope_s(W, pv, r_pv, nh, ci, dst, r_dst):
                cb = coss[:, ci, :].unsqueeze(1).broadcast_to([128, nh, 8])
                sbb = sins[:, ci, :].unsqueeze(1).broadcast_to([128, nh, 8])
                x1, x2 = pv[:, :, 0:8], pv[:, :, 8:16]
                t = [w[:, 0:nh, :] for w in W.tmps]
                rt = W.r_tmps
                rd = [r_cs, r_pv]
                P.op("dve", lambda e: e.tensor_tensor(out=t[0], in0=x1, in1=cb, op=ALU.mult), reads=rd, writes=[rt])
                P.op("dve", lambda e: e.tensor_tensor(out=t[1], in0=x2, in1=sbb, op=ALU.mult), reads=rd, writes=[rt])
                P.op("dve", lambda e: e.tensor_tensor(out=t[2], in0=x2, in1=cb, op=ALU.mult), reads=rd, writes=[rt])
                P.op("dve", lambda e: e.tensor_tensor(out=t[3], in0=x1, in1=sbb, op=ALU.mult), reads=rd, writes=[rt])
                P.op("dve", lambda e: e.tensor_tensor(out=dst[:, :, 0:8], in0=t[0], in1=t[1], op=ALU.subtract), reads=[rt], writes=[r_dst])
                P.op("dve", lambda e: e.tensor_tensor(out=dst[:, :, 8:16], in0=t[2], in1=t[3], op=ALU.add), reads=[rt], writes=[r_dst])
                P.op("act", lambda e: e.activation(out=dst[:, :, 16:64], in_=pv[:, :, 16:64], func=AF.Copy), reads=[r_pv], writes=[r_dst])

            for s2 in range(2):
                def qs_step(slots, s2=s2):
                    W = box["W"]
                    (wv, r_w), = slots
                    for c in range(CH):
                        b = proj_tm(wv, r_w, c)
                        i = (s2 * CH + c) % 2
                        dst = W.qs_tm[i].rearrange("p (h d) -> p h d", h=8)
                        pv = ps[b][:].rearrange("p (h d) -> p h d", h=8)
                        rope_s(W, pv, r_ps[b], 8, 1 + t * CH + c, dst, W.r_qstm[i])
                        b2 = bank()
                        pvt = transposes(b2, [W.qs_tm[i][:, j * 128:(j + 1) * 128] for j in range(4)], [W.r_qstm[i]])
                        copy_any(W.qsT[:, 4 * s2:4 * s2 + 4, c * 128:(c + 1) * 128],
                                 pvt[:, 0:512].rearrange("p (a b) -> p a b", a=4), [r_ps[b2]], [W.r_qsT[c]])
                step([wdesc(w_in, 0, KC, O_QS + s2 * 512, 512)], qs_step)

            def kv_step(slots):
                W = box["W"]
                (wv, r_w), = slots
                chunks = ([(-1, 0)] if first else []) + [(c, c + 1) for c in range(CH)]
                for c, slot in chunks:
                    if c < 0:
                        b = proj_tm(wv, r_w, 0, src=lambda kc: W.nTh[:, kc, :], r_src=W.r_nTh)
                        ci = 0
                    else:
                        b = proj_tm(wv, r_w, c)
                        ci = 1 + t * CH + c
                    pv = ps[b][:, 0:256].rearrange("p (h d) -> p h d", h=4)
                    dst = W.ks_tm[:, :, 0:64]
                    rope_s(W, pv, r_ps[b], 4, ci, dst, W.r_kstm)
                    P.op("dve", lambda e: e.tensor_copy(out=W.ks_tm[:, :, 64:128], in_=W.ks_tm[:, :, 0:64]),
                         reads=[W.r_kstm], writes=[W.r_kstm])
                    P.op("act", lambda e, b=b, slot=slot: e.activation(out=vs_t[:, slot, :], in_=ps[b][:, 256:512], func=AF.Copy),
                         reads=[r_ps[b]], writes=[r_vs[slot]])
                    b2 = bank()
                    pvt = transposes(b2, [W.ks_tm[:, g, :] for g in range(4)], [W.r_kstm])
                    copy_any(ksT[:, :, slot * 128:(slot + 1) * 128], pvt[:, 0:512].rearrange("p (a b) -> p a b", a=4),
                             [r_ps[b2]], [r_ks[slot]])
            step([wdesc(w_in, 0, KC, O_KS, 512)], kv_step)

            def mi_mask(first_, c_):
                return 1 if (first_ and c_ == 0) else 0

            def swa_attn(slots):
                W = box["W"]
                mx, r_mx = stat("mx", 16)
                sm, r_sm = stat("sm", 16)
                esk, r_es = stat("es", 16)
                rden, r_rden = stat("rden", 16)
                for c in range(CH):
                    mi = 1 if (first and c == 0) else 0
                    for mp in range(4):
                        bAB = (bank(), bank())
                        for mi in range(2):
                            m = 2 * mp + mi
                            for hh in range(2):
                                hq = 2 * m + hh
                                g = hq // 4
                                po = hh * 64
                                mm_group(bAB[hh], (mi * 256, mi * 256 + 256),
                                         [(W.qsT[po:po + 64, m, c * 128:(c + 1) * 128], ksT[po:po + 64, g, c * 128:c * 128 + 256])],
                                         [W.r_qsT[c], r_ks[c], r_ks[c + 1]])
                        for hh in range(2):
                            b = bAB[hh]
                            for mi in range(2):
                                hq = 4 * mp + 2 * mi + hh
                                P.op("dve", lambda e: e.scalar_tensor_tensor(
                                    out=W.sc[:, hq, :], in0=ps[b][:, mi * 256:(mi + 1) * 256], scalar=0.125,
                                    in1=maskb[:, mi_mask(first, c), :], op0=ALU.mult, op1=ALU.add),
                                    reads=[r_ps[b], r_const], writes=[W.r_sc])
                    P.op("dve", lambda e: e.tensor_reduce(out=mx, in_=W.sc, axis=AX.X, op=ALU.max), reads=[W.r_sc], writes=[r_mx])
                    P.op("dve", lambda e: e.tensor_scalar(out=mx, in0=mx, scalar1=-1.0, scalar2=None, op0=ALU.mult), reads=[r_mx], writes=[r_mx])
                    for hq in range(16):
                        P.op("act", lambda e: e.activation(out=W.sc[:, hq, :], in_=W.sc[:, hq, :], func=AF.Exp, bias=mx[:, hq:hq + 1], scale=1.0,
                                                           accum_out=sm[:, hq:hq + 1]), reads=[W.r_sc, r_mx], writes=[W.r_sc, r_sm])
                    P.op("dve", lambda e: e.tensor_tensor(out=esk, in0=sinkb[:], in1=mx, op=ALU.add), reads=[r_mx, r_const], writes=[r_es])
                    P.op("act", lambda e: e.activation(out=esk, in_=esk, func=AF.Exp), reads=[r_es], writes=[r_es])
                    P.op("dve", lambda e: e.tensor_tensor(out=esk, in0=esk, in1=sm, op=ALU.add), reads=[r_es, r_sm], writes=[r_es])
                    P.op("dve", lambda e: e.reciprocal(out=rden, in_=esk), reads=[r_es], writes=[r_rden])
                    P.op("dve", lambda e: e.tensor_tensor(out=W.probs, in0=W.sc, in1=rden.unsqueeze(2).broadcast_to([128, 16, 256]),
                                                           op=ALU.mult), reads=[W.r_sc, r_rden], writes=[W.r_probs])
                    for q4 in range(4):
                        b = bank()
                        srcs = []
                        for hh in range(4):
                            for half in range(2):
                                srcs.append(W.probs[:, 4 * q4 + hh, half * 128:(half + 1) * 128])
                        pvt = transposes(b, srcs, [W.r_probs])
                        copy_any(W.pT[:, 4 * q4:4 * q4 + 4, :, :].rearrange("p a b c -> p (a b c)"), pvt, [r_ps[b]], [W.r_pT])
                    for bb in range(2):
                        b = bank()
                        for mm_ in range(4):
                            m = bb * 4 + mm_
                            for hh in range(2):
                                hq = 2 * m + hh
                                g = hq // 4

                                def f(e, b=b, mm_=mm_, hh=hh, hq=hq, g=g, c=c):
                                    inst = None
                                    for half in range(2):
                                        inst = e.matmul(ps[b][hh * 64:(hh + 1) * 64, mm_ * 128:(mm_ + 1) * 128],
                                                        lhsT=vs_t[:, c + half, g * 64:(g + 1) * 64], rhs=W.pT[:, hq, half, :],
                                                        start=(half == 0), stop=(half == 1))
                                    return inst
                                P.op("pe", f, reads=[W.r_pT, r_vs[c], r_vs[c + 1]], writes=[r_ps[b]])
                        copy_any(ysT[:, 4 * bb:4 * bb + 4, c * 128:(c + 1) * 128], ps[b][:].rearrange("p (a q) -> p a q", a=4),
                                 [r_ps[b]], [r_ysT[c]])
            step([], swa_attn)
            steps.append(("MARK", "swa"))

            def ret_pre(slots):
                new_phase()
                R = ret_bufs()
                box["R"] = R
                ret_tables(R, 1 + t * CH)
            step([], ret_pre)
            Rp = RetProxy(box)
            for hsl in range(2):
                def qst(slots, hsl=hsl):
                    R = box["R"]
                    (wv, r_w), = slots
                    for c in range(CH):
                        b = proj_tm(wv, r_w, c)
                        rope_evac(R, b, c, R.q_tm, R.r_q[c], hsl)
                step([wdesc(w_in, 0, KC, O_QR + hsl * 512, 512)], qst)
            ret_kv_steps(Rp)
            for hsl in range(2):
                def gst(slots, hsl=hsl):
                    R = box["R"]
                    (wv, r_w), = slots
                    for c in range(CH):
                        b = proj_tm(wv, r_w, c)
                        P.op("act", lambda e, b=b, c=c: e.activation(out=R.gs[:, c, hsl * 512:(hsl + 1) * 512], in_=ps[b][:], func=AF.Silu),
                             reads=[r_ps[b]], writes=[R.r_g[c]])
                step([wdesc(w_in, 0, KC, O_GR + hsl * 512, 512)], gst)

            steps.append(("MARK", "retproj"))

            def ret_core(slots):
                R = box["R"]
                ssr, r_ssr = stat("ssr", 4)
                rs4, r_rs4 = stat("rs4", 4)
                for c in range(CH):
                    bq = bank()
                    pvq = transposes(bq, [R.q_tm[:, c, j * 128:(j + 1) * 128] for j in range(8)], [R.r_q[c]])
                    P.op("act", lambda e: e.activation(out=R.qT[:].rearrange("p a b -> p (a b)"), in_=pvq, func=AF.Copy),
                         reads=[r_ps[bq]], writes=[R.r_qT])
                    for dc in range(2):
                        P.op("dve", lambda e: e.tensor_tensor(
                            out=R.qdT[:].rearrange("p (h a) b -> p h a b", h=4)[:, :, dc, :],
                            in0=pvq.rearrange("p (h a b) -> p h a b", h=4, a=2)[:, :, dc, :],
                            in1=qdecT[:], op=ALU.mult),
                            reads=[r_ps[bq], r_const], writes=[R.r_qdT])
                    bk = bank()
                    pvk = transposes(bk, [R.k_tm[:, c, j * 128:(j + 1) * 128] for j in range(8)], [R.r_k[c]])
                    copy_any(R.kT[:].rearrange("p a b -> p (a b)"), pvk, [r_ps[bk]], [R.r_kT])
                    if ret_level < 2:
                        continue
                    bs = bank()
                    for h in range(4):
                        mm_group(bs, (h * 128, h * 128 + 128),
                                 [(R.kT[:, 2 * h + dc, :], R.qT[:, 2 * h + dc, :]) for dc in range(2)], [R.r_kT, R.r_qT])
                    P.op("dve", lambda e: e.tensor_tensor(out=R.sT, in0=ps[bs][:].rearrange("p (h i) -> p h i", h=4), in1=dmT[:], op=ALU.mult),
                         reads=[r_ps[bs], r_const], writes=[R.r_sT])
                    if ret_level < 3:
                        continue
                    bo = [bank(), bank()]
                    for h in range(4):
                        b = bo[h // 2]
                        pairs = [(R.sT[:, h, :], R.v_tm[:, c, h * 256:(h + 1) * 256])]
                        pairs += [(R.qdT[:, 2 * h + dc, :], Sb_t[:, 2 * h + dc, :]) for dc in range(2)]
                        mm_group(b, ((h % 2) * 256, (h % 2) * 256 + 256), pairs, [R.r_sT, R.r_v[c], R.r_qdT, r_Sb[h]])
                    state_update(R, c)
                    if ret_level < 4:
                        continue
                    for h in range(4):
                        b = bo[h // 2]
                        P.op("act", lambda e, h=h, b=b: e.activation(out=R.ytmp[:, h, :], in_=ps[b][:, (h % 2) * 256:(h % 2) * 256 + 256],
                                                                    func=AF.Square, accum_out=ssr[:, h:h + 1]),
                             reads=[r_ps[b]], writes=[R.r_ytmp, r_ssr])
                    rstd_from(ssr, r_ssr, rs4, r_rs4, 4, 1.0 / 256.0)
                    for hb in range(2):
                        b = bo[hb]
                        P.op("dve", lambda e, hb=hb, b=b: e.tensor_tensor(
                            out=R.ytmp[:, 2 * hb:2 * hb + 2, :], in0=ps[b][:].rearrange("p (a d) -> p a d", a=2),
                            in1=rs4[:, 2 * hb:2 * hb + 2].unsqueeze(2).broadcast_to([128, 2, 256]), op=ALU.mult),
                            reads=[r_ps[b], r_rs4], writes=[R.r_ytmp])
                    P.op("dve", lambda e, c=c: e.tensor_tensor(out=R.yr_tm, in0=R.ytmp[:].rearrange("p a b -> p (a b)"), in1=R.gs[:, c, :], op=ALU.mult),
                         reads=[R.r_ytmp, R.r_g[c]], writes=[R.r_yr])
                    by = bank()
                    pvy = transposes(by, [R.yr_tm[:, j * 128:(j + 1) * 128] for j in range(8)], [R.r_yr])
                    copy_any(yrT[:, :, c * 128:(c + 1) * 128], pvy.rearrange("p (a b) -> p a b", a=8), [r_ps[by]], [r_yrT[c]])
            step([], ret_core)
            steps.append(("MARK", "ret"))

            C_T1, C_SG, C_MT = 0, 8192, 12288

            def merge_pre(slots):
                new_phase()
                M = RetBufs()
                box["M"] = M
                M.t1 = carve(C_T1, [128, 4, TT], F32)
                M.sg = [carve(C_SG + i * 2048, [128, TT], F32) for i in range(2)]
                M.mT = carve(C_MT, [128, KC, TT], BF16)
                M.r_t1 = [AR(f"t1_{j}") for j in range(4)]
                M.r_sg = [AR("sg0"), AR("sg1")]
                M.r_mT = [AR(f"mT{k}") for k in range(KC)]
            step([], merge_pre)

            sg_ctr = [0]
            for s in range(4):
                def mr(slots, s=s):
                    M = box["M"]
                    (wg, r_wg), (wu, r_wu) = slots
                    for j in range(4):
                        bg, bu = bank(), bank()
                        mm_group(bg, (0, 512), [(wg[:, kc, j * 128:(j + 1) * 128], nT[:, kc, :]) for kc in range(KC)], [r_wg] + r_nT)
                        mm_group(bu, (0, 512), [(wu[:, kc, j * 128:(j + 1) * 128], yrT[:, kc, :]) for kc in range(8)], [r_wu] + r_yrT)
                        i = sg_ctr[0] % 2
                        sg_ctr[0] += 1
                        P.op("act", lambda e, bg=bg, i=i: e.activation(out=M.sg[i], in_=ps[bg][:], func=AF.Sigmoid),
                             reads=[r_ps[bg]], writes=[M.r_sg[i]])
                        P.op("dve", lambda e, bu=bu, i=i, j=j: e.tensor_tensor(out=M.t1[:, j, :], in0=M.sg[i], in1=ps[bu][:], op=ALU.mult),
                             reads=[r_ps[bu], M.r_sg[i]], writes=[M.r_t1[j]])
                step([wdesc(w_in, 0, KC, O_GTR + s * 512, 512), wdesc(w_up_ret, 0, 8, s * 512, 512)], mr)

                def ms(slots, s=s):
                    M = box["M"]
                    (wg, r_wg), (wu, r_wu) = slots
                    for j in range(4):
                        bg, bu = bank(), bank()
                        mm_group(bg, (0, 512), [(wg[:, kc, j * 128:(j + 1) * 128], nT[:, kc, :]) for kc in range(KC)], [r_wg] + r_nT)
                        mm_group(bu, (0, 512), [(wu[:, kc, j * 128:(j + 1) * 128], ysT[:, kc, :]) for kc in range(8)], [r_wu] + r_ysT)
                        i = sg_ctr[0] % 2
                        sg_ctr[0] += 1
                        P.op("act", lambda e, bg=bg, i=i: e.activation(out=M.sg[i], in_=ps[bg][:], func=AF.Sigmoid),
                             reads=[r_ps[bg]], writes=[M.r_sg[i]])
                        P.op("dve", lambda e, bu=bu, i=i: e.tensor_tensor(out=M.sg[i], in0=M.sg[i], in1=ps[bu][:], op=ALU.mult),
                             reads=[r_ps[bu], M.r_sg[i]], writes=[M.r_sg[i]])
                        P.op("dve", lambda e, i=i, j=j: e.tensor_tensor(out=M.mT[:, 4 * s + j, :], in0=M.sg[i], in1=M.t1[:, j, :], op=ALU.add),
                             reads=[M.r_sg[i], M.r_t1[j]], writes=[M.r_mT[4 * s + j]])
                step([wdesc(w_in, 0, KC, O_GTS + s * 512, 512), wdesc(w_up_swa, 0, 8, s * 512, 512)], ms)

            def add_into_h(b, c, c0, cols=512):
                P.op("dve", lambda e: e.tensor_tensor(out=h_t[:, c, c0:c0 + cols], in0=h_t[:, c, c0:c0 + cols], in1=ps[b][:, 0:cols], op=ALU.add),
                     reads=[r_ps[b], r_h[c]], writes=[r_h[c]])

            for s in range(4):
                def wo_step(slots, s=s):
                    M = box["M"]
                    (wv, r_w), = slots
                    for c in range(CH):
                        b = bank()
                        mm_group(b, (0, 512), [(M.mT[:, kc, c * 128:(c + 1) * 128], wv[:, kc, :]) for kc in range(KC)], [r_w] + M.r_mT)
                        add_into_h(b, c, s * 512)
                step([wdesc(w_o, 0, KC, s * 512, 512)], wo_step)

            steps.append(("MARK", "wo"))
            X_QX, X_SC, X_PX, X_PT, X_OX = 0, 4096, 8192, 10240, 12288

            def x_pre(slots):
                new_phase()
                X = RetBufs()
                box["X"] = X
                X.qxT = carve(X_QX, [128, 4, TT], BF16)
                X.sc = carve(X_SC, [128, 4, 256], F32)
                X.px = carve(X_PX, [128, 4, 256], BF16)
                X.pT = carve(X_PT, [128, 4, 2, 128], BF16)
                X.oxT = carve(X_OX, [128, 4, TT], BF16)
                X.r_qx, X.r_sc, X.r_px, X.r_pT = AR("qxT"), AR("xsc"), AR("px"), AR("xpT")
                X.r_ox = [AR(f"oxT{c}") for c in range(CH)]
                for c in range(CH):
                    norm_T(h_t[:, c, :], r_h[c], 1, nT[:, :, c * 128:(c + 1) * 128], r_nT[c])
            step([], x_pre)

            def xq_step(slots):
                X = box["X"]
                (wv, r_w), = slots
                for hd in range(4):
                    b = bank()
                    mm_group(b, (0, 512), [(wv[:, kc, hd * 128:(hd + 1) * 128], nT[:, kc, :]) for kc in range(KC)], [r_w] + r_nT)
                    P.op("act", lambda e, b=b, hd=hd: e.activation(out=X.qxT[:, hd, :], in_=ps[b][:], func=AF.Copy, scale=128.0 ** -0.5),
                         reads=[r_ps[b]], writes=[X.r_qx])
                mx, r_mx = stat("xmx", 4)
                sm, r_sm = stat("xsm", 4)
                for c in range(CH):
                    bb = [bank(), bank()]
                    for hd in range(4):
                        mm_group(bb[hd // 2], ((hd % 2) * 256, (hd % 2) * 256 + 256),
                                 [(X.qxT[:, hd, c * 128:(c + 1) * 128], kmT[:, hd, :])], [X.r_qx, r_km])
                    for i2 in range(2):
                        copy_any(X.sc[:, 2 * i2:2 * i2 + 2, :], ps[bb[i2]][:].rearrange("p (a k) -> p a k", a=2), [r_ps[bb[i2]]], [X.r_sc])
                    P.op("dve", lambda e: e.tensor_reduce(out=mx, in_=X.sc, axis=AX.X, op=ALU.max), reads=[X.r_sc], writes=[r_mx])
                    P.op("dve", lambda e: e.tensor_tensor(out=X.sc, in0=X.sc, in1=mx.unsqueeze(2).broadcast_to([128, 4, 256]), op=ALU.subtract),
                         reads=[X.r_sc, r_mx], writes=[X.r_sc])
                    P.op("act", lambda e: e.activation(out=X.sc, in_=X.sc, func=AF.Exp), reads=[X.r_sc], writes=[X.r_sc])
                    P.op("dve", lambda e: e.tensor_reduce(out=sm, in_=X.sc, axis=AX.X, op=ALU.add), reads=[X.r_sc], writes=[r_sm])
                    P.op("dve", lambda e: e.reciprocal(out=sm, in_=sm), reads=[r_sm], writes=[r_sm])
                    P.op("dve", lambda e: e.tensor_tensor(out=X.px, in0=X.sc, in1=sm.unsqueeze(2).broadcast_to([128, 4, 256]), op=ALU.mult),
                         reads=[X.r_sc, r_sm], writes=[X.r_px])
                    b = bank()
                    srcs = [X.px[:, hd, half * 128:(half + 1) * 128] for hd in range(4) for half in range(2)]
                    pvt = transposes(b, srcs, [X.r_px])
                    copy_any(X.pT[:].rearrange("p a b c -> p (a b c)"), pvt, [r_ps[b]], [X.r_pT])
                    b = bank()
                    for hd in range(4):
                        mm_group(b, (hd * 128, hd * 128 + 128),
                                 [(vm[:, half, hd * 128:(hd + 1) * 128], X.pT[:, hd, half, :]) for half in range(2)], [r_vm, X.r_pT])
                    copy_any(X.oxT[:, :, c * 128:(c + 1) * 128], ps[b][:].rearrange("p (a q) -> p a q", a=4), [r_ps[b]], [X.r_ox[c]])
            step([wdesc(w_xq, 0, KC, 0, 512)], xq_step)

            def xo_step(slots):
                X = box["X"]
                (wv, r_w), = slots
                for s in range(4):
                    for c in range(CH):
                        b = bank()
                        mm_group(b, (0, 512), [(X.oxT[:, kc, c * 128:(c + 1) * 128], wv[:, kc, s * 512:(s + 1) * 512]) for kc in range(4)],
                                 [r_w, X.r_ox[c]])
                        add_into_h(b, c, s * 512)
            step([wdesc(w_xo, 0, 4, 0, 2048)], xo_step)

            steps.append(("MARK", "xattn"))
            F_ACT, F_SG, F_GF = 0, 45056, 0

            def f_pre(slots):
                new_phase()
                Fb = RetBufs()
                box["F"] = Fb
                Fb.aT = carve(F_ACT, [128, 44, TT], BF16)
                Fb.sg = [carve(F_SG + i * 2048, [128, TT], F32) for i in range(2)]
                Fb.r_aT = [AR(f"aT{k}") for k in range(44)]
                Fb.r_sg = [AR("fsg0"), AR("fsg1")]
                for c in range(CH):
                    norm_T(h_t[:, c, :], r_h[c], 2, nT[:, :, c * 128:(c + 1) * 128], r_nT[c])
            step([], f_pre)

            for s in range(11):
                def fgu(slots, s=s):
                    Fb = box["F"]
                    (wg, r_wg), (wu, r_wu) = slots
                    for j in range(4):
                        bg, bu = bank(), bank()
                        mm_group(bg, (0, 512), [(wg[:, kc, j * 128:(j + 1) * 128], nT[:, kc, :]) for kc in range(KC)], [r_wg] + r_nT)
                        mm_group(bu, (0, 512), [(wu[:, kc, j * 128:(j + 1) * 128], nT[:, kc, :]) for kc in range(KC)], [r_wu] + r_nT)
                        i = sg_ctr[0] % 2
                        sg_ctr[0] += 1
                        P.op("act", lambda e, bg=bg, i=i: e.activation(out=Fb.sg[i], in_=ps[bg][:], func=AF.Silu),
                             reads=[r_ps[bg]], writes=[Fb.r_sg[i]])
                        P.op("dve", lambda e, bu=bu, i=i, j=j: e.tensor_tensor(out=Fb.aT[:, 4 * s + j, :], in0=Fb.sg[i], in1=ps[bu][:], op=ALU.mult),
                             reads=[r_ps[bu], Fb.r_sg[i]], writes=[Fb.r_aT[4 * s + j]])
                step([wdesc(w_fg, 0, KC, s * 512, 512), wdesc(w_fu, 0, KC, s * 512, 512)], fgu)

            kgroups = [(0, 16), (16, 16), (32, 12)]
            for cs in range(4):
                dbanks = {}
                for gi, (k0, kn) in enumerate(kgroups):
                    def fd(slots, cs=cs, gi=gi, k0=k0, kn=kn, dbanks=dbanks):
                        Fb = box["F"]
                        (wv, r_w), = slots
                        if gi == 0:
                            for c in range(CH):
                                dbanks[c] = bank()
                        for c in range(CH):
                            b = dbanks[c]

                            def f(e, b=b, c=c):
                                inst = None
                                for kk in range(kn):
                                    kc = k0 + kk
                                    inst = e.matmul(ps[b][:], lhsT=Fb.aT[:, kc, c * 128:(c + 1) * 128], rhs=wv[:, kk, :],
                                                    start=(kc == 0), stop=(kc == 43))
                                return inst
                            P.op("pe", f, reads=[r_w] + Fb.r_aT[k0:k0 + kn], writes=[r_ps[b]])
                            if gi == 2:
                                add_into_h(b, c, cs * 512)
                    step([wdesc(w_fd, k0 * 128, kn, cs * 512, 512)], fd)

            steps.append(("MARK", "ffn"))
            def fin(slots):
                new_phase()
                gf = carve(F_GF, [128, D], F32)
                r_gf = AR("gfin")
                P.op("sp", lambda e: e.dma_start(out=gf, in_=gfin_in), writes=[r_gf], dsem=s_misc)
                for c in range(CH):
                    P.op("act", lambda e, c=c: e.activation(out=xnb[:], in_=h_t[:, c, :], func=AF.Square, accum_out=ss_ap),
                         reads=[r_h[c]], writes=[r_xnb, r_ss])
                    rstd_from(ss_ap, r_ss, rstd_ap, r_rstd, 1, 1.0 / D)
                    P.op("dve", lambda e, c=c: e.scalar_tensor_tensor(out=h_t[:, c, :], in0=h_t[:, c, :], scalar=rstd_ap, in1=gf,
                                                                      op0=ALU.mult, op1=ALU.mult),
                         reads=[r_h[c], r_rstd, r_gf], writes=[r_h[c]])
                    P.op("sp", lambda e, c=c: e.dma_start(out=y_out[(t * CH + c) * 128:(t * CH + c + 1) * 128, :], in_=h_t[:, c, :]),
                         reads=[r_h[c]], dsem=s_y[c])
            step([], fin)

        def zero_state(slots):
            for h in range(4):
                P.op("dve", lambda e, h=h: e.memset(S_t[:, 2 * h:2 * h + 2, :], 0.0), writes=[r_S[h]])
                P.op("dve", lambda e, h=h: e.memset(Sb_t[:, 2 * h:2 * h + 2, :], 0.0), writes=[r_Sb[h]])
        step([], zero_state)
        mem_setup()
        if n_pre > 0:
            prepass_all(list(range(NPRE - n_pre, NPRE)))
        for t in range(n_main_tiles):
            main_tile(t)

        if stop_after is not None:
            cut = [i for i, st_ in enumerate(steps) if st_ == ("MARK", stop_after)][0]
            del steps[cut:]
        steps[:] = [st_ for st_ in steps if st_[0] != "MARK"]
        all_descs = []
        for descs, fn in steps:
            for d in descs:
                all_descs.append(d)
        issued = [0]
        consumed = [0]

        scratch = {}
        s_ws = [P.dma_sem(f"s_ws{i}") for i in range(NSLOT)]

        def issue(i):
            v, kcs, cols, key = all_descs[i]
            slot = i % NSLOT
            dst = wview(slot, kcs, cols)
            n = kcs * cols
            if key not in scratch:
                P.op("pool", lambda e: e.dma_start(out=dst, in_=v), writes=[r_wr[slot]], dsem=s_wr[slot])
                if USE_SCRATCH:
                    scr = nc.dram_tensor(f"scr{len(scratch)}", [128, n], BF16, kind="Internal").ap()
                    r_scr = Res(f"scr{len(scratch)}")
                    scratch[key] = (scr, r_scr)
                    flat = wr[slot][:].bitcast(BF16)[:, 0:n]
                    P.op("sp", lambda e: e.dma_start(out=scr, in_=flat), reads=[r_wr[slot]], writes=[r_scr], dsem=s_ws[slot])
            else:
                scr, r_scr = scratch[key]
                flat = wr[slot][:].bitcast(BF16)[:, 0:n]
                P.op("pool", lambda e: e.dma_start(out=flat, in_=scr), reads=[r_scr], writes=[r_wr[slot]], dsem=s_wr[slot])

        total = len(all_descs)
        for descs, fn in steps:
            need = len(descs)
            assert need <= NSLOT
            while issued[0] < consumed[0] + need:
                issue(issued[0])
                issued[0] += 1
            while issued[0] < total and issued[0] - consumed[0] < NSLOT:
                issue(issued[0])
                issued[0] += 1
            slots = []
            for k in range(need):
                i = consumed[0] + k
                v, kcs, cols, key = all_descs[i]
                slots.append((wview(i % NSLOT, kcs, cols), r_wr[i % NSLOT]))
            fn(slots)
            consumed[0] += need

        finals = [(s.idx, s.val) for s in s_y if s.val]
        P.run(finals)
        build.n_ops = P.n_ops
    return nc


def _consts(core):
    f32 = np.float32
    half = 128
    invr = (f32(1.0) / (f32(10000.0) ** (np.arange(half, dtype=f32) / f32(half)))).astype(f32)
    invs = (f32(1.0) / (f32(500000.0) ** (np.arange(8, dtype=f32) / f32(8)))).astype(f32)
    qi = np.arange(128)[:, None]
    kj = np.arange(256)[None, :]
    valid = (kj >= qi + 1) & (kj <= qi + 128)
    m0 = np.where(valid, 0.0, -1e30).astype(f32)
    m1 = m0.copy()
    if core == 0:
        m1[:, :128] = -1e30
    maskb = np.concatenate([m0, m1], axis=1)
    log_g = np.log(1.0 - np.power(2.0, -5.0 - np.arange(4, dtype=np.float64)))
    j = np.arange(128)[:, None]
    i = np.arange(128)[None, :]
    dmT = np.zeros((128, 4, 128), f32)
    qdecT = np.zeros((128, 4, 128), f32)
    kdec = np.zeros((128, 4), f32)
    for h in range(4):
        dmT[:, h, :] = np.where(i >= j, np.exp(log_g[h] * np.maximum(i - j, 0)), 0.0) / 16.0
        qdecT[:, h, :] = np.exp(log_g[h] * (np.arange(128) + 1.0))[None, :]
        kdec[:, h] = np.exp(log_g[h] * (127.0 - np.arange(128))) / 16.0
    return {
        "invr": np.ascontiguousarray(np.broadcast_to(invr[None, :], (128, 128))),
        "invs": np.ascontiguousarray(np.broadcast_to(invs[None, :], (128, 8))),
        "maskb": np.ascontiguousarray(maskb),
        "dmT": np.ascontiguousarray(dmT.reshape(128, 512)),
        "qdecT": np.ascontiguousarray(qdecT.reshape(128, 512)),
        "kdec": kdec,
    }


def _make_in_maps(x, mem, positions, g_mix, w_in, w_up_ret, w_up_swa, sinks, w_o, g_x, g_mem,
                  w_xq, w_xkv, w_xo, g_ffn, w_ffn_gate, w_ffn_up, w_ffn_down, g_final):
    c = np.ascontiguousarray
    x2 = np.asarray(x, np.float32).reshape(SEQ, D)
    pos = np.asarray(positions).reshape(SEQ).astype(np.int32)
    shared = {
        "mem": c(np.asarray(mem, np.float32).reshape(256, D)),
        "w_in": c(np.asarray(w_in, np.float32)[0]),
        "w_up_ret": c(np.asarray(w_up_ret, np.float32)[0]),
        "w_up_swa": c(np.asarray(w_up_swa, np.float32)[0]),
        "w_o": c(np.asarray(w_o, np.float32)[0]),
        "w_xq": c(np.asarray(w_xq, np.float32)[0]),
        "w_xkv": c(np.asarray(w_xkv, np.float32)[0]),
        "w_xo": c(np.asarray(w_xo, np.float32)[0]),
        "w_fg": c(np.asarray(w_ffn_gate, np.float32)[0]),
        "w_fu": c(np.asarray(w_ffn_up, np.float32)[0]),
        "w_fd": c(np.asarray(w_ffn_down, np.float32)[0]),
        "gfin": c(np.broadcast_to(np.asarray(g_final, np.float32).reshape(1, D), (128, D))),
        "sinkb": c(np.broadcast_to(np.asarray(sinks, np.float32).reshape(1, 16), (128, 16))),
    }
    gs = [np.asarray(g, np.float32).reshape(D) for g in (g_mix, g_x, g_ffn, g_mem)]
    shared["gcols"] = c(np.stack([g.reshape(16, 128).T for g in gs], axis=1).reshape(128, 64))
    in_maps = []
    npre_tok = NPRE * TT
    for core in range(NCORE):
        t0 = core * TOK
        xp = np.zeros((npre_tok + 128, D), np.float32)
        pp = np.zeros((npre_tok,), np.int32)
        lo = t0 - npre_tok
        if lo < 0:
            n = t0
            if n > 0:
                xp[npre_tok - n:npre_tok] = x2[0:t0]
                pp[npre_tok - n:] = pos[0:t0]
        else:
            xp[:npre_tok] = x2[lo:t0]
            pp[:] = pos[lo:t0]
        x_prev = c(xp[:npre_tok])
        x_halo = c(x_prev[npre_tok - 128:npre_tok])
        pos_halo = pp[npre_tok - 128:]
        pos_all = np.concatenate([pos_halo, pos[t0:t0 + TOK], pp])
        pos_dev = c(pos_all.reshape(17 + NPRE * CH, 128).T)
        m = dict(shared)
        m.update(_consts(core))
        m["x_own"] = c(x2[t0:t0 + TOK])
        m["x_halo"] = x_halo
        m["x_prev"] = x_prev
        m["pos"] = pos_dev
        in_maps.append(m)
    return in_maps


_NC_CACHE = {}


def kernel(**inputs):
    in_maps = _make_in_maps(**inputs)
    if "nc" not in _NC_CACHE:
        _NC_CACHE["nc"] = build()
    nc = _NC_CACHE["nc"]
    res = run_bass_kernel_spmd(nc, in_maps, core_ids=list(range(NCORE)))
    out = np.concatenate([np.asarray(r["y"], np.float32) for r in res.results], axis=0)
    return out.reshape(1, SEQ, D)
```

```python
import math
from contextlib import ExitStack

import numpy as np
import concourse.bass as bass
import concourse.mybir as mybir
from concourse.bass_utils import run_bass_kernel_spmd

F32 = mybir.dt.float32
BF16 = mybir.dt.bfloat16
I32 = mybir.dt.int32
AF = mybir.ActivationFunctionType
ALU = mybir.AluOpType
AX = mybir.AxisListType

D = 2048
KC = 16
SEQ = 16384
NCORE = 8
TOK = SEQ // NCORE
TT = 512
CH = 4
NTILE = TOK // TT
NPRE = 8
DFF = 5632
EPS = 1e-6
NSLOT = 3
USE_SCRATCH = True
TWO_PI = 2.0 * math.pi

O_QR, O_KR, O_VR, O_GR, O_QS, O_KS, O_VS, O_GTR, O_GTS = 0, 1024, 2048, 3072, 4096, 5120, 5376, 5632, 7680


class Res:
    __slots__ = ("name", "w", "r")

    def __init__(self, name, r=None):
        self.name = name
        self.w = None
        self.r = list(r) if r else []


class DmaSem:
    def __init__(self, idx):
        self.idx = idx
        self.val = 0


class _Rec:
    def __init__(self):
        self.calls = []

    def __getattr__(self, name):
        def m(*a, **k):
            self.calls.append((name, a, k))
            return self
        return m


class Prog:
    ENGS = ("pe", "act", "dve", "pool", "sp")

    def __init__(self, nc, es):
        self.nc = nc
        self.es = es
        self.sems = []
        self.q = {e: [] for e in self.ENGS}
        self.cnt = {e: 0 for e in self.ENGS}
        self.waited = {e: {} for e in self.ENGS}
        self.esem = {}
        for e in self.ENGS:
            self.esem[e] = self.new_sem("s_" + e)
        self.n_ops = 0

    def new_sem(self, name):
        h = self.es.enter_context(self.nc.semaphore(name))
        self.sems.append(h)
        return len(self.sems) - 1

    def dma_sem(self, name):
        return DmaSem(self.new_sem(name))

    def op(self, eng, fn, reads=(), writes=(), dsem=None):
        waits = {}
        wd = self.waited[eng]
        own = self.esem[eng]

        def need(ev):
            if ev is None:
                return
            s, v = ev
            if s == own and eng == "pe":
                return
            if wd.get(s, 0) >= v:
                return
            if waits.get(s, 0) < v:
                waits[s] = v

        for r in reads:
            need(r.w)
            if r.name.startswith("ps") and eng != "pe":
                for ev in r.r:
                    if ev[0] != own:
                        need(ev)
        for w in writes:
            need(w.w)
            for ev in w.r:
                need(ev)
        if dsem is not None and dsem.val > 0:
            need((dsem.idx, dsem.val))
        for s, v in waits.items():
            wd[s] = v
        if dsem is None:
            self.cnt[eng] += 1
            ev = (self.esem[eng], self.cnt[eng])
            inc = (self.esem[eng], 1)
        else:
            dsem.val += 16
            ev = (dsem.idx, dsem.val)
            inc = (dsem.idx, 16)
        rec = _Rec()
        fn(rec)
        self.q[eng].append((list(waits.items()), rec.calls, inc))
        for r in reads:
            r.r.append(ev)
            if len(r.r) > 24:
                m = {}
                for s, v in r.r:
                    if m.get(s, 0) < v:
                        m[s] = v
                r.r = list(m.items())
        for w in writes:
            w.w = ev
            w.r = []
        self.n_ops += 1
        return ev

    def replay(self, eng, e):
        for waits, calls, inc in self.q[eng]:
            for s, v in waits:
                e.wait_ge(self.sems[s], v)
            inst = None
            for name, a, k in calls:
                inst = getattr(e, name)(*a, **k)
            inst.then_inc(self.sems[inc[0]], inc[1])

    def run(self, final_events):
        nc = self.nc
        with nc.Block() as block:
            @block.tensor
            def _(e):
                self.replay("pe", e)

            @block.scalar
            def _(e):
                self.replay("act", e)

            @block.vector
            def _(e):
                self.replay("dve", e)

            @block.gpsimd
            def _(e):
                self.replay("pool", e)

            @block.sync
            def _(e):
                self.replay("sp", e)
                for s, v in final_events:
                    e.wait_ge(self.sems[s], v)


def build(n_main_tiles=NTILE, n_pre=NPRE, dbg=None, stop_after=None, ret_level=9):
    nc = bass.Bass("TRN2", target_bir_lowering=False)

    def din(n, s, d=F32):
        return nc.dram_tensor(n, s, d, kind="ExternalInput").ap()

    x_own = din("x_own", [TOK, D])
    x_halo = din("x_halo", [128, D])
    x_prev = din("x_prev", [NPRE * TT, D])
    pos_in = din("pos", [128, 17 + NPRE * CH], I32)
    mem_in = din("mem", [256, D])
    w_in = din("w_in", [D, 9728])
    w_up_ret = din("w_up_ret", [1024, D])
    w_up_swa = din("w_up_swa", [1024, D])
    w_o = din("w_o", [D, D])
    w_xq = din("w_xq", [D, 512])
    w_xkv = din("w_xkv", [D, 1024])
    w_xo = din("w_xo", [512, D])
    w_fg = din("w_fg", [D, DFF])
    w_fu = din("w_fu", [D, DFF])
    w_fd = din("w_fd", [DFF, D])
    gcols_in = din("gcols", [128, 4 * 16])
    gfin_in = din("gfin", [128, D])
    sinkb_in = din("sinkb", [128, 16])
    invr_in = din("invr", [128, 128])
    invs_in = din("invs", [128, 8])
    maskb_in = din("maskb", [128, 2 * 256])
    dmT_in = din("dmT", [128, 4 * 128])
    qdecT_in = din("qdecT", [128, 4 * 128])
    kdec_in = din("kdec", [128, 4])
    y_out = nc.dram_tensor("y", [TOK, D], F32, kind="ExternalOutput").ap()
    dbg_outs = {}

    g128 = [float(np.exp(np.log(1.0 - 2.0 ** (-5.0 - h)) * 128.0)) for h in range(4)]

    es = ExitStack()
    with es:
        P = Prog(nc, es)

        def sb(n, s, d):
            return es.enter_context(nc.sbuf_tensor("sb_" + n, s, d))

        h_t = sb("h", [128, CH, D], F32)
        xnb = sb("xnb", [128, D], BF16)
        nT = sb("nT", [128, KC, TT], BF16)
        wr = [sb(f"wr{i}", [128, 4096], F32) for i in range(NSLOT)]
        ysT = sb("ysT", [128, 8, TT], BF16)
        yrT = sb("yrT", [128, 8, TT], BF16)
        S_t = sb("S", [128, 8, 256], F32)
        Sb_t = sb("Sb", [128, 8, 256], BF16)
        ksT = sb("ksT", [128, 4, 640], BF16)
        vs_t = sb("vs", [128, 5, 256], BF16)
        maskb = sb("maskb", [128, 2, 256], F32)
        dmT = sb("dmT", [128, 4, 128], F32)
        qdecT = sb("qdecT", [128, 4, 128], F32)
        kdec = sb("kdec", [128, 4], F32)
        invr = sb("invr", [128, 128], F32)
        invs = sb("invs", [128, 8], F32)
        sinkb = sb("sinkb", [128, 16], F32)
        gcols = sb("gcols", [128, 4, 16], F32)
        ident = sb("ident", [128, 128], BF16)
        identf = sb("identf", [128, 128], F32)
        kmT = sb("kmT", [128, 4, 256], BF16)
        vm = sb("vm", [128, 2, 512], BF16)
        posi = sb("posi", [128, 17 + NPRE * CH], I32)
        posf = sb("posf", [128, 17 + NPRE * CH], F32)
        coss = sb("coss", [128, 17, 8], F32)
        sins = sb("sins", [128, 17, 8], F32)
        st = sb("st", [128, 160], F32)
        cst = sb("cst", [128, 4], F32)
        ARENA_F32 = 13440
        arena = sb("arena", [128, ARENA_F32], F32)
        ps = [es.enter_context(nc.psum_tensor(f"ps{i}", [128, 512], F32)) for i in range(8)]

        r_h = [Res(f"h{c}") for c in range(CH)]
        r_xnb = Res("xnb")
        r_nT = [Res(f"nT{c}") for c in range(CH)]
        r_wr = [Res(f"wr{i}") for i in range(NSLOT)]
        s_wr = [P.dma_sem(f"s_wr{i}") for i in range(NSLOT)]
        r_ps = [Res(f"ps{i}") for i in range(8)]
        r_ysT = [Res(f"ysT{c}") for c in range(CH)]
        r_yrT = [Res(f"yrT{c}") for c in range(CH)]
        r_S = [Res(f"S{h}") for h in range(4)]
        r_Sb = [Res(f"Sb{h}") for h in range(4)]
        r_ks = [Res(f"ks{s}") for s in range(5)]
        r_vs = [Res(f"vs{s}") for s in range(5)]
        r_const = Res("const")
        r_id = Res("ident")
        r_km = Res("kmT")
        r_vm = Res("vm")
        r_pos = Res("pos")
        r_st = {}
        s_ld = P.dma_sem("s_ld")
        s_x = [P.dma_sem(f"s_x{c}") for c in range(CH)]
        s_y = [P.dma_sem(f"s_y{c}") for c in range(CH)]
        s_misc = P.dma_sem("s_misc")

        bank_ctr = [0]

        def bank():
            b = bank_ctr[0] % 8
            bank_ctr[0] += 1
            return b

        arena_res = []
        fence = [[]]

        def AR(name):
            r = Res(name, fence[0])
            arena_res.append(r)
            return r

        def new_phase():
            m = {}
            for s, v in fence[0]:
                m[s] = max(m.get(s, 0), v)
            for r in arena_res:
                evs = list(r.r)
                if r.w is not None:
                    evs.append(r.w)
                for s, v in evs:
                    if m.get(s, 0) < v:
                        m[s] = v
            fence[0] = list(m.items())
            arena_res.clear()

        def carve(off_bytes, shape, dt):
            n = 1
            for s in shape[1:]:
                n *= s
            if dt == F32:
                a = arena[:, off_bytes // 4: off_bytes // 4 + n]
            else:
                nf = (n + 1) // 2
                a = arena[:, off_bytes // 4: off_bytes // 4 + nf].bitcast(BF16)[:, 0:n]
            if len(shape) == 2:
                return a
            if len(shape) == 3:
                return a.rearrange("p (a b) -> p a b", a=shape[1])
            if len(shape) == 4:
                return a.rearrange("p (a b c) -> p a b c", a=shape[1], b=shape[2])
            raise ValueError

        def stat(name, n):
            if name not in r_st:
                off = sum(v[1] for v in r_st.values())
                assert off + n <= 160
                r_st[name] = (off, n, Res("st_" + name))
            off, n0, r = r_st[name]
            return st[:, off:off + n0], r

        def wview(slot, kcs, cols):
            return wr[slot][:].bitcast(BF16)[:, 0:kcs * cols].rearrange("p (k c) -> p k c", k=kcs)

        def ld(dst_ap, src_ap, res, q="sp"):
            P.op(q, lambda e: e.dma_start(out=dst_ap, in_=src_ap), writes=[res], dsem=s_ld)

        ld(maskb[:].rearrange("p a b -> p (a b)"), maskb_in, r_const)
        ld(dmT[:].rearrange("p a b -> p (a b)"), dmT_in, r_const)
        ld(qdecT[:].rearrange("p a b -> p (a b)"), qdecT_in, r_const)
        ld(kdec[:], kdec_in, r_const)
        ld(invr[:], invr_in, r_const)
        ld(invs[:], invs_in, r_const)
        ld(sinkb[:], sinkb_in, r_const)
        ld(gcols[:].rearrange("p a b -> p (a b)"), gcols_in, r_const)
        ld(posi[:], pos_in, r_pos)
        P.op("pool", lambda e: e.memset(identf[:], 1.0), writes=[r_id])
        P.op("pool", lambda e: e.affine_select(out=identf[:], in_=identf[:], pattern=[[-1, 128]],
                                               compare_op=ALU.is_equal, fill=0.0, base=0, channel_multiplier=1),
             reads=[r_id], writes=[r_id])
        P.op("dve", lambda e: e.tensor_copy(out=ident[:], in_=identf[:]), reads=[r_id], writes=[r_id])
        P.op("dve", lambda e: e.memset(cst[:, 0:1], math.pi), writes=[r_const])
        P.op("dve", lambda e: e.memset(cst[:, 1:2], EPS), writes=[r_const])
        P.op("dve", lambda e: e.tensor_copy(out=posf[:], in_=posi[:]), reads=[r_pos], writes=[r_pos])

        tg_ang = sb("tg_ang", [128, 128], F32)
        tg_r = sb("tg_r", [128, 128], F32)
        tg_kf = sb("tg_kf", [128, 128], F32)
        tg_ki = sb("tg_ki", [128, 128], I32)
        r_tg = Res("tg")
        CW1 = 6.28125
        CW2 = TWO_PI - 6.28125

        def sincos(dst_sin, dst_cos, inv_ap, pcol, n, r_dst, tmp=None, r_tmp=None):
            ang, rr, kf, ki = tg_ang[:, 0:n], tg_r[:, 0:n], tg_kf[:, 0:n], tg_ki[:, 0:n]
            P.op("dve", lambda e: e.tensor_scalar(out=ang, in0=inv_ap, scalar1=posf[:, pcol:pcol + 1], scalar2=None,
                                                  op0=ALU.mult), reads=[r_pos, r_const], writes=[r_tg])
            for which, dst in ((0, dst_sin), (1, dst_cos)):
                if which == 1:
                    P.op("dve", lambda e: e.tensor_scalar(out=ang, in0=ang, scalar1=0.5 * math.pi, scalar2=None, op0=ALU.add),
                         reads=[r_tg], writes=[r_tg])
                P.op("dve", lambda e: e.tensor_scalar(out=ki, in0=ang, scalar1=1.0 / TWO_PI, scalar2=None, op0=ALU.mult),
                     reads=[r_tg], writes=[r_tg])
                P.op("dve", lambda e: e.tensor_copy(out=kf, in_=ki), reads=[r_tg], writes=[r_tg])
                P.op("dve", lambda e: e.scalar_tensor_tensor(out=rr, in0=kf, scalar=-CW1, in1=ang, op0=ALU.mult, op1=ALU.add),
                     reads=[r_tg], writes=[r_tg])
                P.op("dve", lambda e: e.scalar_tensor_tensor(out=rr, in0=kf, scalar=-CW2, in1=rr, op0=ALU.mult, op1=ALU.add),
                     reads=[r_tg], writes=[r_tg])
                P.op("dve", lambda e: e.tensor_scalar(out=kf, in0=rr, scalar1=math.pi, scalar2=-TWO_PI, op0=ALU.is_gt, op1=ALU.mult),
                     reads=[r_tg], writes=[r_tg])
                P.op("dve", lambda e: e.tensor_tensor(out=rr, in0=rr, in1=kf, op=ALU.add), reads=[r_tg], writes=[r_tg])
                P.op("dve", lambda e: e.tensor_scalar(out=kf, in0=rr, scalar1=-math.pi, scalar2=TWO_PI, op0=ALU.is_lt, op1=ALU.mult),
                     reads=[r_tg], writes=[r_tg])
                P.op("dve", lambda e: e.tensor_tensor(out=rr, in0=rr, in1=kf, op=ALU.add), reads=[r_tg], writes=[r_tg])
                P.op("act", lambda e, dst=dst: e.activation(out=dst, in_=rr, func=AF.Sin), reads=[r_tg], writes=[r_dst, r_tg])

        r_cs = Res("coss")
        tmp8, r_tmp8 = stat("tmp8", 8)
        for ci in range(17):
            sincos(sins[:, ci, :], coss[:, ci, :], invs[:], ci, 8, r_cs)

        ss_ap, r_ss = stat("ss", 1)
        rstd_ap, r_rstd = stat("rstd", 1)

        def rstd_from(ss, r_s, out, r_o, n, inv_n):
            P.op("act", lambda e: e.activation(out=out, in_=ss, func=AF.Sqrt, bias=cst[:, 1:2], scale=inv_n),
                 reads=[r_s, r_const], writes=[r_o])
            P.op("dve", lambda e: e.reciprocal(out=out, in_=out), reads=[r_o], writes=[r_o])

        def norm_T(src, r_src, gi, dst, r_dst):
            P.op("act", lambda e: e.activation(out=xnb[:], in_=src, func=AF.Square, accum_out=ss_ap),
                 reads=[r_src], writes=[r_xnb, r_ss])
            rstd_from(ss_ap, r_ss, rstd_ap, r_rstd, 1, 1.0 / D)
            P.op("dve", lambda e: e.tensor_scalar(out=xnb[:], in0=src, scalar1=rstd_ap, scalar2=None, op0=ALU.mult),
                 reads=[r_src, r_rstd], writes=[r_xnb])
            for half in range(2):
                b = bank()
                pv = ps[b][:].bitcast(BF16)

                def tr(e, half=half, pv=pv):
                    inst = None
                    for j in range(8):
                        kc = half * 8 + j
                        inst = e.transpose(out=pv[:, j * 128:(j + 1) * 128], in_=xnb[:, kc * 128:(kc + 1) * 128], identity=ident[:])
                    return inst
                P.op("pe", tr, reads=[r_xnb, r_id], writes=[r_ps[b]])
                P.op("dve", lambda e, half=half, pv=pv: e.tensor_tensor(
                    out=dst[:, half * 8:(half + 1) * 8, :], in0=pv.rearrange("p (a b) -> p a b", a=8),
                    in1=gcols[:, gi, half * 8:(half + 1) * 8].unsqueeze(2).broadcast_to([128, 8, 128]), op=ALU.mult),
                    reads=[r_ps[b], r_const], writes=[r_dst])

        def mm_group(b, cols, pairs, reads):
            n = len(pairs)

            def f(e):
                inst = None
                for i, (l, r) in enumerate(pairs):
                    inst = e.matmul(ps[b][:, cols[0]:cols[1]], lhsT=l, rhs=r, start=(i == 0), stop=(i == n - 1))
                return inst
            P.op("pe", f, reads=reads, writes=[r_ps[b]])

        def transposes(b, srcs, reads):
            pv = ps[b][:].bitcast(BF16)

            def f(e):
                inst = None
                for j, s in enumerate(srcs):
                    inst = e.transpose(out=pv[:, j * 128:(j + 1) * 128], in_=s, identity=ident[:])
                return inst
            P.op("pe", f, reads=list(reads) + [r_id], writes=[r_ps[b]])
            return pv

        evac_ctr = [0]

        def copy_any(out, in_, reads, writes, eng=None):
            if eng is None:
                eng = "act" if evac_ctr[0] % 2 == 0 else "dve"
                evac_ctr[0] += 1
            if eng == "act":
                P.op("act", lambda e: e.activation(out=out, in_=in_, func=AF.Copy), reads=reads, writes=writes)
            else:
                P.op("dve", lambda e: e.tensor_copy(out=out, in_=in_), reads=reads, writes=writes)

        def load_x(src_rows, c):
            P.op("sp", lambda e: e.dma_start(out=h_t[:, c, :], in_=src_rows), writes=[r_h[c]], dsem=s_x[c])

        steps = []

        wnames = {}

        def wdesc(w, r0, kcs, c0, cols):
            v = w[r0:r0 + kcs * 128, c0:c0 + cols].rearrange("(k p) c -> p k c", p=128)
            wn = wnames.setdefault(id(w.tensor), f"w{len(wnames)}") if False else None
            return (v, kcs, cols, (w.tensor.name, r0, kcs, c0, cols))

        def step(descs, fn):
            steps.append((descs, fn))

        def mem_setup():
            def pre(slots):
                for mc in range(2):
                    P.op("sp", lambda e, mc=mc: e.dma_start(out=h_t[:, mc, :], in_=mem_in[mc * 128:(mc + 1) * 128, :]),
                         writes=[r_h[mc]], dsem=s_x[mc])
                    norm_T(h_t[:, mc, :], r_h[mc], 3, nT[:, :, mc * 128:(mc + 1) * 128], r_nT[mc])
            step([], pre)

            def kstep(slots):
                (wv, r_w), = slots
                for hd in range(4):
                    b = bank()
                    mm_group(b, (0, 256), [(wv[:, kc, hd * 128:(hd + 1) * 128], nT[:, kc, 0:256]) for kc in range(KC)],
                             [r_w, r_nT[0], r_nT[1]])
                    copy_any(kmT[:, hd, :], ps[b][:, 0:256], [r_ps[b]], [r_km])
            step([wdesc(w_xkv, 0, KC, 0, 512)], kstep)

            def vstep(slots):
                (wv, r_w), = slots
                for mc in range(2):
                    b = bank()
                    mm_group(b, (0, 512), [(nT[:, kc, mc * 128:(mc + 1) * 128], wv[:, kc, :]) for kc in range(KC)],
                             [r_w, r_nT[mc]])
                    copy_any(vm[:, mc, :], ps[b][:], [r_ps[b]], [r_vm])
            step([wdesc(w_xkv, 0, KC, 512, 512)], vstep)

        A_QTM, A_KTM, A_VTM, A_GS = 0, 8192, 16384, 24576
        A_QT, A_QDT, A_KT, A_ST, A_VD = 32768, 34816, 36864, 38912, 39936
        A_YTMP, A_YRTM, A_COSR, A_SINR = 41984, 46080, 48128, 50176
        A_TMPR = 41984

        class RetBufs:
            pass

        def ret_bufs():
            R = RetBufs()
            R.q_tm = carve(A_QTM, [128, CH, 1024], BF16)
            R.k_tm = carve(A_KTM, [128, CH, 1024], BF16)
            R.v_tm = carve(A_VTM, [128, CH, 1024], BF16)
            R.gs = carve(A_GS, [128, CH, 1024], BF16)
            R.qT = carve(A_QT, [128, 8, 128], BF16)
            R.qdT = carve(A_QDT, [128, 8, 128], BF16)
            R.kT = carve(A_KT, [128, 8, 128], BF16)
            R.sT = carve(A_ST, [128, 4, 128], BF16)
            R.vd = carve(A_VD, [128, 4, 256], BF16)
            R.ytmp = carve(A_YTMP, [128, 4, 256], F32)
            R.yr_tm = carve(A_YRTM, [128, 1024], BF16)
            R.cosr = carve(A_COSR, [128, CH, 128], F32)
            R.sinr = carve(A_SINR, [128, CH, 128], F32)
            R.tmps = [carve(A_TMPR + i * 1024, [128, 2, 128], F32) for i in range(4)]
            R.r_q = [AR(f"q_tm{c}") for c in range(CH)]
            R.r_k = [AR(f"k_tm{c}") for c in range(CH)]
            R.r_v = [AR(f"v_tm{c}") for c in range(CH)]
            R.r_g = [AR(f"gs{c}") for c in range(CH)]
            R.r_qT, R.r_qdT, R.r_kT, R.r_sT, R.r_vd = AR("qT"), AR("qdT"), AR("kT"), AR("sT"), AR("vd")
            R.r_ytmp, R.r_yr, R.r_cs = AR("ytmp"), AR("yr_tm"), AR("cosr")
            R.r_tmp = R.r_ytmp
            return R

        def ret_tables(R, pcol0):
            for c in range(CH):
                sincos(R.sinr[:, c, :], R.cosr[:, c, :], invr[:], pcol0 + c, 128, R.r_cs)

        def rope_evac(R, b, c, dst, r_dst, hsl, nh=2, col0=None):
            if col0 is None:
                col0 = hsl * 512
            pv = ps[b][:, 0:nh * 256].rearrange("p (h d) -> p h d", h=nh)
            x1, x2 = pv[:, :, 0:128], pv[:, :, 128:256]
            cb = R.cosr[:, c, :].unsqueeze(1).broadcast_to([128, nh, 128])
            sbb = R.sinr[:, c, :].unsqueeze(1).broadcast_to([128, nh, 128])
            dv = dst[:, c, col0:col0 + nh * 256].rearrange("p (h d) -> p h d", h=nh)
            t = [w[:, 0:nh, :] for w in R.tmps]
            rt = R.r_tmp
            P.op("dve", lambda e: e.tensor_tensor(out=t[0], in0=x1, in1=cb, op=ALU.mult), reads=[r_ps[b], R.r_cs], writes=[rt])
            P.op("dve", lambda e: e.tensor_tensor(out=t[1], in0=x2, in1=sbb, op=ALU.mult), reads=[r_ps[b], R.r_cs], writes=[rt])
            P.op("dve", lambda e: e.tensor_tensor(out=t[2], in0=x2, in1=cb, op=ALU.mult), reads=[r_ps[b], R.r_cs], writes=[rt])
            P.op("dve", lambda e: e.tensor_tensor(out=t[3], in0=x1, in1=sbb, op=ALU.mult), reads=[r_ps[b], R.r_cs], writes=[rt])
            P.op("dve", lambda e: e.tensor_tensor(out=dv[:, :, 0:128], in0=t[0], in1=t[1], op=ALU.subtract), reads=[rt], writes=[r_dst])
            P.op("dve", lambda e: e.tensor_tensor(out=dv[:, :, 128:256], in0=t[2], in1=t[3], op=ALU.add), reads=[rt], writes=[r_dst])

        def proj_tm(wv, r_w, c, kcs=KC, cols=512, src=None, r_src=None):
            b = bank()
            if src is None:
                src, r_src = nT, r_nT[c]
                l = lambda kc: nT[:, kc, c * 128:(c + 1) * 128]
            else:
                l = src
            mm_group(b, (0, cols), [(l(kc), wv[:, kc, 0:cols]) for kc in range(kcs)], [r_w, r_src])
            return b

        proj_tm0 = proj_tm

        def ret_kv_steps(R, heads=(0, 1, 2, 3)):
            def proj_tm(wv, r_w, c, cols=512):
                try:
                    nb = R.nTbuf
                except AttributeError:
                    nb = None
                if nb is None:
                    return proj_tm0(wv, r_w, c, cols=cols)
                return proj_tm0(wv, r_w, c, cols=cols, src=lambda kc: nb[:, kc, c * 128:(c + 1) * 128], r_src=R.r_nTbuf[c])
            if len(heads) == 4:
                for hsl in range(2):
                    def kst(slots, hsl=hsl):
                        (wv, r_w), = slots
                        for c in range(CH):
                            b = proj_tm(wv, r_w, c)
                            rope_evac(R, b, c, R.k_tm, R.r_k[c], hsl)
                    step([wdesc(w_in, 0, KC, O_KR + hsl * 512, 512)], kst)
                for hsl in range(2):
                    def vst(slots, hsl=hsl):
                        (wv, r_w), = slots
                        for c in range(CH):
                            b = proj_tm(wv, r_w, c)
                            copy_any(R.v_tm[:, c, hsl * 512:(hsl + 1) * 512], ps[b][:], [r_ps[b]], [R.r_v[c]])
                    step([wdesc(w_in, 0, KC, O_VR + hsl * 512, 512)], vst)
            else:
                def kst(slots):
                    (wv, r_w), = slots
                    for c in range(CH):
                        b = proj_tm(wv, r_w, c, cols=256)
                        rope_evac(R, b, c, R.k_tm, R.r_k[c], 1, nh=1, col0=768)
                step([wdesc(w_in, 0, KC, O_KR + 768, 256)], kst)

                def vst(slots):
                    (wv, r_w), = slots
                    for c in range(CH):
                        b = proj_tm(wv, r_w, c, cols=256)
                        copy_any(R.v_tm[:, c, 768:1024], ps[b][:, 0:256], [r_ps[b]], [R.r_v[c]])
                step([wdesc(w_in, 0, KC, O_VR + 768, 256)], vst)

        def state_update(R, c, heads=(0, 1, 2, 3)):
            h0 = heads[0]
            nh = len(heads)
            P.op("dve", lambda e: e.tensor_tensor(
                out=R.vd[:, h0:h0 + nh, :], in0=R.v_tm[:, c, h0 * 256:(h0 + nh) * 256].rearrange("p (h d) -> p h d", h=nh),
                in1=kdec[:, h0:h0 + nh].unsqueeze(2).broadcast_to([128, nh, 256]), op=ALU.mult),
                reads=[R.r_v[c], r_const], writes=[R.r_vd])
            for h in heads:
                b = bank()
                for dc in range(2):
                    mm_group(b, (dc * 256, dc * 256 + 256),
                             [(R.k_tm[:, c, h * 256 + dc * 128: h * 256 + dc * 128 + 128], R.vd[:, h, :])],
                             [R.r_k[c], R.r_vd])
                P.op("dve", lambda e, h=h, b=b: e.scalar_tensor_tensor(
                    out=S_t[:, 2 * h:2 * h + 2, :], in0=S_t[:, 2 * h:2 * h + 2, :], scalar=g128[h],
                    in1=ps[b][:].rearrange("p (a d) -> p a d", a=2), op0=ALU.mult, op1=ALU.add),
                    reads=[r_ps[b], r_S[h]], writes=[r_S[h]])
                P.op("act", lambda e, h=h: e.activation(out=Sb_t[:, 2 * h:2 * h + 2, :], in_=S_t[:, 2 * h:2 * h + 2, :], func=AF.Copy),
                     reads=[r_S[h]], writes=[r_Sb[h]])

        PP = {}

        def pp_setup(slots):
            new_phase()
            PP["k_tm"] = carve(0, [128, CH, 1024], BF16)
            PP["v_tm"] = carve(8192, [128, CH, 1024], BF16)
            PP["vd"] = carve(16384, [128, 4, 256], BF16)
            PP["tmps"] = [carve(18432 + i * 1024, [128, 2, 128], F32) for i in range(4)]
            PP["cos"] = [carve(22528 + par * 4096, [128, CH, 128], F32) for par in range(2)]
            PP["sin"] = [carve(22528 + par * 4096 + 2048, [128, CH, 128], F32) for par in range(2)]
            PP["nT2"] = carve(30720, [128, KC, TT], BF16)
            PP["r_k"] = [AR(f"ppk{c}") for c in range(CH)]
            PP["r_v"] = [AR(f"ppv{c}") for c in range(CH)]
            PP["r_vd"], PP["r_tmp"] = AR("ppvd"), AR("pptmp")
            PP["r_cs"] = [AR("ppcs0"), AR("ppcs1")]
            PP["r_nT2"] = [AR(f"ppnT2{c}") for c in range(CH)]

        def pp_bufs(pt):
            par = pt % 2
            R = RetBufs()
            R.k_tm, R.v_tm, R.vd, R.tmps = PP["k_tm"], PP["v_tm"], PP["vd"], PP["tmps"]
            R.cosr, R.sinr = PP["cos"][par], PP["sin"][par]
            R.r_k, R.r_v, R.r_vd, R.r_tmp, R.r_cs = PP["r_k"], PP["r_v"], PP["r_vd"], PP["r_tmp"], PP["r_cs"][par]
            if par == 0:
                R.nTbuf, R.r_nTbuf = nT, r_nT
            else:
                R.nTbuf, R.r_nTbuf = PP["nT2"], PP["r_nT2"]
            return R

        def prepass_all(tiles):
            step([], pp_setup)
            boxes = {pt: {} for pt in tiles}

            def mk_norm(pt):
                def f(slots):
                    R = pp_bufs(pt)
                    boxes[pt]["R"] = R
                    for c in range(CH):
                        load_x(x_prev[(pt * CH + c) * 128:(pt * CH + c + 1) * 128, :], c)
                    for c in range(CH):
                        norm_T(h_t[:, c, :], r_h[c], 0, R.nTbuf[:, :, c * 128:(c + 1) * 128], R.r_nTbuf[c])
                    ret_tables(R, 17 + pt * CH)
                return f

            def mk_post(pt, heads):
                def f(slots):
                    R = boxes[pt]["R"]
                    for c in range(CH):
                        state_update(R, c, heads)
                return f

            for i, pt in enumerate(tiles):
                if i == 0:
                    step([], mk_norm(pt))
                if i + 1 < len(tiles):
                    step([], mk_norm(tiles[i + 1]))
                far = pt < NPRE - NTILE
                heads = (3,) if far else (0, 1, 2, 3)
                ret_kv_steps(RetProxy(boxes[pt]), heads)
                step([], mk_post(pt, heads))

        class RetProxy:
            def __init__(self, box):
                object.__setattr__(self, "_box", box)

            def __getattr__(self, k):
                return getattr(self._box["R"], k)

        B_SC, B_PROBS, B_PT, B_QST = 0, 16384, 24576, 32768
        B_QSTM, B_KSTM, B_TMPS, B_NTH, B_XH = 40960, 43008, 44032, 45056, 0

        def main_tile(t):
            box = {}
            first = (t == 0)

            def swa_pre(slots):
                new_phase()
                W = RetBufs()
                box["W"] = W
                W.sc = carve(B_SC, [128, 16, 256], F32)
                W.probs = carve(B_PROBS, [128, 16, 256], BF16)
                W.pT = carve(B_PT, [128, 16, 2, 128], BF16)
                W.qsT = carve(B_QST, [128, 8, TT], BF16)
                W.qs_tm = [carve(B_QSTM + i * 1024, [128, 512], BF16) for i in range(2)]
                W.ks_tm = carve(B_KSTM, [128, 4, 128], BF16)
                W.tmps = [carve(B_TMPS + i * 256, [128, 8, 8], F32) for i in range(4)]
                W.nTh = carve(B_NTH, [128, KC, 128], BF16)
                W.xh = carve(B_XH, [128, D], F32)
                W.r_sc, W.r_probs, W.r_pT = AR("sc"), AR("probs"), AR("pT")
                W.r_qsT = [AR(f"qsT{c}") for c in range(CH)]
                W.r_qstm = [AR("qstm0"), AR("qstm1")]
                W.r_kstm, W.r_tmps, W.r_nTh = AR("kstm"), AR("tmps"), AR("nTh")
                W.r_xh = W.r_sc
                if first:
                    for c in range(CH):
                        load_x(x_own[(t * CH + c) * 128:(t * CH + c + 1) * 128, :], c)
                if first:
                    P.op("sp", lambda e: e.dma_start(out=W.xh, in_=x_halo), writes=[W.r_xh], dsem=s_misc)
                    norm_T(W.xh, W.r_xh, 0, W.nTh, W.r_nTh)
                else:
                    P.op("dve", lambda e: e.tensor_copy(out=ksT[:, :, 0:128], in_=ksT[:, :, 512:640]),
                         reads=[r_ks[4]], writes=[r_ks[0]])
                    P.op("dve", lambda e: e.tensor_copy(out=vs_t[:, 0, :], in_=vs_t[:, 4, :]),
                         reads=[r_vs[4]], writes=[r_vs[0]])
                for c in range(CH):
                    norm_T(h_t[:, c, :], r_h[c], 0, nT[:, :, c * 128:(c + 1) * 128], r_nT[c])
            step([], swa_pre)

            def rope_s(W, pv, r_pv, nh, ci, dst, r_dst):
                cb = coss[:, ci, :].unsqueeze(1).broadcast_to([128, nh, 8])
                sbb = sins[:, ci, :].unsqueeze(1).broadcast_to([128, nh, 8])
                x1, x2 = pv[:, :, 0:8], pv[:, :, 8:16]
                t = [w[:, 0:nh, :] for w in W.tmps]
                rt = W.r_tmps
                rd = [r_cs, r_pv]
                P.op("dve", lambda e: e.tensor_tensor(out=t[0], in0=x1, in1=cb, op=ALU.mult), reads=rd, writes=[rt])
                P.op("dve", lambda e: e.tensor_tensor(out=t[1], in0=x2, in1=sbb, op=ALU.mult), reads=rd, writes=[rt])
                P.op("dve", lambda e: e.tensor_tensor(out=t[2], in0=x2, in1=cb, op=ALU.mult), reads=rd, writes=[rt])
                P.op("dve", lambda e: e.tensor_tensor(out=t[3], in0=x1, in1=sbb, op=ALU.mult), reads=rd, writes=[rt])
                P.op("dve", lambda e: e.tensor_tensor(out=dst[:, :, 0:8], in0=t[0], in1=t[1], op=ALU.subtract), reads=[rt], writes=[r_dst])
                P.op("dve", lambda e: e.tensor_tensor(out=dst[:, :, 8:16], in0=t[2], in1=t[3], op=ALU.add), reads=[rt], writes=[r_dst])
                P.op("act", lambda e: e.activation(out=dst[:, :, 16:64], in_=pv[:, :, 16:64], func=AF.Copy), reads=[r_pv], writes=[r_dst])

            for s2 in range(2):
                def qs_step(slots, s2=s2):
                    W = box["W"]
                    (wv, r_w), = slots

                    def emit_T(c, i):
                        b2 = bank()
                        pvt = transposes(b2, [W.qs_tm[i][:, j * 128:(j + 1) * 128] for j in range(4)], [W.r_qstm[i]])
                        copy_any(W.qsT[:, 4 * s2:4 * s2 + 4, c * 128:(c + 1) * 128],
                                 pvt[:, 0:512].rearrange("p (a b) -> p a b", a=4), [r_ps[b2]], [W.r_qsT[c]])
                    pend = None
                    for c in range(CH):
                        b = proj_tm(wv, r_w, c)
                        i = (s2 * CH + c) % 2
                        dst = W.qs_tm[i].rearrange("p (h d) -> p h d", h=8)
                        pv = ps[b][:].rearrange("p (h d) -> p h d", h=8)
                        rope_s(W, pv, r_ps[b], 8, 1 + t * CH + c, dst, W.r_qstm[i])
                        if pend is not None:
                            emit_T(*pend)
                        pend = (c, i)
                    emit_T(*pend)
                step([wdesc(w_in, 0, KC, O_QS + s2 * 512, 512)], qs_step)

            def kv_step(slots):
                W = box["W"]
                (wv, r_w), = slots
                chunks = ([(-1, 0)] if first else []) + [(c, c + 1) for c in range(CH)]
                for c, slot in chunks:
                    if c < 0:
                        b = proj_tm(wv, r_w, 0, src=lambda kc: W.nTh[:, kc, :], r_src=W.r_nTh)
                        ci = 0
                    else:
                        b = proj_tm(wv, r_w, c)
                        ci = 1 + t * CH + c
                    pv = ps[b][:, 0:256].rearrange("p (h d) -> p h d", h=4)
                    dst = W.ks_tm[:, :, 0:64]
                    rope_s(W, pv, r_ps[b], 4, ci, dst, W.r_kstm)
                    P.op("dve", lambda e: e.tensor_copy(out=W.ks_tm[:, :, 64:128], in_=W.ks_tm[:, :, 0:64]),
                         reads=[W.r_kstm], writes=[W.r_kstm])
                    P.op("act", lambda e, b=b, slot=slot: e.activation(out=vs_t[:, slot, :], in_=ps[b][:, 256:512], func=AF.Copy),
                         reads=[r_ps[b]], writes=[r_vs[slot]])
                    b2 = bank()
                    pvt = transposes(b2, [W.ks_tm[:, g, :] for g in range(4)], [W.r_kstm])
                    copy_any(ksT[:, :, slot * 128:(slot + 1) * 128], pvt[:, 0:512].rearrange("p (a b) -> p a b", a=4),
                             [r_ps[b2]], [r_ks[slot]])
            step([wdesc(w_in, 0, KC, O_KS, 512)], kv_step)

            def mi_mask(first_, c_):
                return 1 if (first_ and c_ == 0) else 0

            def swa_attn(slots):
                W = box["W"]
                mx, r_mx = stat("mx", 16)
                sm, r_sm = stat("sm", 16)
                esk, r_es = stat("es", 16)
                rden, r_rden = stat("rden", 16)
                for c in range(CH):
                    mi = 1 if (first and c == 0) else 0
                    for mp in range(4):
                        bAB = (bank(), bank())
                        for mi in range(2):
                            m = 2 * mp + mi
                            for hh in range(2):
                                hq = 2 * m + hh
                                g = hq // 4
                                po = hh * 64
                                mm_group(bAB[hh], (mi * 256, mi * 256 + 256),
                                         [(W.qsT[po:po + 64, m, c * 128:(c + 1) * 128], ksT[po:po + 64, g, c * 128:c * 128 + 256])],
                                         [W.r_qsT[c], r_ks[c], r_ks[c + 1]])
                        for hh in range(2):
                            b = bAB[hh]
                            for mi in range(2):
                                hq = 4 * mp + 2 * mi + hh
                                P.op("dve", lambda e: e.scalar_tensor_tensor(
                                    out=W.sc[:, hq, :], in0=ps[b][:, mi * 256:(mi + 1) * 256], scalar=0.125,
                                    in1=maskb[:, mi_mask(first, c), :], op0=ALU.mult, op1=ALU.add),
                                    reads=[r_ps[b], r_const], writes=[W.r_sc])
                    P.op("dve", lambda e: e.tensor_reduce(out=mx, in_=W.sc, axis=AX.X, op=ALU.max), reads=[W.r_sc], writes=[r_mx])
                    P.op("dve", lambda e: e.tensor_scalar(out=mx, in0=mx, scalar1=-1.0, scalar2=None, op0=ALU.mult), reads=[r_mx], writes=[r_mx])
                    for hq in range(16):
                        P.op("act", lambda e: e.activation(out=W.sc[:, hq, :], in_=W.sc[:, hq, :], func=AF.Exp, bias=mx[:, hq:hq + 1], scale=1.0,
                                                           accum_out=sm[:, hq:hq + 1]), reads=[W.r_sc, r_mx], writes=[W.r_sc, r_sm])
                    P.op("dve", lambda e: e.tensor_tensor(out=esk, in0=sinkb[:], in1=mx, op=ALU.add), reads=[r_mx, r_const], writes=[r_es])
                    P.op("act", lambda e: e.activation(out=esk, in_=esk, func=AF.Exp), reads=[r_es], writes=[r_es])
                    P.op("dve", lambda e: e.tensor_tensor(out=esk, in0=esk, in1=sm, op=ALU.add), reads=[r_es, r_sm], writes=[r_es])
                    P.op("dve", lambda e: e.reciprocal(out=rden, in_=esk), reads=[r_es], writes=[r_rden])
                    P.op("dve", lambda e: e.tensor_tensor(out=W.probs, in0=W.sc, in1=rden.unsqueeze(2).broadcast_to([128, 16, 256]),
                                                           op=ALU.mult), reads=[W.r_sc, r_rden], writes=[W.r_probs])
                    for q4 in range(4):
                        b = bank()
                        srcs = []
                        for hh in range(4):
                            for half in range(2):
                                srcs.append(W.probs[:, 4 * q4 + hh, half * 128:(half + 1) * 128])
                        pvt = transposes(b, srcs, [W.r_probs])
                        copy_any(W.pT[:, 4 * q4:4 * q4 + 4, :, :].rearrange("p a b c -> p (a b c)"), pvt, [r_ps[b]], [W.r_pT])
                    for bb in range(2):
                        b = bank()
                        for mm_ in range(4):
                            m = bb * 4 + mm_
                            for hh in range(2):
                                hq = 2 * m + hh
                                g = hq // 4

                                def f(e, b=b, mm_=mm_, hh=hh, hq=hq, g=g, c=c):
                                    inst = None
                                    for half in range(2):
                                        inst = e.matmul(ps[b][hh * 64:(hh + 1) * 64, mm_ * 128:(mm_ + 1) * 128],
                                                        lhsT=vs_t[:, c + half, g * 64:(g + 1) * 64], rhs=W.pT[:, hq, half, :],
                                                        start=(half == 0), stop=(half == 1))
                                    return inst
                                P.op("pe", f, reads=[W.r_pT, r_vs[c], r_vs[c + 1]], writes=[r_ps[b]])
                        copy_any(ysT[:, 4 * bb:4 * bb + 4, c * 128:(c + 1) * 128], ps[b][:].rearrange("p (a q) -> p a q", a=4),
                                 [r_ps[b]], [r_ysT[c]])
            step([], swa_attn)
            steps.append(("MARK", "swa"))

            def ret_pre(slots):
                new_phase()
                R = ret_bufs()
                box["R"] = R
                ret_tables(R, 1 + t * CH)
            step([], ret_pre)
            Rp = RetProxy(box)
            for hsl in range(2):
                def qst(slots, hsl=hsl):
                    R = box["R"]
                    (wv, r_w), = slots
                    for c in range(CH):
                        b = proj_tm(wv, r_w, c)
                        rope_evac(R, b, c, R.q_tm, R.r_q[c], hsl)
                step([wdesc(w_in, 0, KC, O_QR + hsl * 512, 512)], qst)
            ret_kv_steps(Rp)
            for hsl in range(2):
                def gst(slots, hsl=hsl):
                    R = box["R"]
                    (wv, r_w), = slots
                    for c in range(CH):
                        b = proj_tm(wv, r_w, c)
                        P.op("act", lambda e, b=b, c=c: e.activation(out=R.gs[:, c, hsl * 512:(hsl + 1) * 512], in_=ps[b][:], func=AF.Silu),
                             reads=[r_ps[b]], writes=[R.r_g[c]])
                step([wdesc(w_in, 0, KC, O_GR + hsl * 512, 512)], gst)

            steps.append(("MARK", "retproj"))

            def ret_core(slots):
                R = box["R"]
                ssr, r_ssr = stat("ssr", 4)
                rs4, r_rs4 = stat("rs4", 4)
                for c in range(CH):
                    bq = bank()
                    pvq = transposes(bq, [R.q_tm[:, c, j * 128:(j + 1) * 128] for j in range(8)], [R.r_q[c]])
                    P.op("act", lambda e: e.activation(out=R.qT[:].rearrange("p a b -> p (a b)"), in_=pvq, func=AF.Copy),
                         reads=[r_ps[bq]], writes=[R.r_qT])
                    for dc in range(2):
                        P.op("dve", lambda e: e.tensor_tensor(
                            out=R.qdT[:].rearrange("p (h a) b -> p h a b", h=4)[:, :, dc, :],
                            in0=pvq.rearrange("p (h a b) -> p h a b", h=4, a=2)[:, :, dc, :],
                            in1=qdecT[:], op=ALU.mult),
                            reads=[r_ps[bq], r_const], writes=[R.r_qdT])
                    bk = bank()
                    pvk = transposes(bk, [R.k_tm[:, c, j * 128:(j + 1) * 128] for j in range(8)], [R.r_k[c]])
                    copy_any(R.kT[:].rearrange("p a b -> p (a b)"), pvk, [r_ps[bk]], [R.r_kT])
                    if ret_level < 2:
                        continue
                    bs = bank()
                    for h in range(4):
                        mm_group(bs, (h * 128, h * 128 + 128),
                                 [(R.kT[:, 2 * h + dc, :], R.qT[:, 2 * h + dc, :]) for dc in range(2)], [R.r_kT, R.r_qT])
                    P.op("dve", lambda e: e.tensor_tensor(out=R.sT, in0=ps[bs][:].rearrange("p (h i) -> p h i", h=4), in1=dmT[:], op=ALU.mult),
                         reads=[r_ps[bs], r_const], writes=[R.r_sT])
                    if ret_level < 3:
                        continue
                    bo = [bank(), bank()]
                    for h in range(4):
                        b = bo[h // 2]
                        pairs = [(R.sT[:, h, :], R.v_tm[:, c, h * 256:(h + 1) * 256])]
                        pairs += [(R.qdT[:, 2 * h + dc, :], Sb_t[:, 2 * h + dc, :]) for dc in range(2)]
                        mm_group(b, ((h % 2) * 256, (h % 2) * 256 + 256), pairs, [R.r_sT, R.r_v[c], R.r_qdT, r_Sb[h]])
                    state_update(R, c)
                    if ret_level < 4:
                        continue
                    for h in range(4):
                        b = bo[h // 2]
                        P.op("act", lambda e, h=h, b=b: e.activation(out=R.ytmp[:, h, :], in_=ps[b][:, (h % 2) * 256:(h % 2) * 256 + 256],
                                                                    func=AF.Square, accum_out=ssr[:, h:h + 1]),
                             reads=[r_ps[b]], writes=[R.r_ytmp, r_ssr])
                    rstd_from(ssr, r_ssr, rs4, r_rs4, 4, 1.0 / 256.0)
                    for hb in range(2):
                        b = bo[hb]
                        P.op("dve", lambda e, hb=hb, b=b: e.tensor_tensor(
                            out=R.ytmp[:, 2 * hb:2 * hb + 2, :], in0=ps[b][:].rearrange("p (a d) -> p a d", a=2),
                            in1=rs4[:, 2 * hb:2 * hb + 2].unsqueeze(2).broadcast_to([128, 2, 256]), op=ALU.mult),
                            reads=[r_ps[b], r_rs4], writes=[R.r_ytmp])
                    P.op("dve", lambda e, c=c: e.tensor_tensor(out=R.yr_tm, in0=R.ytmp[:].rearrange("p a b -> p (a b)"), in1=R.gs[:, c, :], op=ALU.mult),
                         reads=[R.r_ytmp, R.r_g[c]], writes=[R.r_yr])
                    by = bank()
                    pvy = transposes(by, [R.yr_tm[:, j * 128:(j + 1) * 128] for j in range(8)], [R.r_yr])
                    copy_any(yrT[:, :, c * 128:(c + 1) * 128], pvy.rearrange("p (a b) -> p a b", a=8), [r_ps[by]], [r_yrT[c]])
            step([], ret_core)
            steps.append(("MARK", "ret"))

            C_T1, C_SG, C_MT = 0, 8192, 12288

            def merge_pre(slots):
                new_phase()
                M = RetBufs()
                box["M"] = M
                M.t1 = carve(C_T1, [128, 4, TT], F32)
                M.sg = [carve(C_SG + i * 2048, [128, TT], F32) for i in range(2)]
                M.mT = carve(C_MT, [128, KC, TT], BF16)
                M.r_t1 = [AR(f"t1_{j}") for j in range(4)]
                M.r_sg = [AR("sg0"), AR("sg1")]
                M.r_mT = [AR(f"mT{k}") for k in range(KC)]
            step([], merge_pre)

            sg_ctr = [0]
            for s in range(4):
                def mr(slots, s=s):
                    M = box["M"]
                    (wg, r_wg), (wu, r_wu) = slots
                    for j in range(4):
                        bg, bu = bank(), bank()
                        mm_group(bg, (0, 512), [(wg[:, kc, j * 128:(j + 1) * 128], nT[:, kc, :]) for kc in range(KC)], [r_wg] + r_nT)
                        mm_group(bu, (0, 512), [(wu[:, kc, j * 128:(j + 1) * 128], yrT[:, kc, :]) for kc in range(8)], [r_wu] + r_yrT)
                        i = sg_ctr[0] % 2
                        sg_ctr[0] += 1
                        P.op("act", lambda e, bg=bg, i=i: e.activation(out=M.sg[i], in_=ps[bg][:], func=AF.Sigmoid),
                             reads=[r_ps[bg]], writes=[M.r_sg[i]])
                        P.op("dve", lambda e, bu=bu, i=i, j=j: e.tensor_tensor(out=M.t1[:, j, :], in0=M.sg[i], in1=ps[bu][:], op=ALU.mult),
                             reads=[r_ps[bu], M.r_sg[i]], writes=[M.r_t1[j]])
                step([wdesc(w_in, 0, KC, O_GTR + s * 512, 512), wdesc(w_up_ret, 0, 8, s * 512, 512)], mr)

                def ms(slots, s=s):
                    M = box["M"]
                    (wg, r_wg), (wu, r_wu) = slots
                    for j in range(4):
                        bg, bu = bank(), bank()
                        mm_group(bg, (0, 512), [(wg[:, kc, j * 128:(j + 1) * 128], nT[:, kc, :]) for kc in range(KC)], [r_wg] + r_nT)
                        mm_group(bu, (0, 512), [(wu[:, kc, j * 128:(j + 1) * 128], ysT[:, kc, :]) for kc in range(8)], [r_wu] + r_ysT)
                        i = sg_ctr[0] % 2
                        sg_ctr[0] += 1
                        P.op("act", lambda e, bg=bg, i=i: e.activation(out=M.sg[i], in_=ps[bg][:], func=AF.Sigmoid),
                             reads=[r_ps[bg]], writes=[M.r_sg[i]])
                        P.op("dve", lambda e, bu=bu, i=i: e.tensor_tensor(out=M.sg[i], in0=M.sg[i], in1=ps[bu][:], op=ALU.mult),
                             reads=[r_ps[bu], M.r_sg[i]], writes=[M.r_sg[i]])
                        P.op("dve", lambda e, i=i, j=j: e.tensor_tensor(out=M.mT[:, 4 * s + j, :], in0=M.sg[i], in1=M.t1[:, j, :], op=ALU.add),
                             reads=[M.r_sg[i], M.r_t1[j]], writes=[M.r_mT[4 * s + j]])
                step([wdesc(w_in, 0, KC, O_GTS + s * 512, 512), wdesc(w_up_swa, 0, 8, s * 512, 512)], ms)

            def add_into_h(b, c, c0, cols=512):
                P.op("dve", lambda e: e.tensor_tensor(out=h_t[:, c, c0:c0 + cols], in0=h_t[:, c, c0:c0 + cols], in1=ps[b][:, 0:cols], op=ALU.add),
                     reads=[r_ps[b], r_h[c]], writes=[r_h[c]])

            for s in range(4):
                def wo_step(slots, s=s):
                    M = box["M"]
                    (wv, r_w), = slots
                    for c in range(CH):
                        b = bank()
                        mm_group(b, (0, 512), [(M.mT[:, kc, c * 128:(c + 1) * 128], wv[:, kc, :]) for kc in range(KC)], [r_w] + M.r_mT)
                        add_into_h(b, c, s * 512)
                step([wdesc(w_o, 0, KC, s * 512, 512)], wo_step)

            steps.append(("MARK", "wo"))
            X_QX, X_SC, X_PX, X_PT, X_OX = 0, 4096, 8192, 10240, 12288

            def x_pre(slots):
                new_phase()
                X = RetBufs()
                box["X"] = X
                X.qxT = carve(X_QX, [128, 4, TT], BF16)
                X.sc = carve(X_SC, [128, 4, 256], F32)
                X.px = carve(X_PX, [128, 4, 256], BF16)
                X.pT = carve(X_PT, [128, 4, 2, 128], BF16)
                X.oxT = carve(X_OX, [128, 4, TT], BF16)
                X.r_qx, X.r_sc, X.r_px, X.r_pT = AR("qxT"), AR("xsc"), AR("px"), AR("xpT")
                X.r_ox = [AR(f"oxT{c}") for c in range(CH)]
                for c in range(CH):
                    norm_T(h_t[:, c, :], r_h[c], 1, nT[:, :, c * 128:(c + 1) * 128], r_nT[c])
            step([], x_pre)

            def xq_step(slots):
                X = box["X"]
                (wv, r_w), = slots
                for hd in range(4):
                    b = bank()
                    mm_group(b, (0, 512), [(wv[:, kc, hd * 128:(hd + 1) * 128], nT[:, kc, :]) for kc in range(KC)], [r_w] + r_nT)
                    P.op("act", lambda e, b=b, hd=hd: e.activation(out=X.qxT[:, hd, :], in_=ps[b][:], func=AF.Copy, scale=128.0 ** -0.5),
                         reads=[r_ps[b]], writes=[X.r_qx])
                mx, r_mx = stat("xmx", 4)
                sm, r_sm = stat("xsm", 4)
                for c in range(CH):
                    bb = [bank(), bank()]
                    for hd in range(4):
                        mm_group(bb[hd // 2], ((hd % 2) * 256, (hd % 2) * 256 + 256),
                                 [(X.qxT[:, hd, c * 128:(c + 1) * 128], kmT[:, hd, :])], [X.r_qx, r_km])
                    for i2 in range(2):
                        copy_any(X.sc[:, 2 * i2:2 * i2 + 2, :], ps[bb[i2]][:].rearrange("p (a k) -> p a k", a=2), [r_ps[bb[i2]]], [X.r_sc])
                    P.op("dve", lambda e: e.tensor_reduce(out=mx, in_=X.sc, axis=AX.X, op=ALU.max), reads=[X.r_sc], writes=[r_mx])
                    P.op("dve", lambda e: e.tensor_tensor(out=X.sc, in0=X.sc, in1=mx.unsqueeze(2).broadcast_to([128, 4, 256]), op=ALU.subtract),
                         reads=[X.r_sc, r_mx], writes=[X.r_sc])
                    P.op("act", lambda e: e.activation(out=X.sc, in_=X.sc, func=AF.Exp), reads=[X.r_sc], writes=[X.r_sc])
                    P.op("dve", lambda e: e.tensor_reduce(out=sm, in_=X.sc, axis=AX.X, op=ALU.add), reads=[X.r_sc], writes=[r_sm])
                    P.op("dve", lambda e: e.reciprocal(out=sm, in_=sm), reads=[r_sm], writes=[r_sm])
                    P.op("dve", lambda e: e.tensor_tensor(out=X.px, in0=X.sc, in1=sm.unsqueeze(2).broadcast_to([128, 4, 256]), op=ALU.mult),
                         reads=[X.r_sc, r_sm], writes=[X.r_px])
                    b = bank()
                    srcs = [X.px[:, hd, half * 128:(half + 1) * 128] for hd in range(4) for half in range(2)]
                    pvt = transposes(b, srcs, [X.r_px])
                    copy_any(X.pT[:].rearrange("p a b c -> p (a b c)"), pvt, [r_ps[b]], [X.r_pT])
                    b = bank()
                    for hd in range(4):
                        mm_group(b, (hd * 128, hd * 128 + 128),
                                 [(vm[:, half, hd * 128:(hd + 1) * 128], X.pT[:, hd, half, :]) for half in range(2)], [r_vm, X.r_pT])
                    copy_any(X.oxT[:, :, c * 128:(c + 1) * 128], ps[b][:].rearrange("p (a q) -> p a q", a=4), [r_ps[b]], [X.r_ox[c]])
            step([wdesc(w_xq, 0, KC, 0, 512)], xq_step)

            def xo_step(slots):
                X = box["X"]
                (wv, r_w), = slots
                for s in range(4):
                    for c in range(CH):
                        b = bank()
                        mm_group(b, (0, 512), [(X.oxT[:, kc, c * 128:(c + 1) * 128], wv[:, kc, s * 512:(s + 1) * 512]) for kc in range(4)],
                                 [r_w, X.r_ox[c]])
                        add_into_h(b, c, s * 512)
            step([wdesc(w_xo, 0, 4, 0, 2048)], xo_step)

            steps.append(("MARK", "xattn"))
            F_ACT, F_SG, F_GF = 0, 45056, 0

            def f_pre(slots):
                new_phase()
                Fb = RetBufs()
                box["F"] = Fb
                Fb.aT = carve(F_ACT, [128, 44, TT], BF16)
                Fb.sg = [carve(F_SG + i * 2048, [128, TT], F32) for i in range(2)]
                Fb.r_aT = [AR(f"aT{k}") for k in range(44)]
                Fb.r_sg = [AR("fsg0"), AR("fsg1")]
                for c in range(CH):
                    norm_T(h_t[:, c, :], r_h[c], 2, nT[:, :, c * 128:(c + 1) * 128], r_nT[c])
            step([], f_pre)

            for s in range(11):
                def fgu(slots, s=s):
                    Fb = box["F"]
                    (wg, r_wg), (wu, r_wu) = slots
                    for j in range(4):
                        bg, bu = bank(), bank()
                        mm_group(bg, (0, 512), [(wg[:, kc, j * 128:(j + 1) * 128], nT[:, kc, :]) for kc in range(KC)], [r_wg] + r_nT)
                        mm_group(bu, (0, 512), [(wu[:, kc, j * 128:(j + 1) * 128], nT[:, kc, :]) for kc in range(KC)], [r_wu] + r_nT)
                        i = sg_ctr[0] % 2
                        sg_ctr[0] += 1
                        P.op("act", lambda e, bg=bg, i=i: e.activation(out=Fb.sg[i], in_=ps[bg][:], func=AF.Silu),
                             reads=[r_ps[bg]], writes=[Fb.r_sg[i]])
                        P.op("dve", lambda e, bu=bu, i=i, j=j: e.tensor_tensor(out=Fb.aT[:, 4 * s + j, :], in0=Fb.sg[i], in1=ps[bu][:], op=ALU.mult),
                             reads=[r_ps[bu], Fb.r_sg[i]], writes=[Fb.r_aT[4 * s + j]])
                step([wdesc(w_fg, 0, KC, s * 512, 512), wdesc(w_fu, 0, KC, s * 512, 512)], fgu)

            kgroups = [(0, 16), (16, 16), (32, 12)]
            for cs in range(4):
                dbanks = {}
                for gi, (k0, kn) in enumerate(kgroups):
                    def fd(slots, cs=cs, gi=gi, k0=k0, kn=kn, dbanks=dbanks):
                        Fb = box["F"]
                        (wv, r_w), = slots
                        if gi == 0:
                            for c in range(CH):
                                dbanks[c] = bank()
                        for c in range(CH):
                            b = dbanks[c]

                            def f(e, b=b, c=c):
                                inst = None
                                for kk in range(kn):
                                    kc = k0 + kk
                                    inst = e.matmul(ps[b][:], lhsT=Fb.aT[:, kc, c * 128:(c + 1) * 128], rhs=wv[:, kk, :],
                                                    start=(kc == 0), stop=(kc == 43))
                                return inst
                            P.op("pe", f, reads=[r_w] + Fb.r_aT[k0:k0 + kn], writes=[r_ps[b]])
                            if gi == 2:
                                add_into_h(b, c, cs * 512)
                    step([wdesc(w_fd, k0 * 128, kn, cs * 512, 512)], fd)

            steps.append(("MARK", "ffn"))
            def fin(slots):
                new_phase()
                gf = carve(F_GF, [128, D], F32)
                r_gf = AR("gfin")
                P.op("sp", lambda e: e.dma_start(out=gf, in_=gfin_in), writes=[r_gf], dsem=s_misc)
                for c in range(CH):
                    P.op("act", lambda e, c=c: e.activation(out=xnb[:], in_=h_t[:, c, :], func=AF.Square, accum_out=ss_ap),
                         reads=[r_h[c]], writes=[r_xnb, r_ss])
                    rstd_from(ss_ap, r_ss, rstd_ap, r_rstd, 1, 1.0 / D)
                    P.op("dve", lambda e, c=c: e.scalar_tensor_tensor(out=h_t[:, c, :], in0=h_t[:, c, :], scalar=rstd_ap, in1=gf,
                                                                      op0=ALU.mult, op1=ALU.mult),
                         reads=[r_h[c], r_rstd, r_gf], writes=[r_h[c]])
                    P.op("sp", lambda e, c=c: e.dma_start(out=y_out[(t * CH + c) * 128:(t * CH + c + 1) * 128, :], in_=h_t[:, c, :]),
                         reads=[r_h[c]], dsem=s_y[c])
                    if t + 1 < n_main_tiles:
                        load_x(x_own[((t + 1) * CH + c) * 128:((t + 1) * CH + c + 1) * 128, :], c)
            step([], fin)

        def zero_state(slots):
            for h in range(4):
                P.op("dve", lambda e, h=h: e.memset(S_t[:, 2 * h:2 * h + 2, :], 0.0), writes=[r_S[h]])
                P.op("dve", lambda e, h=h: e.memset(Sb_t[:, 2 * h:2 * h + 2, :], 0.0), writes=[r_Sb[h]])
        step([], zero_state)
        mem_setup()
        if n_pre > 0:
            prepass_all(list(range(NPRE - n_pre, NPRE)))
        for t in range(n_main_tiles):
            main_tile(t)

        if stop_after is not None:
            cut = [i for i, st_ in enumerate(steps) if st_ == ("MARK", stop_after)][0]
            del steps[cut:]
        steps[:] = [st_ for st_ in steps if st_[0] != "MARK"]
        all_descs = []
        for descs, fn in steps:
            for d in descs:
                all_descs.append(d)
        issued = [0]
        consumed = [0]

        scratch = {}
        s_ws = [P.dma_sem(f"s_ws{i}") for i in range(NSLOT)]

        def issue(i):
            v, kcs, cols, key = all_descs[i]
            slot = i % NSLOT
            dst = wview(slot, kcs, cols)
            n = kcs * cols
            if key not in scratch:
                P.op("pool", lambda e: e.dma_start(out=dst, in_=v), writes=[r_wr[slot]], dsem=s_wr[slot])
                if USE_SCRATCH:
                    scr = nc.dram_tensor(f"scr{len(scratch)}", [128, n], BF16, kind="Internal").ap()
                    r_scr = Res(f"scr{len(scratch)}")
                    scratch[key] = (scr, r_scr)
                    flat = wr[slot][:].bitcast(BF16)[:, 0:n]
                    P.op("sp", lambda e: e.dma_start(out=scr, in_=flat), reads=[r_wr[slot]], writes=[r_scr], dsem=s_ws[slot])
            else:
                scr, r_scr = scratch[key]
                flat = wr[slot][:].bitcast(BF16)[:, 0:n]
                P.op("pool", lambda e: e.dma_start(out=flat, in_=scr), reads=[r_scr], writes=[r_wr[slot]], dsem=s_wr[slot])

        total = len(all_descs)
        for descs, fn in steps:
            need = len(descs)
            assert need <= NSLOT
            while issued[0] < consumed[0] + need:
                issue(issued[0])
                issued[0] += 1
            while issued[0] < total and issued[0] - consumed[0] < NSLOT:
                issue(issued[0])
                issued[0] += 1
            slots = []
            for k in range(need):
                i = consumed[0] + k
                v, kcs, cols, key = all_descs[i]
                slots.append((wview(i % NSLOT, kcs, cols), r_wr[i % NSLOT]))
            fn(slots)
            consumed[0] += need

        finals = [(s.idx, s.val) for s in s_y if s.val]
        P.run(finals)
        build.n_ops = P.n_ops
    return nc


def _consts(core):
    f32 = np.float32
    half = 128
    invr = (f32(1.0) / (f32(10000.0) ** (np.arange(half, dtype=f32) / f32(half)))).astype(f32)
    invs = (f32(1.0) / (f32(500000.0) ** (np.arange(8, dtype=f32) / f32(8)))).astype(f32)
    qi = np.arange(128)[:, None]
    kj = np.arange(256)[None, :]
    valid = (kj >= qi + 1) & (kj <= qi + 128)
    m0 = np.where(valid, 0.0, -1e30).astype(f32)
    m1 = m0.copy()
    if core == 0:
        m1[:, :128] = -1e30
    maskb = np.concatenate([m0, m1], axis=1)
    log_g = np.log(1.0 - np.power(2.0, -5.0 - np.arange(4, dtype=np.float64)))
    j = np.arange(128)[:, None]
    i = np.arange(128)[None, :]
    dmT = np.zeros((128, 4, 128), f32)
    qdecT = np.zeros((128, 4, 128), f32)
    kdec = np.zeros((128, 4), f32)
    for h in range(4):
        dmT[:, h, :] = np.where(i >= j, np.exp(log_g[h] * np.maximum(i - j, 0)), 0.0) / 16.0
        qdecT[:, h, :] = np.exp(log_g[h] * (np.arange(128) + 1.0))[None, :]
        kdec[:, h] = np.exp(log_g[h] * (127.0 - np.arange(128))) / 16.0
    return {
        "invr": np.ascontiguousarray(np.broadcast_to(invr[None, :], (128, 128))),
        "invs": np.ascontiguousarray(np.broadcast_to(invs[None, :], (128, 8))),
        "maskb": np.ascontiguousarray(maskb),
        "dmT": np.ascontiguousarray(dmT.reshape(128, 512)),
        "qdecT": np.ascontiguousarray(qdecT.reshape(128, 512)),
        "kdec": kdec,
    }


def _make_in_maps(x, mem, positions, g_mix, w_in, w_up_ret, w_up_swa, sinks, w_o, g_x, g_mem,
                  w_xq, w_xkv, w_xo, g_ffn, w_ffn_gate, w_ffn_up, w_ffn_down, g_final):
    c = np.ascontiguousarray
    x2 = np.asarray(x, np.float32).reshape(SEQ, D)
    pos = np.asarray(positions).reshape(SEQ).astype(np.int32)
    shared = {
        "mem": c(np.asarray(mem, np.float32).reshape(256, D)),
        "w_in": c(np.asarray(w_in, np.float32)[0]),
        "w_up_ret": c(np.asarray(w_up_ret, np.float32)[0]),
        "w_up_swa": c(np.asarray(w_up_swa, np.float32)[0]),
        "w_o": c(np.asarray(w_o, np.float32)[0]),
        "w_xq": c(np.asarray(w_xq, np.float32)[0]),
        "w_xkv": c(np.asarray(w_xkv, np.float32)[0]),
        "w_xo": c(np.asarray(w_xo, np.float32)[0]),
        "w_fg": c(np.asarray(w_ffn_gate, np.float32)[0]),
        "w_fu": c(np.asarray(w_ffn_up, np.float32)[0]),
        "w_fd": c(np.asarray(w_ffn_down, np.float32)[0]),
        "gfin": c(np.broadcast_to(np.asarray(g_final, np.float32).reshape(1, D), (128, D))),
        "sinkb": c(np.broadcast_to(np.asarray(sinks, np.float32).reshape(1, 16), (128, 16))),
    }
    gs = [np.asarray(g, np.float32).reshape(D) for g in (g_mix, g_x, g_ffn, g_mem)]
    shared["gcols"] = c(np.stack([g.reshape(16, 128).T for g in gs], axis=1).reshape(128, 64))
    in_maps = []
    npre_tok = NPRE * TT
    for core in range(NCORE):
        t0 = core * TOK
        xp = np.zeros((npre_tok + 128, D), np.float32)
        pp = np.zeros((npre_tok,), np.int32)
        lo = t0 - npre_tok
        if lo < 0:
            n = t0
            if n > 0:
                xp[npre_tok - n:npre_tok] = x2[0:t0]
                pp[npre_tok - n:] = pos[0:t0]
        else:
            xp[:npre_tok] = x2[lo:t0]
            pp[:] = pos[lo:t0]
        x_prev = c(xp[:npre_tok])
        x_halo = c(x_prev[npre_tok - 128:npre_tok])
        pos_halo = pp[npre_tok - 128:]
        pos_all = np.concatenate([pos_halo, pos[t0:t0 + TOK], pp])
        pos_dev = c(pos_all.reshape(17 + NPRE * CH, 128).T)
        m = dict(shared)
        m.update(_consts(core))
        m["x_own"] = c(x2[t0:t0 + TOK])
        m["x_halo"] = x_halo
        m["x_prev"] = x_prev
        m["pos"] = pos_dev
        in_maps.append(m)
    return in_maps


_NC_CACHE = {}


def kernel(**inputs):
    in_maps = _make_in_maps(**inputs)
    if "nc" not in _NC_CACHE:
        _NC_CACHE["nc"] = build()
    nc = _NC_CACHE["nc"]
    res = run_bass_kernel_spmd(nc, in_maps, core_ids=list(range(NCORE)))
    out = np.concatenate([np.asarray(r["y"], np.float32) for r in res.results], axis=0)
    return out.reshape(1, SEQ, D)
```
